# Optimizing a Trainium2 kernel written in Bass

```python
import math
import jax
import jax.numpy as jnp
from jax import lax
import numpy as np

D_MODEL = 2048
BATCH = 4
SEQ = 2048
DEPTH = 4

DA_HEADS = 8
DA_QK_DIM = 64
DA_V_DIM = 2 * DA_QK_DIM
GDN_HEADS = 8
GDN_K_DIM = 128
GDN_V_DIM = 128
GDN_CHUNK = 64
CONV_WIDTH = 4
FOX_HEADS = 16
FOX_HEAD_DIM = 128
Q_BLOCK = 128
D_FF = 4 * D_MODEL
NORM_EPS = 1e-6
L2_EPS = 1e-6

DA_Q_COLS = DA_HEADS * 2 * DA_QK_DIM
DA_K_COLS = DA_HEADS * 2 * DA_QK_DIM
DA_V_COLS = DA_HEADS * DA_V_DIM
GDN_QK_COLS = GDN_HEADS * GDN_K_DIM
GDN_V_COLS = GDN_HEADS * GDN_V_DIM
GDN_CONV_COLS = 2 * GDN_QK_COLS + GDN_V_COLS
_E1 = DA_Q_COLS
_E2 = _E1 + DA_K_COLS
_E3 = _E2 + DA_V_COLS
_E4 = _E3 + GDN_CONV_COLS
_E5 = _E4 + GDN_V_COLS
_E6 = _E5 + GDN_HEADS
EVEN_IN = _E6 + GDN_HEADS
EVEN_SPLITS = (_E1, _E2, _E3, _E4, _E5, _E6)
EVEN_MIX = DA_V_COLS + GDN_V_COLS
FOX_COLS = FOX_HEADS * FOX_HEAD_DIM
ODD_IN = 3 * FOX_COLS + FOX_HEADS
N_EVEN = (DEPTH + 1) // 2
N_ODD = DEPTH // 2

kernel_name = 'hybrid_diffattn_gdn_fox_trunk'


def rms_norm(x, g):
    xf = x.astype(jnp.float32)
    y = xf * lax.rsqrt(jnp.mean(xf * xf, axis=-1, keepdims=True) + NORM_EPS)
    return (y * g.astype(jnp.float32)).astype(x.dtype)


def _l2norm(t):
    return t * lax.rsqrt(jnp.sum(t * t, axis=-1, keepdims=True) + L2_EPS)


def _split_heads(t, n_heads):
    b, s, _ = t.shape
    return t.reshape(b, s, n_heads, -1).transpose(0, 2, 1, 3)


def _merge_heads(t):
    b, h, s, d = t.shape
    return t.transpose(0, 2, 1, 3).reshape(b, s, h * d)


def causal_depthwise_conv(x, w):
    width = w.shape[0]
    return lax.conv_general_dilated(
        x, w[:, None, :], window_strides=(1,), padding=[(width - 1, 0)],
        dimension_numbers=('NWC', 'WIO', 'NWC'), feature_group_count=x.shape[-1])


def differential_attention(q1, q2, k1, k2, v, lam, slopes):
    seq = q1.shape[2]
    scale = DA_QK_DIM ** -0.5
    outs = []
    for start in range(0, seq, Q_BLOCK):
        end = start + Q_BLOCK
        dist = (jnp.arange(start, end)[:, None] - jnp.arange(end)[None, :]).astype(jnp.float32)
        bias = jnp.where(dist >= 0, -slopes[:, None, None] * dist, -jnp.inf)
        s1 = jnp.einsum('bhqd,bhkd->bhqk', q1[:, :, start:end], k1[:, :, :end]).astype(jnp.float32) * scale + bias
        s2 = jnp.einsum('bhqd,bhkd->bhqk', q2[:, :, start:end], k2[:, :, :end]).astype(jnp.float32) * scale + bias
        p = jax.nn.softmax(s1, axis=-1) - lam * jax.nn.softmax(s2, axis=-1)
        outs.append(jnp.einsum('bhqk,bhkd->bhqd', p.astype(v.dtype), v[:, :, :end]))
    return jnp.concatenate(outs, axis=2)


def forgetting_attention(q, k, v, cum_log_f):
    seq = q.shape[2]
    scale = FOX_HEAD_DIM ** -0.5
    outs = []
    for start in range(0, seq, Q_BLOCK):
        end = start + Q_BLOCK
        causal = jnp.arange(start, end)[:, None] >= jnp.arange(end)[None, :]
        bias = cum_log_f[:, :, start:end, None] - cum_log_f[:, :, None, :end]
        s = jnp.einsum('bhqd,bhkd->bhqk', q[:, :, start:end], k[:, :, :end]).astype(jnp.float32) * scale + bias
        p = jax.nn.softmax(jnp.where(causal, s, -jnp.inf), axis=-1)
        outs.append(jnp.einsum('bhqk,bhkd->bhqd', p.astype(v.dtype), v[:, :, :end]))
    return jnp.concatenate(outs, axis=2)


def gated_delta_rule(q, k, v, g, beta):
    b, h, s, dk = q.shape
    dv = v.shape[-1]
    c = GDN_CHUNK
    n = s // c
    q = q.reshape(b, h, n, c, dk)
    k = k.reshape(b, h, n, c, dk)
    v = v.reshape(b, h, n, c, dv)
    g = jnp.cumsum(g.reshape(b, h, n, c), axis=-1)
    beta = beta.reshape(b, h, n, c)[..., None]
    idx = jnp.arange(c)
    incl = idx[:, None] >= idx[None, :]
    strict = idx[:, None] > idx[None, :]
    decay = jnp.exp(jnp.where(incl, g[..., :, None] - g[..., None, :], -jnp.inf))
    k_beta = k * beta
    a = jnp.where(strict, jnp.einsum('bhnid,bhnjd->bhnij', k_beta, k) * decay, 0.0)
    eye = jnp.eye(c, dtype=jnp.float32)
    t_inv = lax.linalg.triangular_solve(a + eye, jnp.broadcast_to(eye, a.shape),
                                        left_side=True, lower=True, unit_diagonal=True)
    u = jnp.einsum('bhnij,bhnjv->bhniv', t_inv, v * beta)
    w = jnp.einsum('bhnij,bhnjk->bhnik', t_inv, k_beta * jnp.exp(g)[..., None])
    intra = jnp.einsum('bhnid,bhnjd->bhnij', q, k) * decay
    q_dec = q * jnp.exp(g)[..., None]
    g_last = g[..., -1]
    k_dec = k * jnp.exp(g_last[..., None] - g)[..., None]

    def step(state, inp):
        u_c, w_c, intra_c, q_c, k_c, gl_c = inp
        v_new = u_c - jnp.einsum('bhck,bhkv->bhcv', w_c, state)
        o_c = jnp.einsum('bhck,bhkv->bhcv', q_c, state) + jnp.einsum('bhij,bhjv->bhiv', intra_c, v_new)
        state = state * jnp.exp(gl_c)[..., None, None] + jnp.einsum('bhck,bhcv->bhkv', k_c, v_new)
        return state, o_c

    xs = (jnp.moveaxis(u, 2, 0), jnp.moveaxis(w, 2, 0), jnp.moveaxis(intra, 2, 0),
          jnp.moveaxis(q_dec, 2, 0), jnp.moveaxis(k_dec, 2, 0), jnp.moveaxis(g_last, 2, 0))
    state0 = jnp.zeros((b, h, dk, dv), jnp.float32)
    _, o = lax.scan(step, state0, xs)
    return jnp.moveaxis(o, 0, 2).reshape(b, h, s, dv)


def even_mixer(h, w_in, conv_w, lam_q1, lam_k1, lam_q2, lam_k2, da_norm, a_log, dt_bias, gdn_norm, w_out, layer):
    b, s, _ = h.shape
    f32 = jnp.float32
    proj = jnp.einsum('bsd,de->bse', h, w_in)
    da_q, da_k, da_v, gdn_qkv, gdn_z, gdn_a, gdn_b = jnp.split(proj, EVEN_SPLITS, axis=-1)
    q = da_q.reshape(b, s, DA_HEADS, 2, DA_QK_DIM).transpose(3, 0, 2, 1, 4)
    k = da_k.reshape(b, s, DA_HEADS, 2, DA_QK_DIM).transpose(3, 0, 2, 1, 4)
    v = _split_heads(da_v, DA_HEADS)
    lam_init = 0.8 - 0.6 * math.exp(-0.3 * layer)
    lam = (jnp.exp(jnp.dot(lam_q1.astype(f32), lam_k1.astype(f32)))
           - jnp.exp(jnp.dot(lam_q2.astype(f32), lam_k2.astype(f32))) + lam_init)
    slopes = 2.0 ** (-8.0 * jnp.arange(1, DA_HEADS + 1, dtype=f32) / DA_HEADS)
    o_a = differential_attention(q[0], q[1], k[0], k[1], v, lam, slopes)
    o_a = _merge_heads(rms_norm(o_a, da_norm) * (1.0 - lam_init)).astype(h.dtype)
    qkv = jax.nn.silu(causal_depthwise_conv(gdn_qkv, conv_w))
    gq, gk, gv = jnp.split(qkv, (GDN_QK_COLS, 2 * GDN_QK_COLS), axis=-1)
    gq = _l2norm(_split_heads(gq, GDN_HEADS).astype(f32)) * (GDN_K_DIM ** -0.5)
    gk = _l2norm(_split_heads(gk, GDN_HEADS).astype(f32))
    gv = _split_heads(gv, GDN_HEADS).astype(f32)
    log_decay = -jnp.exp(a_log.astype(f32)) * jax.nn.softplus(gdn_a.astype(f32) + dt_bias.astype(f32))
    beta = jax.nn.sigmoid(gdn_b.astype(f32))
    o_b = gated_delta_rule(gq, gk, gv, log_decay.transpose(0, 2, 1), beta.transpose(0, 2, 1))
    z = _split_heads(gdn_z, GDN_HEADS).astype(f32)
    o_b = _merge_heads(rms_norm(o_b, gdn_norm) * jax.nn.silu(z)).astype(h.dtype)
    mixed = jnp.concatenate([o_a, o_b], axis=-1)
    return jnp.einsum('bse,ed->bsd', mixed, w_out)


def odd_mixer(h, w_in, b_f, w_out):
    proj = jnp.einsum('bsd,de->bse', h, w_in)
    q, k, v, f_logit = jnp.split(proj, (FOX_COLS, 2 * FOX_COLS, 3 * FOX_COLS), axis=-1)
    log_f = jax.nn.log_sigmoid(f_logit.astype(jnp.float32) + b_f.astype(jnp.float32))
    cum_log_f = jnp.cumsum(log_f, axis=1).transpose(0, 2, 1)
    o = forgetting_attention(_split_heads(q, FOX_HEADS), _split_heads(k, FOX_HEADS),
                             _split_heads(v, FOX_HEADS), cum_log_f)
    return jnp.einsum('bse,ed->bsd', _merge_heads(o), w_out)


def squared_relu_mlp(h, w_up, w_down):
    a = jax.nn.relu(jnp.einsum('bsd,df->bsf', h, w_up))
    return jnp.einsum('bsf,fd->bsd', a * a, w_down)


def setup_inputs(seed: int = 0) -> dict:
    key = jax.random.key(seed)
    ks = jax.random.split(key, 20)
    f32 = jnp.float32

    def nrm(k, shape, std):
        return std * jax.random.normal(k, shape, f32)

    dt = jnp.exp(jax.random.uniform(ks[12], (N_EVEN, GDN_HEADS), f32, math.log(1e-3), math.log(1e-1)))
    return {
        'x': nrm(ks[0], (BATCH, SEQ, D_MODEL), 1.0),
        'norm_mix': 1.0 + nrm(ks[1], (DEPTH, D_MODEL), 0.02),
        'norm_mlp': 1.0 + nrm(ks[2], (DEPTH, D_MODEL), 0.02),
        'norm_final': 1.0 + nrm(ks[3], (D_MODEL,), 0.02),
        'w_in_even': nrm(ks[4], (N_EVEN, D_MODEL, EVEN_IN), D_MODEL ** -0.5),
        'conv_w': nrm(ks[5], (N_EVEN, CONV_WIDTH, GDN_CONV_COLS), CONV_WIDTH ** -0.5),
        'lam_q1': nrm(ks[6], (N_EVEN, DA_QK_DIM), 0.1),
        'lam_k1': nrm(ks[7], (N_EVEN, DA_QK_DIM), 0.1),
        'lam_q2': nrm(ks[8], (N_EVEN, DA_QK_DIM), 0.1),
        'lam_k2': nrm(ks[9], (N_EVEN, DA_QK_DIM), 0.1),
        'da_norm': 1.0 + nrm(ks[10], (N_EVEN, DA_V_DIM), 0.02),
        'gdn_a_log': jnp.log(jax.random.uniform(ks[11], (N_EVEN, GDN_HEADS), f32, 1.0, 16.0)),
        'gdn_dt_bias': dt + jnp.log(-jnp.expm1(-dt)),
        'gdn_norm': 1.0 + nrm(ks[13], (N_EVEN, GDN_V_DIM), 0.02),
        'w_out_even': nrm(ks[14], (N_EVEN, EVEN_MIX, D_MODEL), EVEN_MIX ** -0.5),
        'w_in_odd': nrm(ks[15], (N_ODD, D_MODEL, ODD_IN), D_MODEL ** -0.5),
        'fox_b_f': nrm(ks[16], (N_ODD, FOX_HEADS), 0.1),
        'w_out_odd': nrm(ks[17], (N_ODD, FOX_COLS, D_MODEL), FOX_COLS ** -0.5),
        'w_up': nrm(ks[18], (DEPTH, D_MODEL, D_FF), D_MODEL ** -0.5),
        'w_down': nrm(ks[19], (DEPTH, D_FF, D_MODEL), D_FF ** -0.5),
    }


def reference(x, norm_mix, norm_mlp, norm_final, w_in_even, conv_w, lam_q1, lam_k1, lam_q2, lam_k2,
              da_norm, gdn_a_log, gdn_dt_bias, gdn_norm, w_out_even, w_in_odd, fox_b_f, w_out_odd,
              w_up, w_down):
    h = x
    for layer in range(DEPTH):
        i = layer // 2
        hn = rms_norm(h, norm_mix[layer])
        if layer % 2 == 0:
            h = h + even_mixer(hn, w_in_even[i], conv_w[i], lam_q1[i], lam_k1[i], lam_q2[i], lam_k2[i],
                               da_norm[i], gdn_a_log[i], gdn_dt_bias[i], gdn_norm[i], w_out_even[i], layer)
        else:
            h = h + odd_mixer(hn, w_in_odd[i], fox_b_f[i], w_out_odd[i])
        h = h + squared_relu_mlp(rms_norm(h, norm_mlp[layer]), w_up[layer], w_down[layer])
    return rms_norm(h, norm_final)
```

```python
import contextlib
import math
import numpy as np
import concourse.bass as bass
import concourse.mybir as mybir
from concourse.bass_utils import run_bass_kernel_spmd

F32 = mybir.dt.float32
BF16 = mybir.dt.bfloat16
AF = mybir.ActivationFunctionType
ALU = mybir.AluOpType

ENGS = ('pe', 'act', 'dve', 'pool', 'sp')
T = 2048
D = 2048
DFF = 8192
EVEN_IN = 7184
ODD_IN = 6160
NEG = -30000.0


class Prog:
    def __init__(self, nc):
        self.nc = nc
        self.ops = []
        self.last_w = {}
        self.readers = {}
        self.stream_last = {}
        self.eng_last = {}
        self.pending_bar = {}
        self.stack = contextlib.ExitStack()

    def sbuf(self, name, shape, dtype):
        return self.stack.enter_context(self.nc.sbuf_tensor("sb_" + name, list(shape), dtype))

    def psum(self, name, shape, dtype):
        return self.stack.enter_context(self.nc.psum_tensor(name, list(shape), dtype))

    def add(self, eng, fn, r=(), w=(), stream=None):
        idx = len(self.ops)
        raw = set()
        oth = set()
        for k in r:
            lw = self.last_w.get(k)
            if lw is not None:
                raw.add(lw)
        for k in w:
            lw = self.last_w.get(k)
            if lw is not None:
                oth.add(lw)
            oth.update(self.readers.get(k, ()))
        if stream is not None:
            p = self.stream_last.get(stream)
            if p is not None:
                raw.add(p)
            self.stream_last[stream] = idx
        if eng in self.pending_bar:
            raw |= self.pending_bar.pop(eng)
        for k in r:
            self.readers.setdefault(k, []).append(idx)
        for k in w:
            self.last_w[k] = idx
            self.readers[k] = []
        raw.discard(idx)
        oth.discard(idx)
        self.ops.append(dict(eng=eng, fn=fn, raw=raw, oth=oth - raw, stream=stream, bar=False))
        if stream is None:
            self.eng_last[eng] = idx
        return idx

    def barrier(self):
        deps = set(self.eng_last.values()) | set(self.stream_last.values())
        for e in ENGS:
            self.pending_bar[e] = set(deps) | self.pending_bar.get(e, set())
        self.last_w = {}
        self.readers = {}

    def pe(self, fn, r=(), w=()):
        return self.add('pe', fn, r, w)

    def act(self, fn, r=(), w=()):
        return self.add('act', fn, r, w)

    def dve(self, fn, r=(), w=()):
        return self.add('dve', fn, r, w)

    def pool(self, fn, r=(), w=()):
        return self.add('pool', fn, r, w)

    def dma(self, eng, fn, r=(), w=(), stream=None):
        return self.add(eng, fn, r, w, stream=stream)

    def emit(self, final_wait_streams=()):
        nc = self.nc
        ops = self.ops
        for o in ops:
            deps = set()
            for d in o['raw']:
                if ops[d]['stream'] is None and ops[d]['eng'] == o['eng'] and o['eng'] == 'pe':
                    continue
                deps.add(d)
            for d in o['oth']:
                if ops[d]['stream'] is None and ops[d]['eng'] == o['eng']:
                    continue
                deps.add(d)
            best = {}
            for d in deps:
                k = ('s', ops[d]['stream']) if ops[d]['stream'] is not None else ('e', ops[d]['eng'])
                if k not in best or best[k] < d:
                    best[k] = d
            o['deps'] = set(best.values())
        needed = set()
        for o in ops:
            needed |= o['deps']
        cnt = {e: 0 for e in ENGS}
        scnt = {}
        for i, o in enumerate(ops):
            if o['stream'] is not None:
                s = o['stream']
                scnt[s] = scnt.get(s, 0) + 16
                o['done'] = (('dma', s), scnt[s])
            elif i in needed:
                cnt[o['eng']] += 1
                o['done'] = (('eng', o['eng']), cnt[o['eng']])
            else:
                o['done'] = None
        self.sem_counts = dict(cnt)
        sems = {}
        for e in ENGS:
            sems[('eng', e)] = self.stack.enter_context(nc.semaphore('s_' + e))
        for s in scnt:
            sems[('dma', s)] = self.stack.enter_context(nc.semaphore('d_' + str(s)))
        self.n_sems = len(sems)

        def run_engine(eng_name, engine):
            waited = {}
            for o in ops:
                if o['eng'] != eng_name:
                    continue
                need = {}
                for d in o['deps']:
                    semkey, val = ops[d]['done']
                    if need.get(semkey, 0) < val:
                        need[semkey] = val
                for semkey, val in need.items():
                    if waited.get(semkey, 0) >= val:
                        continue
                    engine.wait_ge(sems[semkey], val)
                    waited[semkey] = val
                ins = o['fn'](engine)
                if o['done'] is not None:
                    semkey, val = o['done']
                    ins.then_inc(sems[semkey], 16 if semkey[0] == 'dma' else 1)
            if eng_name == 'sp':
                for s in final_wait_streams:
                    engine.wait_ge(sems[('dma', s)], scnt[s])

        with nc.Block() as block:
            @block.tensor
            def _(e):
                run_engine('pe', e)

            @block.scalar
            def _(e):
                run_engine('act', e)

            @block.vector
            def _(e):
                run_engine('dve', e)

            @block.gpsimd
            def _(e):
                run_engine('pool', e)

            @block.sync
            def _(e):
                run_engine('sp', e)
        self.stack.close()


class Arena:
    def __init__(self, P, nbytes):
        self.t32 = P.sbuf("arena", [128, nbytes // 4], F32)
        self.t16 = self.t32.bitcast(BF16)
        self.nbytes = nbytes
        self.off = 0
        self.uid = 0

    def reset(self, off=0):
        self.off = off

    def _take(self, nbytes):
        o = self.off
        self.off += (nbytes + 31) // 32 * 32
        assert self.off <= self.nbytes, ("arena overflow", self.off, self.nbytes)
        return o

    def f32(self, n, parts=128):
        o = self._take(n * 4)
        return self.t32[0:parts, o // 4:o // 4 + n]

    def bf(self, n, parts=128):
        o = self._take(n * 2)
        return self.t16[0:parts, o // 2:o // 2 + n]


FULL_PLAN = [('even', 0), ('mlp', 0), ('fox', 1), ('mlp', 1), ('even', 2), ('mlp', 2), ('fox', 3), ('mlp', 3)]


def build_program(plan=None, dbg=False):
    plan = FULL_PLAN if plan is None else plan
    nc = bass.Bass("TRN2", target_bir_lowering=False)
    P = Prog(nc)

    def din(name, shape, dt=F32):
        return nc.dram_tensor(name, list(shape), dt, kind="ExternalInput").ap()

    x = din("x", [T, D])
    w_in_even = din("w_in_even", [2, D, EVEN_IN])
    w_out_even = din("w_out_even", [2, D, D])
    w_in_odd = din("w_in_odd", [2, D, ODD_IN])
    w_out_odd = din("w_out_odd", [2, D, D])
    w_up = din("w_up", [4, D, DFF])
    w_down = din("w_down", [4, DFF, D])
    sm_d = din("sm", [128, SM_COLS])
    c_ident = din("c_ident", [128, 128])
    c_maskneg = din("c_maskneg", [128, 128])
    c_alibi = din("c_alibi", [4, 2 * T])
    c_tri = din("c_tri", [64, 3 * 512])
    c_cmask = din("c_cmask", [8, T])
    y = nc.dram_tensor("y", [T, D], F32, kind="ExternalOutput").ap()
    hT = nc.dram_tensor("hT", [16, 128, T], F32).ap()
    oscr = nc.dram_tensor("oscr", [16, 128, T], BF16, **({"kind": "ExternalOutput"} if dbg else {})).ap()
    pscr = nc.dram_tensor("pscr", [32, 128, T], F32, **({"kind": "ExternalOutput"} if dbg else {})).ap()
    rows = nc.dram_tensor("rows", [10, 8, T], F32, **({"kind": "ExternalOutput"} if dbg else {})).ap()
    eglr = nc.dram_tensor("eglr", [8, 32], F32).ap()
    cbs = nc.dram_tensor("cbs", [6, 16, T], BF16).ap()

    sm = P.sbuf("sm", [128, SM_COLS], F32)
    ident = P.sbuf("ident", [128, 128], F32)
    identb = P.sbuf("identb", [128, 128], BF16)
    masknegb = P.sbuf("masknegb", [128, 128], BF16)
    ones32 = P.sbuf("ones32", [128, 128], F32)
    onesb = P.sbuf("onesb", [128, 128], BF16)
    ps = [P.psum("ps%d" % i, [128, 512], F32) for i in range(8)]
    AR = Arena(P, 202 * 1024)

    uid = [0]

    def U(s):
        uid[0] += 1
        return "%s_%d" % (s, uid[0])

    P.dma('sp', lambda e: e.dma_start(out=sm[:], in_=sm_d), w=['sm'], stream='c0')
    P.dma('sp', lambda e: e.dma_start(out=ident[:], in_=c_ident), w=['ident'], stream='c1')
    P.dma('pool', lambda e: e.dma_start(out=masknegb[:], in_=c_maskneg), w=['masknegb'], stream='c2')
    P.dve(lambda e: e.tensor_copy(out=identb[:], in_=ident[:]), r=['ident'], w=['identb'])
    P.dve(lambda e: e.memset(ones32[:], 1.0), w=['ones32'])
    P.dve(lambda e: e.memset(onesb[:], 1.0), w=['onesb'])
    P.barrier()

    wstate = dict(n=0, bufs=None)

    def wtile(src, ncols=512, nk=16):
        b = wstate['n'] % len(wstate['bufs'])
        wstate['n'] += 1
        buf = wstate['bufs'][b]
        view = buf[:, 0:nk * ncols].rearrange("p (k n) -> p k n", k=nk, n=ncols)
        key = 'wb%d' % b
        P.dma('pool', lambda e: e.dma_start(out=view, in_=src.rearrange("(k p) n -> p k n", p=128)),
              w=[key], stream=key)
        return view, key

    def phase_load_x():
        AR.reset()
        xs = [AR.f32(D) for _ in range(2)]
        xo = [AR.f32(2048) for _ in range(2)]
        for tt in range(16):
            b = tt % 2
            P.dma('sp', lambda e, b=b, tt=tt: e.dma_start(out=xs[b], in_=x[tt * 128:(tt + 1) * 128, :]),
                  w=['xs%d' % b], stream='xs%d' % b)
            for q in range(4):
                bank = (tt * 4 + q) % 8
                for j in range(4):
                    kc = q * 4 + j
                    P.pe(lambda e, b=b, kc=kc, bank=bank, j=j: e.transpose(
                        out=ps[bank][:, j * 128:(j + 1) * 128], in_=xs[b][:, kc * 128:(kc + 1) * 128], identity=ident[:]),
                        r=['xs%d' % b, 'ident'], w=['ps%d' % bank])
                eng = P.act if q % 2 == 0 else P.dve
                if q % 2 == 0:
                    P.act(lambda e, b=b, q=q, bank=bank: e.activation(out=xo[b][:, q * 512:(q + 1) * 512], in_=ps[bank][:], func=AF.Copy),
                          r=['ps%d' % bank], w=['xo%d_%d' % (b, q)])
                else:
                    P.dve(lambda e, b=b, q=q, bank=bank: e.tensor_copy(out=xo[b][:, q * 512:(q + 1) * 512], in_=ps[bank][:]),
                          r=['ps%d' % bank], w=['xo%d_%d' % (b, q)])
            P.dma('sp', lambda e, b=b, tt=tt: e.dma_start(
                out=hT[:, :, tt * 128:(tt + 1) * 128].rearrange("k p t -> p k t"),
                in_=xo[b].rearrange("p (k t) -> p k t", k=16, t=128)),
                r=['xo%d_%d' % (b, q) for q in range(4)], w=['hT'], stream='xo%d' % b)
        P.barrier()

    def phase_norm(norm_idx, hn):
        base = AR.off
        st1 = AR.f32(16 * 512)
        st = [st1, st1]
        sq = AR.bf(16 * 512)
        rs = AR.f32(512)
        for tb in range(4):
            b = 0
            P.dma('sp', lambda e, b=b, tb=tb: e.dma_start(
                out=st[b].rearrange("p (k t) -> p k t", k=16, t=512),
                in_=hT[:, :, tb * 512:(tb + 1) * 512].rearrange("k p t -> p k t")),
                w=['nst%d' % b], stream='nst%d' % b)
            P.act(lambda e, b=b: e.activation(out=sq, in_=st[b], func=AF.Square), r=['nst%d' % b], w=['nsq'])
            bank = tb % 2
            for kc in range(16):
                P.pe(lambda e, kc=kc, bank=bank: e.matmul(ps[bank][:], lhsT=onesb[:], rhs=sq[:, kc * 512:(kc + 1) * 512],
                                                           start=(kc == 0), stop=(kc == 15)),
                     r=['nsq', 'onesb'], w=['ps%d' % bank])
            P.act(lambda e, bank=bank: e.activation(out=rs, in_=ps[bank][:], func=AF.Sqrt, bias=1e-6, scale=1.0 / D),
                  r=['ps%d' % bank], w=['nrs'])
            P.dve(lambda e: e.reciprocal(out=rs, in_=rs), r=['nrs'], w=['nrs'])
            for kc in range(16):
                P.dve(lambda e, b=b, kc=kc, tb=tb: e.scalar_tensor_tensor(
                    out=hn[:, kc * T + tb * 512: kc * T + (tb + 1) * 512], in0=st[b][:, kc * 512:(kc + 1) * 512],
                    scalar=sm[:, norm_idx * 16 + kc: norm_idx * 16 + kc + 1], in1=rs, op0=ALU.mult, op1=ALU.mult),
                    r=['nst%d' % b, 'nrs', 'sm'], w=['hn'])
        AR.reset(base)
        P.barrier()

    att = dict(n=0, sc=0)

    pj = dict(n=0)

    def proj_fm(hn, wv, wkey, c0, M, evac):
        for tb in range(4):
            bank = pj['n'] % 4 + 4
            pj['n'] += 1
            for kc in range(16):
                P.pe(lambda e, bank=bank, kc=kc, tb=tb: e.matmul(
                    ps[bank][0:M, :], lhsT=wv[:, kc, c0:c0 + M], rhs=hn[:, kc * T + tb * 512: kc * T + (tb + 1) * 512],
                    start=(kc == 0), stop=(kc == 15)), r=[wkey, 'hn'], w=['ps%d' % bank])
            evac(tb, bank)

    def phase_outproj(w_out):
        AR.reset()
        oall = AR.bf(16 * T)
        wstate['bufs'] = [AR.bf(16 * 512) for _ in range(3)]
        hst = [AR.f32(512) for _ in range(3)]
        for k4 in range(4):
            P.dma('sp', lambda e, k4=k4: e.dma_start(
                out=oall[:, k4 * 4 * T:(k4 + 1) * 4 * T].rearrange("p (k t) -> p k t", k=4, t=T),
                in_=oscr[k4 * 4:(k4 + 1) * 4].rearrange("k p t -> p k t")), w=['hn'], stream='oall%d' % k4)
        residual_matmul(oall, 16, lambda cg: w_out[:, cg * 512:(cg + 1) * 512], hst)
        P.barrier()

    rz = dict(n=0)

    def residual_matmul(act, nk, wsrc, hst):
        for cg in range(4):
            tiles = []
            for kg in range(nk // 16):
                src = wsrc(cg)
                tiles.append(wtile(src[kg * 2048:(kg + 1) * 2048, :]))
            for c4 in range(4):
                dc = cg * 4 + c4
                for tb in range(4):
                    n = rz['n']
                    rz['n'] += 1
                    bank = n % 8
                    hb = n % 3
                    P.dma('sp', lambda e, hb=hb, dc=dc, tb=tb: e.dma_start(out=hst[hb], in_=hT[dc, :, tb * 512:(tb + 1) * 512]),
                          r=['hT%d_%d' % (dc, tb)], w=['hst%d' % hb], stream='hst%d' % hb)
                    for kk in range(nk):
                        wv, wkey = tiles[kk // 16]
                        P.pe(lambda e, bank=bank, wv=wv, kk=kk, c4=c4, tb=tb: e.matmul(
                            ps[bank][:], lhsT=wv[:, kk % 16, c4 * 128:(c4 + 1) * 128],
                            rhs=act[:, kk * T + tb * 512: kk * T + (tb + 1) * 512], start=(kk == 0), stop=(kk == nk - 1)),
                            r=[wkey, 'hn'], w=['ps%d' % bank])
                    P.dve(lambda e, bank=bank, hb=hb: e.tensor_tensor(out=hst[hb], in0=ps[bank][:], in1=hst[hb], op=ALU.add),
                          r=['ps%d' % bank, 'hst%d' % hb], w=['hst%d' % hb])
                    P.dma('sp', lambda e, hb=hb, dc=dc, tb=tb: e.dma_start(out=hT[dc, :, tb * 512:(tb + 1) * 512], in_=hst[hb]),
                          r=['hst%d' % hb], w=['hT%d_%d' % (dc, tb)], stream='hst%d' % hb)

    def phase_mlp(l):
        AR.reset()
        hn = AR.bf(16 * T)
        wstate['bufs'] = [AR.bf(16 * 512) for _ in range(3)]
        phase_norm(4 + l, hn)
        aT = AR.bf(64 * 512)
        rl = [AR.f32(512) for _ in range(2)]
        hst = [AR.f32(512) for _ in range(3)]
        n = 0
        for tb in range(4):
            for fg in range(16):
                wv, wkey = wtile(w_up[l][:, fg * 512:(fg + 1) * 512])
                for c4 in range(4):
                    bank = n % 8
                    rb_ = n % 2
                    n += 1
                    for kc in range(16):
                        P.pe(lambda e, bank=bank, wv=wv, kc=kc, c4=c4, tb=tb: e.matmul(
                            ps[bank][:], lhsT=wv[:, kc, c4 * 128:(c4 + 1) * 128],
                            rhs=hn[:, kc * T + tb * 512: kc * T + (tb + 1) * 512], start=(kc == 0), stop=(kc == 15)),
                            r=[wkey, 'hn'], w=['ps%d' % bank])
                    P.act(lambda e, bank=bank, rb_=rb_: e.activation(out=rl[rb_], in_=ps[bank][:], func=AF.Relu),
                          r=['ps%d' % bank], w=['rl%d' % rb_])
                    fc = fg * 4 + c4
                    P.dve(lambda e, rb_=rb_, fc=fc: e.tensor_tensor(out=aT[:, fc * 512:(fc + 1) * 512], in0=rl[rb_], in1=rl[rb_], op=ALU.mult),
                          r=['rl%d' % rb_], w=['aT%d' % fc])
            for cg in range(4):
                banks = [(n + j) % 8 for j in range(4)]
                n += 4
                hbs = []
                for c4 in range(4):
                    dc = cg * 4 + c4
                    hb = rz['n'] % 3
                    rz['n'] += 1
                    hbs.append(hb)
                for fr in range(4):
                    wv, wkey = wtile(w_down[l][fr * 2048:(fr + 1) * 2048, cg * 512:(cg + 1) * 512])
                    for c4 in range(4):
                        for fc in range(16):
                            ff = fr * 16 + fc
                            P.pe(lambda e, bank=banks[c4], wv=wv, fc=fc, c4=c4, ff=ff, fr=fr: e.matmul(
                                ps[bank][:], lhsT=wv[:, fc, c4 * 128:(c4 + 1) * 128], rhs=aT[:, ff * 512:(ff + 1) * 512],
                                start=(fr == 0 and fc == 0), stop=(fr == 3 and fc == 15)),
                                r=[wkey, 'aT%d' % ff], w=['ps%d' % banks[c4]])
                for c4 in range(4):
                    dc = cg * 4 + c4
                    hb = hbs[c4]
                    bank = banks[c4]
                    P.dma('sp', lambda e, hb=hb, dc=dc, tb=tb: e.dma_start(out=hst[hb], in_=hT[dc, :, tb * 512:(tb + 1) * 512]),
                          r=['hT%d_%d' % (dc, tb)], w=['hst%d' % hb], stream='hst%d' % hb)
                    P.dve(lambda e, bank=bank, hb=hb: e.tensor_tensor(out=hst[hb], in0=ps[bank][:], in1=hst[hb], op=ALU.add),
                          r=['ps%d' % bank, 'hst%d' % hb], w=['hst%d' % hb])
                    P.dma('sp', lambda e, hb=hb, dc=dc, tb=tb: e.dma_start(out=hT[dc, :, tb * 512:(tb + 1) * 512], in_=hst[hb]),
                          r=['hst%d' % hb], w=['hT%d_%d' % (dc, tb)], stream='hst%d' % hb)
        P.barrier()

    def phase_fox(l):
        i = l // 2
        win = w_in_odd[i]
        AR.reset()
        hn = AR.bf(16 * T)
        wstate['bufs'] = [AR.bf(16 * 512) for _ in range(3)]
        phase_norm(l, hn)
        base0 = AR.off
        wf = AR.bf(16 * 16)
        frow = AR.f32(T, parts=16)
        c3 = [AR.f32(T, parts=16) for _ in range(2)]
        cb = [AR.bf(T, parts=16) for _ in range(6)]
        wfv = wf.rearrange("p (k n) -> p k n", k=16, n=16)
        P.dma('pool', lambda e: e.dma_start(out=wfv, in_=win[:, 6144:6160].rearrange("(k p) n -> p k n", p=128)),
              w=['wf'], stream='wf')
        P.dve(lambda e: e.tensor_scalar(out=c3[1][:, 0:1], in0=sm[0:16, SM_FOXB + i:SM_FOXB + i + 1], scalar1=-1.0, scalar2=None,
                                        op0=ALU.mult), r=['sm'], w=['nbf'])

        def ev_f(tb, bank):
            P.act(lambda e, tb=tb, bank=bank: e.activation(out=frow[:, tb * 512:(tb + 1) * 512], in_=ps[bank][0:16, :], func=AF.Exp,
                                                           bias=c3[1][:, 0:1], scale=-1.0), r=['ps%d' % bank, 'nbf'], w=['frow'])
        proj_fm(hn, wfv, 'wf', 0, 16, ev_f)
        P.act(lambda e: e.activation(out=frow, in_=frow, func=AF.Ln, bias=1.0, scale=1.0), r=['frow'], w=['frow'])
        P.dve(lambda e: e.memset(c3[1], 1.0), r=['frow'], w=['nbf'])
        P.dve(lambda e: e.tensor_tensor_scan(out=c3[0], data0=c3[1], data1=frow, initial=0.0, op0=ALU.mult, op1=ALU.add),
              r=['frow', 'nbf'], w=['cpos'])
        P.dve(lambda e: e.tensor_copy(out=cb[0], in_=c3[0]), r=['cpos'], w=['cb0'])
        P.dve(lambda e: e.tensor_tensor(out=c3[1], in0=c3[0], in1=cb[0], op=ALU.subtract), r=['cpos', 'cb0'], w=['nbf'])
        P.dve(lambda e: e.tensor_copy(out=cb[1], in_=c3[1]), r=['nbf'], w=['cb1'])
        P.dve(lambda e: e.tensor_tensor(out=c3[0], in0=c3[1], in1=cb[1], op=ALU.subtract), r=['nbf', 'cb1'], w=['cpos'])
        P.dve(lambda e: e.tensor_copy(out=cb[2], in_=c3[0]), r=['cpos'], w=['cb2'])
        for j in range(3):
            P.dve(lambda e, j=j: e.tensor_scalar(out=cb[3 + j], in0=cb[j], scalar1=-1.0, scalar2=None, op0=ALU.mult),
                  r=['cb%d' % j], w=['cb%d' % (3 + j)])
        for j in range(6):
            P.dma('sp', lambda e, j=j: e.dma_start(out=cbs[j], in_=cb[j]), r=['cb%d' % j], w=['cbs'], stream='cbs%d' % j)
        P.barrier()
        AR.reset(base0)
        QT = AR.bf(4 * T)
        KT = AR.bf(4 * T)
        V = AR.bf(16 * 512)
        pT = [AR.bf(512) for _ in range(3)]
        osb = [AR.bf(512) for _ in range(2)]
        rec = [AR.f32(512) for _ in range(2)]
        lb = [AR.bf(T, parts=6) for _ in range(2)]
        rb = [AR.bf(T, parts=6) for _ in range(2)]
        for b in range(2):
            P.dve(lambda e, b=b: e.memset(lb[b], 1.0), w=['lb%d' % b])
            P.dve(lambda e, b=b: e.memset(rb[b], 1.0), w=['rb%d' % b])
        scale = 128 ** -0.5
        for g in range(4):
            wq, kq = wtile(win[:, g * 512:(g + 1) * 512])
            for hh in range(4):
                def ev_q(tb, bank, hh=hh):
                    P.act(lambda e, tb=tb, bank=bank: e.activation(out=QT[:, hh * T + tb * 512: hh * T + (tb + 1) * 512], in_=ps[bank][:],
                                                                   func=AF.Copy, scale=scale), r=['ps%d' % bank], w=['QT'])
                proj_fm(hn, wq, kq, hh * 128, 128, ev_q)
            wk, kk_ = wtile(win[:, 2048 + g * 512:2048 + (g + 1) * 512])
            for hh in range(4):
                def ev_k(tb, bank, hh=hh):
                    P.dve(lambda e, tb=tb, bank=bank: e.tensor_copy(out=KT[:, hh * T + tb * 512: hh * T + (tb + 1) * 512], in_=ps[bank][:]),
                          r=['ps%d' % bank], w=['KT'])
                proj_fm(hn, wk, kk_, hh * 128, 128, ev_k)
            wv, kv = wtile(win[:, 4096 + g * 512:4096 + (g + 1) * 512])
            for tt in range(16):
                bank = pj['n'] % 4 + 4
                pj['n'] += 1
                for kc in range(16):
                    P.pe(lambda e, bank=bank, kc=kc, tt=tt, wv=wv: e.matmul(ps[bank][:], lhsT=hn[:, kc * T + tt * 128: kc * T + (tt + 1) * 128],
                                                                           rhs=wv[:, kc, :], start=(kc == 0), stop=(kc == 15)),
                         r=[kv, 'hn'], w=['ps%d' % bank])
                if tt % 2 == 0:
                    P.act(lambda e, bank=bank, tt=tt: e.activation(out=V[:, tt * 512:(tt + 1) * 512], in_=ps[bank][:], func=AF.Copy),
                          r=['ps%d' % bank], w=['V'])
                else:
                    P.dve(lambda e, bank=bank, tt=tt: e.tensor_copy(out=V[:, tt * 512:(tt + 1) * 512], in_=ps[bank][:]),
                          r=['ps%d' % bank], w=['V'])
            for hh in range(4):
                h = g * 4 + hh
                b = h % 2
                P.dma('sp', lambda e, b=b, h=h: e.dma_start(out=lb[b][3:6, :], in_=cbs[0:3, h, :]), r=['cbs'], w=['lb%d' % b], stream='lb%d' % b)
                P.dma('sp', lambda e, b=b, h=h: e.dma_start(out=rb[b][0:3, :], in_=cbs[3:6, h, :]), r=['cbs'], w=['rb%d' % b], stream='rb%d' % b)

                def osbf(qb, h=h):
                    ob_ = (h * 4 + qb) % 2
                    return osb[ob_], 'osb%d' % ob_, rec[ob_], 'rec%d' % ob_
                attention_with_store(QT[:, hh * T:(hh + 1) * T], KT[:, hh * T:(hh + 1) * T],
                                     lambda kt, hh=hh: V[:, kt * 512 + hh * 128: kt * 512 + (hh + 1) * 128],
                                     lb[b], rb[b], pT, osbf, ['QT', 'KT', 'V', 'lb%d' % b, 'rb%d' % b], h)
        P.barrier()
        phase_outproj(w_out_odd[i])

    def attention_with_store(QT, KT, Vfn, lb, rb, pT, osbf, rkeys, h):
        stores = []

        def osb2(qb):
            return osbf(qb)
        attention_qb(QT, KT, Vfn, lb, rb, pT, osb2, rkeys, lambda qb, dst, dkey: P.dma(
            'sp', lambda e: e.dma_start(out=oscr[h, :, qb * 512:(qb + 1) * 512], in_=dst), r=[dkey], w=['oscr'], stream=dkey))

    def attention_qb(QT, KT, Vfn, lb, rb, pT, osb, rkeys, after):
        for qb in range(4):
            par = att['n'] % 2
            att['n'] += 1
            ob, sb_ = par, 2 + par
            nkt = 4 * (qb + 1)
            for kt in range(nkt):
                i = kt - 4 * qb
                q0 = qb * 512 + max(i, 0) * 128
                N = (qb + 1) * 512 - q0
                off = q0 - qb * 512
                sbk = 4 + att['sc'] % 4
                pb = att['sc'] % 3
                att['sc'] += 1
                P.pe(lambda e, sbk=sbk, kt=kt, q0=q0, N=N: e.matmul(ps[sbk][:, 0:N], lhsT=KT[:, kt * 128:(kt + 1) * 128],
                                                                     rhs=QT[:, q0:q0 + N], start=True, stop=False),
                     r=rkeys, w=['ps%d' % sbk])
                P.pe(lambda e, sbk=sbk, kt=kt, q0=q0, N=N, i=i: e.matmul(ps[sbk][:, 0:N], lhsT=lb[:, kt * 128:(kt + 1) * 128],
                                                                          rhs=rb[:, q0:q0 + N], start=False, stop=(i < 0)),
                     r=rkeys, w=['ps%d' % sbk])
                if i >= 0:
                    P.pe(lambda e, sbk=sbk: e.matmul(ps[sbk][:, 0:128], lhsT=identb[:], rhs=masknegb[:], start=False, stop=True),
                         r=['identb', 'masknegb'], w=['ps%d' % sbk])
                P.act(lambda e, sbk=sbk, pb=pb, N=N: e.activation(out=pT[pb][:, 0:N], in_=ps[sbk][:, 0:N], func=AF.Exp),
                      r=['ps%d' % sbk], w=['pT%d' % pb])
                P.pe(lambda e, ob=ob, kt=kt, pb=pb, off=off, N=N, nkt=nkt: e.matmul(
                    ps[ob][:, off:off + N], lhsT=Vfn(kt), rhs=pT[pb][:, 0:N], start=(kt == 0), stop=(kt == nkt - 1)),
                    r=rkeys + ['pT%d' % pb], w=['ps%d' % ob])
                P.pe(lambda e, sb_=sb_, kt=kt, pb=pb, off=off, N=N, nkt=nkt: e.matmul(
                    ps[sb_][:, off:off + N], lhsT=onesb[:], rhs=pT[pb][:, 0:N], start=(kt == 0), stop=(kt == nkt - 1)),
                    r=['onesb', 'pT%d' % pb], w=['ps%d' % sb_])
            dst, dkey, rec, rkey = osb(qb)
            P.dve(lambda e, sb_=sb_, rec=rec: e.reciprocal(out=rec, in_=ps[sb_][:]), r=['ps%d' % sb_], w=[rkey])
            P.dve(lambda e, ob=ob, dst=dst, rec=rec: e.tensor_tensor(out=dst, in0=ps[ob][:], in1=rec, op=ALU.mult),
                  r=['ps%d' % ob, rkey], w=[dkey])
            after(qb, dst, dkey)

    def phase_even(l):
        i = l // 2
        win = w_in_even[i]
        lam_init = 0.8 - 0.6 * math.exp(-0.3 * l)
        AR.reset()
        hn = AR.bf(16 * T)
        wstate['bufs'] = [AR.bf(16 * 512) for _ in range(2)]
        phase_norm(l, hn)
        base0 = AR.off
        e1, Gc, eb, beta, lnb, EG, BEG, tmp, nGc = [AR.f32(T, parts=8) for _ in range(9)]
        cm = AR.f32(T, parts=8)
        sm8 = AR.f32(64, parts=8)
        wab = AR.bf(16 * 16)
        wabv = wab.rearrange("p (k n) -> p k n", k=16, n=16)
        P.dma('pool', lambda e: e.dma_start(out=wabv, in_=win[:, 7168:7184].rearrange("(k p) n -> p k n", p=128)),
              w=['wab'], stream='wf')
        P.dma('sp', lambda e: e.dma_start(out=cm, in_=c_cmask), w=['cm'], stream='cm')
        P.act(lambda e: e.activation(out=sm8[:, 0:1], in_=sm[0:8, SM_ALOG + i:SM_ALOG + i + 1], func=AF.Exp), r=['sm'], w=['nA'])
        P.dve(lambda e: e.tensor_scalar(out=sm8[:, 0:1], in0=sm8[:, 0:1], scalar1=-1.0, scalar2=None, op0=ALU.mult), r=['nA'], w=['nA'])

        def ev_a(tb, bank):
            P.act(lambda e: e.activation(out=e1[:, tb * 512:(tb + 1) * 512], in_=ps[bank][0:8, :], func=AF.Exp,
                                         bias=sm[0:8, SM_DTB + i:SM_DTB + i + 1], scale=1.0), r=['ps%d' % bank, 'sm'], w=['e1'])
        proj_fm(hn, wabv, 'wab', 0, 8, ev_a)

        def ev_b(tb, bank):
            P.act(lambda e: e.activation(out=eb[:, tb * 512:(tb + 1) * 512], in_=ps[bank][0:8, :], func=AF.Exp, scale=-1.0),
                  r=['ps%d' % bank], w=['eb'])
        proj_fm(hn, wabv, 'wab', 8, 8, ev_b)
        P.act(lambda e: e.activation(out=e1, in_=e1, func=AF.Ln, bias=1.0, scale=1.0), r=['e1'], w=['e1'])
        P.dve(lambda e: e.tensor_scalar(out=e1, in0=e1, scalar1=sm8[:, 0:1], scalar2=None, op0=ALU.mult), r=['e1', 'nA'], w=['e1'])
        P.dve(lambda e: e.tensor_tensor_scan(out=Gc, data0=cm, data1=e1, initial=0.0, op0=ALU.mult, op1=ALU.add),
              r=['cm', 'e1'], w=['Gc'])
        P.dve(lambda e: e.tensor_scalar(out=eb, in0=eb, scalar1=1.0, scalar2=None, op0=ALU.add), r=['eb'], w=['eb'])
        P.dve(lambda e: e.reciprocal(out=beta, in_=eb), r=['eb'], w=['beta'])
        P.act(lambda e: e.activation(out=lnb, in_=eb, func=AF.Ln), r=['eb'], w=['lnb'])
        P.dve(lambda e: e.tensor_tensor(out=lnb, in0=Gc, in1=lnb, op=ALU.subtract), r=['Gc', 'lnb'], w=['lnb'])
        P.act(lambda e: e.activation(out=EG, in_=Gc, func=AF.Exp), r=['Gc'], w=['EG'])
        P.dve(lambda e: e.tensor_tensor(out=BEG, in0=beta, in1=EG, op=ALU.mult), r=['beta', 'EG'], w=['BEG'])
        Gc3 = Gc.rearrange("p (n c) -> p n c", n=32, c=64)
        P.dve(lambda e: e.tensor_copy(out=tmp.rearrange("p (n c) -> p n c", n=32, c=64), in_=Gc3[:, :, 63:64].to_broadcast([8, 32, 64])),
              r=['Gc'], w=['tmp'])
        P.dve(lambda e: e.tensor_tensor(out=tmp, in0=tmp, in1=Gc, op=ALU.subtract), r=['tmp', 'Gc'], w=['tmp'])
        P.act(lambda e: e.activation(out=tmp, in_=tmp, func=AF.Exp), r=['tmp'], w=['tmp'])
        P.act(lambda e: e.activation(out=sm8[:, 8:40], in_=Gc3[:, :, 63], func=AF.Exp), r=['Gc'], w=['egl'])
        P.dve(lambda e: e.tensor_scalar(out=nGc, in0=Gc, scalar1=-1.0, scalar2=None, op0=ALU.mult), r=['Gc'], w=['nGc'])
        for q, (tl, k_) in enumerate(((beta, 'beta'), (EG, 'EG'), (BEG, 'BEG'), (tmp, 'tmp'), (Gc, 'Gc'), (nGc, 'nGc'), (lnb, 'lnb'))):
            P.dma('sp', lambda e, q=q, tl=tl: e.dma_start(out=rows[q], in_=tl), r=[k_], w=['rows'], stream='rows%d' % q)
        P.dma('sp', lambda e: e.dma_start(out=eglr, in_=sm8[:, 8:40]), r=['egl'], w=['eglr'], stream='rows7')
        P.barrier()
        AR.reset(base0)
        QT = AR.bf(4 * T)
        KT = AR.bf(4 * T)
        V = AR.bf(16 * 512)
        pT = [AR.bf(512) for _ in range(3)]
        o12 = [AR.f32(T), AR.f32(T)]
        rec = [AR.f32(512) for _ in range(2)]
        sqd = AR.bf(T)
        rsd = AR.f32(512)
        ob = [AR.bf(512) for _ in range(2)]
        aL = AR.bf(T, parts=68)
        aR0 = AR.f32(T, parts=68)
        aR = [AR.bf(T, parts=68) for _ in range(2)]
        lamv = AR.f32(80, parts=1)
        nlam = AR.f32(2)
        gcol = AR.f32(1)
        for pb in (0, 64):
            P.dma('pool', lambda e, pb=pb: e.dma_start(out=aL[pb:pb + 4, :], in_=c_alibi[:, 0:T]), w=['aL'], stream='aL%d' % pb)
            P.dma('sp', lambda e, pb=pb: e.dma_start(out=aR0[pb:pb + 4, :], in_=c_alibi[:, T:2 * T]), w=['aR0'], stream='aR0%d' % pb)
        lo = SM_LAM + i * 256
        for j in range(2):
            P.dve(lambda e, j=j: e.tensor_tensor(out=lamv[:, 8:72], in0=sm[0:1, lo + 2 * j * 64:lo + 2 * j * 64 + 64],
                                                 in1=sm[0:1, lo + (2 * j + 1) * 64:lo + (2 * j + 1) * 64 + 64], op=ALU.mult), r=['sm', 'lamj'], w=['lamt'])
            P.dve(lambda e, j=j: e.reduce_sum(out=lamv[:, j:j + 1], in_=lamv[:, 8:72], axis=mybir.AxisListType.X),
                  r=['lamt'], w=['lamj'])
        P.act(lambda e: e.activation(out=lamv[:, 0:2], in_=lamv[:, 0:2], func=AF.Exp), r=['lamj'], w=['lamj'])
        P.dve(lambda e: e.tensor_tensor(out=lamv[:, 2:3], in0=lamv[:, 1:2], in1=lamv[:, 0:1], op=ALU.subtract), r=['lamj'], w=['lamn'])
        P.dve(lambda e: e.tensor_scalar(out=lamv[:, 2:3], in0=lamv[:, 2:3], scalar1=-lam_init, scalar2=None, op0=ALU.add), r=['lamn'], w=['lamn'])
        P.dve(lambda e: e.tensor_copy(out=lamv[:, 3:4], in_=lamv[:, 2:3]), r=['lamn'], w=['lamn'])
        P.pe(lambda e: e.matmul(ps[0][:, 0:2], lhsT=ones32[0:1, :], rhs=lamv[0:1, 2:4], start=True, stop=True), r=['lamn', 'ones32'], w=['ps0'])
        P.dve(lambda e: e.tensor_copy(out=nlam, in_=ps[0][:, 0:2]), r=['ps0'], w=['nlam'])
        P.dve(lambda e: e.tensor_scalar(out=gcol, in0=sm[:, SM_DAN + i:SM_DAN + i + 1], scalar1=1.0 - lam_init, scalar2=None, op0=ALU.mult),
              r=['sm'], w=['gcol'])
        for g in range(2):
            wq, kq = wtile(win[:, g * 512:(g + 1) * 512])
            for hh in range(4):
                def ev_q(tb, bank, hh=hh):
                    P.act(lambda e: e.activation(out=QT[:, hh * T + tb * 512: hh * T + (tb + 1) * 512], in_=ps[bank][:],
                                                 func=AF.Copy, scale=0.125), r=['ps%d' % bank], w=['QT'])
                proj_fm(hn, wq, kq, hh * 128, 128, ev_q)
            wk, kk_ = wtile(win[:, 1024 + g * 512:1024 + (g + 1) * 512])
            for hh in range(4):
                def ev_k(tb, bank, hh=hh):
                    P.dve(lambda e: e.tensor_copy(out=KT[:, hh * T + tb * 512: hh * T + (tb + 1) * 512], in_=ps[bank][:]),
                          r=['ps%d' % bank], w=['KT'])
                proj_fm(hn, wk, kk_, hh * 128, 128, ev_k)
            wv, kv = wtile(win[:, 2048 + g * 512:2048 + (g + 1) * 512])
            for tt in range(16):
                bank = pj['n'] % 4 + 4
                pj['n'] += 1
                for kc in range(16):
                    P.pe(lambda e, bank=bank, kc=kc, tt=tt, wv=wv: e.matmul(ps[bank][:], lhsT=hn[:, kc * T + tt * 128: kc * T + (tt + 1) * 128],
                                                                           rhs=wv[:, kc, :], start=(kc == 0), stop=(kc == 15)),
                         r=[kv, 'hn'], w=['ps%d' % bank])
                if tt % 2 == 0:
                    P.act(lambda e, bank=bank, tt=tt: e.activation(out=V[:, tt * 512:(tt + 1) * 512], in_=ps[bank][:], func=AF.Copy),
                          r=['ps%d' % bank], w=['V'])
                else:
                    P.dve(lambda e, bank=bank, tt=tt: e.tensor_copy(out=V[:, tt * 512:(tt + 1) * 512], in_=ps[bank][:]),
                          r=['ps%d' % bank], w=['V'])
            for hh in range(4):
                h = g * 4 + hh
                b = h % 2
                slope = 2.0 ** (-(h + 1))
                for pb in (0, 64):
                    P.dve(lambda e, b=b, slope=slope, pb=pb: e.tensor_scalar(out=aR[b][pb:pb + 4, :], in0=aR0[pb:pb + 4, :], scalar1=slope,
                                                                             scalar2=None, op0=ALU.mult), r=['aR0'], w=['aR%d' % b])
                for m in range(2):
                    def osbf(qb, m=m):
                        return o12[m][:, qb * 512:(qb + 1) * 512], 'o%d_%d' % (m, qb), rec[qb % 2], 'rec%d' % (qb % 2)
                    attention_qb(QT[64 * m:64 * m + 64, hh * T:(hh + 1) * T], KT[64 * m:64 * m + 64, hh * T:(hh + 1) * T],
                                 lambda kt, hh=hh: V[:, kt * 512 + hh * 128: kt * 512 + (hh + 1) * 128],
                                 aL[64 * m:64 * m + 4, :], aR[b][64 * m:64 * m + 4, :], pT, osbf, ['QT', 'KT', 'V', 'aL', 'aR%d' % b],
                                 lambda qb, dst, dkey: None)
                okeys = ['o%d_%d' % (m, qb) for m in range(2) for qb in range(4)]
                P.dve(lambda e: e.scalar_tensor_tensor(out=o12[0], in0=o12[1], scalar=nlam[:, 0:1], in1=o12[0], op0=ALU.mult, op1=ALU.add),
                      r=okeys + ['nlam'], w=okeys)
                P.act(lambda e: e.activation(out=sqd, in_=o12[0], func=AF.Square), r=okeys, w=['sqd'])
                for tb in range(4):
                    bank = 4 + tb
                    P.pe(lambda e, bank=bank, tb=tb: e.matmul(ps[bank][:], lhsT=onesb[:], rhs=sqd[:, tb * 512:(tb + 1) * 512], start=True, stop=True),
                         r=['sqd', 'onesb'], w=['ps%d' % bank])
                    P.act(lambda e, bank=bank: e.activation(out=rsd, in_=ps[bank][:], func=AF.Sqrt, bias=1e-6, scale=1.0 / 128),
                          r=['ps%d' % bank], w=['rsd'])
                    P.dve(lambda e: e.reciprocal(out=rsd, in_=rsd), r=['rsd'], w=['rsd'])
                    ob_ = tb % 2
                    P.dve(lambda e, tb=tb, ob_=ob_: e.scalar_tensor_tensor(out=ob[ob_], in0=o12[0][:, tb * 512:(tb + 1) * 512], scalar=gcol[:, 0:1],
                                                                          in1=rsd, op0=ALU.mult, op1=ALU.mult),
                          r=okeys + ['rsd', 'gcol'], w=['ob%d' % ob_])
                    P.dma('sp', lambda e, tb=tb, ob_=ob_, h=h: e.dma_start(out=oscr[h, :, tb * 512:(tb + 1) * 512], in_=ob[ob_]),
                          r=['ob%d' % ob_], w=['oscr'], stream='ob%d' % ob_)
        P.barrier()
        AR.reset(base0)
        stg = [AR.f32(512) for _ in range(3)]
        sn = 0
        for j in range(8):
            wg, kg = wtile(win[:, 3072 + j * 512:3072 + (j + 1) * 512])
            for c4 in range(4):
                c = j * 4 + c4

                def ev_g(tb, bank, c=c):
                    nonlocal sn
                    k3 = sn % 3
                    sn += 1
                    if sn % 2 == 0:
                        P.act(lambda e: e.activation(out=stg[k3], in_=ps[bank][:], func=AF.Copy), r=['ps%d' % bank], w=['stg%d' % k3])
                    else:
                        P.dve(lambda e: e.tensor_copy(out=stg[k3], in_=ps[bank][:]), r=['ps%d' % bank], w=['stg%d' % k3])
                    P.dma('sp', lambda e: e.dma_start(out=pscr[c, :, tb * 512:(tb + 1) * 512], in_=stg[k3]),
                          r=['stg%d' % k3], w=['pscr'], stream='stg%d' % k3)
                proj_fm(hn, wg, kg, c4 * 128, 128, ev_g)
        P.barrier()
        phase_gdn(l)
        phase_outproj(w_out_even[i])

    def phase_gdn(l):
        i = l // 2
        AR.reset()
        tri = AR.f32(1536, parts=64)
        eye = AR.f32(512, parts=64)
        tL = AR.f32(T, parts=66)
        tR = AR.f32(T, parts=66)
        S = [AR.f32(128) for _ in range(2)]
        vn = [AR.f32(128, parts=64) for _ in range(2)]
        eglh = AR.f32(32)
        rsd = AR.f32(512)
        sqd = AR.bf(T)
        ob = [AR.bf(512) for _ in range(2)]
        qT, kT, qdT = AR.f32(T), AR.f32(T), AR.f32(T)
        vb, kbg, kd = [AR.f32(32 * 128, parts=64) for _ in range(3)]
        intraT = AR.f32(T, parts=64)
        nwT = AR.f32(T)
        Xbase = AR.off
        P.dma('sp', lambda e: e.dma_start(out=tri, in_=c_tri), w=['tri'], stream='tri')
        P.dve(lambda e: e.tensor_tensor(out=eye, in0=tri[:, 0:512], in1=tri[:, 512:1024], op=ALU.subtract), r=['tri'], w=['eye'])
        P.dve(lambda e: e.memset(tL, 1.0), w=['tL'])
        P.dve(lambda e: e.memset(tR, 1.0), w=['tR'])
        for h in range(8):
            AR.reset(Xbase)
            xp = [AR.f32(T + 3) for _ in range(2)]
            B = [AR.f32(T) for _ in range(4)]
            vT, kbgT, kdT = AR.f32(T), AR.f32(T), AR.f32(T)
            for q in range(4):
                P.dma('sp', lambda e, q=q, h=h: e.dma_start(out=B[q], in_=rows[q, h, :].partition_broadcast(128)), w=['B%d' % q], stream='B%d' % q)
            P.dma('sp', lambda e, h=h: e.dma_start(out=eglh, in_=eglr[h, :].partition_broadcast(128)), w=['eglh'], stream='eglh')
            for (tt_, prow, q) in ((tL, 0, 5), (tL, 32, 5), (tL, 64, 6), (tR, 1, 4), (tR, 33, 6), (tR, 65, 5)):
                nm = 'tL' if tt_ is tL else 'tR'
                P.dma('sp', lambda e, tt_=tt_, prow=prow, q=q, h=h: e.dma_start(out=tt_[prow:prow + 1, :], in_=rows[q, h:h + 1, :]),
                      w=[nm], stream='%s%d' % (nm, prow))
            for which, c, dst, dk in ((0, h, qT, 'qT'), (1, 8 + h, kT, 'kT'), (2, 16 + h, vT, 'vT')):
                b = which % 2
                P.dve(lambda e, b=b: e.memset(xp[b][:, 0:3], 0.0), w=['xp%d' % b])
                P.dma('sp', lambda e, b=b, c=c: e.dma_start(out=xp[b][:, 3:3 + T], in_=pscr[c]), w=['xp%d' % b], stream='xp%d' % b)
                wc = SM_CONV + (i * 24 + c) * 4
                P.dve(lambda e, b=b, dst=dst, wc=wc: e.tensor_scalar(out=dst, in0=xp[b][:, 0:T], scalar1=sm[:, wc:wc + 1], scalar2=None, op0=ALU.mult),
                      r=['xp%d' % b, 'sm'], w=[dk])
                for j in range(1, 4):
                    P.dve(lambda e, b=b, dst=dst, wc=wc, j=j: e.scalar_tensor_tensor(out=dst, in0=xp[b][:, j:j + T], scalar=sm[:, wc + j:wc + j + 1],
                                                                                    in1=dst, op0=ALU.mult, op1=ALU.add),
                          r=['xp%d' % b, 'sm', dk], w=[dk])
                P.act(lambda e, dst=dst: e.activation(out=dst, in_=dst, func=AF.Silu), r=[dk], w=[dk])
                if which < 2:
                    P.act(lambda e, dst=dst: e.activation(out=sqd, in_=dst, func=AF.Square), r=[dk], w=['sqd'])
                    qs = (128 ** -0.5) if which == 0 else 1.0
                    for tb in range(4):
                        bank = 4 + tb
                        P.pe(lambda e, bank=bank, tb=tb: e.matmul(ps[bank][:], lhsT=onesb[:], rhs=sqd[:, tb * 512:(tb + 1) * 512], start=True, stop=True),
                             r=['sqd', 'onesb'], w=['ps%d' % bank])
                        P.act(lambda e, bank=bank: e.activation(out=rsd, in_=ps[bank][:], func=AF.Sqrt, bias=1e-6, scale=1.0),
                              r=['ps%d' % bank], w=['rsd'])
                        P.dve(lambda e: e.reciprocal(out=rsd, in_=rsd), r=['rsd'], w=['rsd'])
                        P.dve(lambda e, dst=dst, tb=tb, qs=qs: e.scalar_tensor_tensor(out=dst[:, tb * 512:(tb + 1) * 512], in0=dst[:, tb * 512:(tb + 1) * 512],
                                                                                      scalar=qs, in1=rsd, op0=ALU.mult, op1=ALU.mult),
                              r=[dk, 'rsd'], w=[dk])
            P.dve(lambda e: e.tensor_tensor(out=qdT, in0=qT, in1=B[1], op=ALU.mult), r=['qT', 'B1'], w=['qdT'])
            P.dve(lambda e: e.tensor_tensor(out=vT, in0=vT, in1=B[0], op=ALU.mult), r=['vT', 'B0'], w=['vT'])
            P.dve(lambda e: e.tensor_tensor(out=kbgT, in0=kT, in1=B[2], op=ALU.mult), r=['kT', 'B2'], w=['kbgT'])
            P.dve(lambda e: e.tensor_tensor(out=kdT, in0=kT, in1=B[3], op=ALU.mult), r=['kT', 'B3'], w=['kdT'])
            tn = 0
            for X_, Xk, Y_, Yk in ((vT, 'vT', vb, 'vb'), (kbgT, 'kbgT', kbg, 'kbg'), (kdT, 'kdT', kd, 'kd')):
                for n4 in range(8):
                    bank = tn % 8
                    tn += 1
                    for j in range(4):
                        n = n4 * 4 + j
                        P.pe(lambda e, bank=bank, j=j, n=n, X_=X_: e.transpose(out=ps[bank][0:64, j * 128:(j + 1) * 128],
                                                                               in_=X_[:, n * 64:(n + 1) * 64], identity=ident[:]),
                             r=[Xk, 'ident'], w=['ps%d' % bank])
                    if tn % 2 == 0:
                        P.act(lambda e, bank=bank, n4=n4, Y_=Y_: e.activation(out=Y_[:, n4 * 512:(n4 + 1) * 512], in_=ps[bank][0:64, :], func=AF.Copy),
                              r=['ps%d' % bank], w=[Yk])
                    else:
                        P.dve(lambda e, bank=bank, n4=n4, Y_=Y_: e.tensor_copy(out=Y_[:, n4 * 512:(n4 + 1) * 512], in_=ps[bank][0:64, :]),
                              r=['ps%d' % bank], w=[Yk])
            P.barrier()
            AR.reset(Xbase)
            DT, DTb, DTbT = [AR.f32(T, parts=64) for _ in range(3)]
            Pm = [AR.f32(T, parts=64) for _ in range(2)]
            PTm = [AR.f32(T, parts=64) for _ in range(2)]
            Rm = [AR.f32(T, parts=64) for _ in range(2)]
            bn = 0
            for vi, (pb, dst, dk) in enumerate(((0, DT, 'DT'), (32, DTb, 'DTb'), (64, DTbT, 'DTbT'))):
                for n8 in range(4):
                    bank = bn % 8
                    bn += 1
                    for j in range(8):
                        n = n8 * 8 + j
                        P.pe(lambda e, bank=bank, j=j, n=n, pb=pb: e.matmul(ps[bank][0:64, j * 64:(j + 1) * 64], lhsT=tL[pb:pb + 2, n * 64:(n + 1) * 64],
                                                                            rhs=tR[pb:pb + 2, n * 64:(n + 1) * 64], start=True, stop=True),
                             r=['tL', 'tR'], w=['ps%d' % bank])
                    blk = slice(n8 * 512, (n8 + 1) * 512)
                    P.dve(lambda e, bank=bank, dst=dst, blk=blk: e.tensor_scalar(out=dst[:, blk], in0=ps[bank][0:64, :], scalar1=0.0, scalar2=None, op0=ALU.min),
                          r=['ps%d' % bank], w=[dk])
                    P.act(lambda e, dst=dst, blk=blk: e.activation(out=dst[:, blk], in_=dst[:, blk], func=AF.Exp), r=[dk], w=[dk])
                    P.dve(lambda e, dst=dst, blk=blk, vi=vi: e.tensor_tensor(out=dst[:, blk], in0=dst[:, blk], in1=tri[:, vi * 512:(vi + 1) * 512], op=ALU.mult),
                          r=[dk, 'tri'], w=[dk])
            for n8 in range(4):
                blk = slice(n8 * 512, (n8 + 1) * 512)
                bank = bn % 8
                bn += 1
                for j in range(8):
                    n = n8 * 8 + j
                    P.pe(lambda e, bank=bank, j=j, n=n: e.matmul(ps[bank][0:64, j * 64:(j + 1) * 64], lhsT=kT[:, n * 64:(n + 1) * 64],
                                                                 rhs=kT[:, n * 64:(n + 1) * 64], start=True, stop=True), r=['kT'], w=['ps%d' % bank])
                P.dve(lambda e, bank=bank, blk=blk: e.scalar_tensor_tensor(out=Pm[0][:, blk], in0=ps[bank][0:64, :], scalar=-1.0, in1=DTb[:, blk],
                                                                           op0=ALU.mult, op1=ALU.mult), r=['ps%d' % bank, 'DTb'], w=['P0'])
                P.dve(lambda e, bank=bank, blk=blk: e.scalar_tensor_tensor(out=PTm[0][:, blk], in0=ps[bank][0:64, :], scalar=-1.0, in1=DTbT[:, blk],
                                                                           op0=ALU.mult, op1=ALU.mult), r=['ps%d' % bank, 'DTbT'], w=['PT0'])
                bank2 = bn % 8
                bn += 1
                for j in range(8):
                    n = n8 * 8 + j
                    P.pe(lambda e, bank2=bank2, j=j, n=n: e.matmul(ps[bank2][0:64, j * 64:(j + 1) * 64], lhsT=kT[:, n * 64:(n + 1) * 64],
                                                                   rhs=qT[:, n * 64:(n + 1) * 64], start=True, stop=True), r=['kT', 'qT'], w=['ps%d' % bank2])
                P.dve(lambda e, bank2=bank2, blk=blk: e.tensor_tensor(out=intraT[:, blk], in0=ps[bank2][0:64, :], in1=DT[:, blk], op=ALU.mult),
                      r=['ps%d' % bank2, 'DT'], w=['intraT'])
                P.dve(lambda e, blk=blk: e.tensor_tensor(out=Rm[0][:, blk], in0=Pm[0][:, blk], in1=eye, op=ALU.add), r=['P0', 'eye'], w=['R0'])
            cur = 0
            for m in range(1, 6):
                nxt = 1 - cur
                for n8 in range(4):
                    blk = slice(n8 * 512, (n8 + 1) * 512)
                    if m < 5:
                        bA = bn % 8
                        bn += 1
                        for j in range(8):
                            n = n8 * 8 + j
                            P.pe(lambda e, bA=bA, j=j, n=n, cur=cur: e.matmul(ps[bA][0:64, j * 64:(j + 1) * 64], lhsT=PTm[cur][:, n * 64:(n + 1) * 64],
                                                                              rhs=Pm[cur][:, n * 64:(n + 1) * 64], start=True, stop=True),
                                 r=['P%d' % cur, 'PT%d' % cur], w=['ps%d' % bA])
                        P.act(lambda e, bA=bA, blk=blk, nxt=nxt: e.activation(out=Pm[nxt][:, blk], in_=ps[bA][0:64, :], func=AF.Copy),
                              r=['ps%d' % bA], w=['P%d' % nxt])
                    bB = bn % 8
                    bn += 1
                    for j in range(8):
                        n = n8 * 8 + j
                        P.pe(lambda e, bB=bB, j=j, n=n, cur=cur: e.matmul(ps[bB][0:64, j * 64:(j + 1) * 64], lhsT=Pm[cur][:, n * 64:(n + 1) * 64],
                                                                          rhs=PTm[cur][:, n * 64:(n + 1) * 64], start=True, stop=True),
                             r=['P%d' % cur, 'PT%d' % cur], w=['ps%d' % bB])
                    P.dve(lambda e, bB=bB, blk=blk, nxt=nxt: e.tensor_copy(out=PTm[nxt][:, blk], in_=ps[bB][0:64, :]),
                          r=['ps%d' % bB], w=['PT%d' % nxt])
                    bC = bn % 8
                    bn += 1
                    for j in range(8):
                        n = n8 * 8 + j
                        P.pe(lambda e, bC=bC, j=j, n=n, cur=cur, nxt=nxt: e.matmul(ps[bC][0:64, j * 64:(j + 1) * 64], lhsT=PTm[nxt][:, n * 64:(n + 1) * 64],
                                                                                   rhs=Rm[cur][:, n * 64:(n + 1) * 64], start=True, stop=True),
                             r=['PT%d' % nxt, 'R%d' % cur], w=['ps%d' % bC])
                    P.dve(lambda e, bC=bC, blk=blk, cur=cur, nxt=nxt: e.tensor_tensor(out=Rm[nxt][:, blk], in0=ps[bC][0:64, :], in1=Rm[cur][:, blk], op=ALU.add),
                          r=['ps%d' % bC, 'R%d' % cur], w=['R%d' % nxt])
                cur = nxt
            Rf = Rm[cur]
            rk = 'R%d' % cur
            for n8 in range(4):
                blk = slice(n8 * 512, (n8 + 1) * 512)
                bank = bn % 8
                bn += 1
                for j in range(8):
                    n = n8 * 8 + j
                    P.pe(lambda e, bank=bank, j=j, n=n: e.matmul(ps[bank][:, j * 64:(j + 1) * 64], lhsT=kbg[:, n * 128:(n + 1) * 128],
                                                                 rhs=Rf[:, n * 64:(n + 1) * 64], start=True, stop=True), r=['kbg', rk], w=['ps%d' % bank])
                P.act(lambda e, bank=bank, blk=blk: e.activation(out=nwT[:, blk], in_=ps[bank][:], func=AF.Copy, scale=-1.0),
                      r=['ps%d' % bank], w=['nwT'])
            P.barrier()
            AR.reset(Xbase)
            oT = AR.f32(T)
            zs = AR.f32(T)
            P.dma('sp', lambda e, h=h: e.dma_start(out=zs, in_=pscr[24 + h]), w=['zs'], stream='zs')
            P.act(lambda e: e.activation(out=zs, in_=zs, func=AF.Silu), r=['zs'], w=['zs'])
            P.dve(lambda e: e.memset(S[0], 0.0), w=['S0'])
            for n in range(32):
                c_, x_ = n % 2, (n + 1) % 2
                bv, bs, bo = 4 + n % 2, 6 + n % 2, (n // 8) % 2
                j = n % 8
                P.pe(lambda e, bv=bv, n=n: e.matmul(ps[bv][0:64, 0:128], lhsT=Rf[:, n * 64:(n + 1) * 64], rhs=vb[:, n * 128:(n + 1) * 128],
                                                    start=True, stop=False), r=[rk, 'vb'], w=['ps%d' % bv])
                P.pe(lambda e, bv=bv, n=n, c_=c_: e.matmul(ps[bv][0:64, 0:128], lhsT=nwT[:, n * 64:(n + 1) * 64], rhs=S[c_],
                                                           start=False, stop=True), r=['nwT', 'S%d' % c_], w=['ps%d' % bv])
                P.act(lambda e, bv=bv, c_=c_: e.activation(out=vn[c_], in_=ps[bv][0:64, 0:128], func=AF.Copy), r=['ps%d' % bv], w=['vn%d' % c_])
                P.pe(lambda e, bo=bo, j=j, n=n, c_=c_: e.matmul(ps[bo][:, j * 64:(j + 1) * 64], lhsT=S[c_], rhs=qdT[:, n * 64:(n + 1) * 64],
                                                                start=True, stop=False), r=['S%d' % c_, 'qdT'], w=['ps%d' % bo])
                P.pe(lambda e, bo=bo, j=j, n=n, c_=c_: e.matmul(ps[bo][:, j * 64:(j + 1) * 64], lhsT=vn[c_], rhs=intraT[:, n * 64:(n + 1) * 64],
                                                                start=False, stop=True), r=['vn%d' % c_, 'intraT'], w=['ps%d' % bo])
                P.pe(lambda e, bs=bs, n=n, c_=c_: e.matmul(ps[bs][:, 0:128], lhsT=kd[:, n * 128:(n + 1) * 128], rhs=vn[c_], start=True, stop=True),
                     r=['kd', 'vn%d' % c_], w=['ps%d' % bs])
                P.dve(lambda e, bs=bs, n=n, c_=c_, x_=x_: e.scalar_tensor_tensor(out=S[x_], in0=S[c_], scalar=eglh[:, n:n + 1], in1=ps[bs][:, 0:128],
                                                                                 op0=ALU.mult, op1=ALU.add),
                      r=['S%d' % c_, 'eglh', 'ps%d' % bs], w=['S%d' % x_])
                if j == 7:
                    P.act(lambda e, bo=bo, n=n: e.activation(out=oT[:, (n - 7) * 64:(n + 1) * 64], in_=ps[bo][:], func=AF.Copy),
                          r=['ps%d' % bo], w=['oT'])
            P.act(lambda e: e.activation(out=sqd, in_=oT, func=AF.Square), r=['oT'], w=['sqd'])
            for tb in range(4):
                bank = 2 + tb % 2
                P.pe(lambda e, bank=bank, tb=tb: e.matmul(ps[bank][:], lhsT=onesb[:], rhs=sqd[:, tb * 512:(tb + 1) * 512], start=True, stop=True),
                     r=['sqd', 'onesb'], w=['ps%d' % bank])
                P.act(lambda e, bank=bank: e.activation(out=rsd, in_=ps[bank][:], func=AF.Sqrt, bias=1e-6, scale=1.0 / 128),
                      r=['ps%d' % bank], w=['rsd'])
                P.dve(lambda e: e.reciprocal(out=rsd, in_=rsd), r=['rsd'], w=['rsd'])
                blk = slice(tb * 512, (tb + 1) * 512)
                P.dve(lambda e, blk=blk: e.scalar_tensor_tensor(out=oT[:, blk], in0=oT[:, blk], scalar=sm[:, SM_GDN + i:SM_GDN + i + 1], in1=rsd,
                                                                op0=ALU.mult, op1=ALU.mult), r=['oT', 'rsd', 'sm'], w=['oT'])
                ob_ = tb % 2
                P.dve(lambda e, blk=blk, ob_=ob_: e.tensor_tensor(out=ob[ob_], in0=oT[:, blk], in1=zs[:, blk], op=ALU.mult),
                      r=['oT', 'zs'], w=['gob%d' % ob_])
                P.dma('sp', lambda e, tb=tb, ob_=ob_, h=h: e.dma_start(out=oscr[8 + h, :, tb * 512:(tb + 1) * 512], in_=ob[ob_]),
                      r=['gob%d' % ob_], w=['oscr'], stream='gob%d' % ob_)
            P.barrier()

    def phase_final():
        AR.reset()
        st = [AR.f32(16 * 512) for _ in range(2)]
        sq = AR.bf(16 * 512)
        rs = AR.f32(512)
        yo = [AR.f32(D) for _ in range(2)]
        n = 0
        for tb in range(4):
            b = tb % 2
            P.dma('sp', lambda e, b=b, tb=tb: e.dma_start(
                out=st[b].rearrange("p (k t) -> p k t", k=16, t=512),
                in_=hT[:, :, tb * 512:(tb + 1) * 512].rearrange("k p t -> p k t")),
                w=['nst%d' % b], stream='nst%d' % b)
            P.act(lambda e, b=b: e.activation(out=sq, in_=st[b], func=AF.Square), r=['nst%d' % b], w=['nsq'])
            bank = tb % 2
            for kc in range(16):
                P.pe(lambda e, kc=kc, bank=bank: e.matmul(ps[bank][:], lhsT=onesb[:], rhs=sq[:, kc * 512:(kc + 1) * 512],
                                                           start=(kc == 0), stop=(kc == 15)),
                     r=['nsq', 'onesb'], w=['ps%d' % bank])
            P.act(lambda e, bank=bank: e.activation(out=rs, in_=ps[bank][:], func=AF.Sqrt, bias=1e-6, scale=1.0 / D),
                  r=['ps%d' % bank], w=['nrs'])
            P.dve(lambda e: e.reciprocal(out=rs, in_=rs), r=['nrs'], w=['nrs'])
            for kc in range(16):
                P.dve(lambda e, b=b, kc=kc: e.scalar_tensor_tensor(
                    out=st[b][:, kc * 512:(kc + 1) * 512], in0=st[b][:, kc * 512:(kc + 1) * 512],
                    scalar=sm[:, 8 * 16 + kc: 8 * 16 + kc + 1], in1=rs, op0=ALU.mult, op1=ALU.mult),
                    r=['nst%d' % b, 'nrs', 'sm'], w=['nst%d' % b])
            for t4 in range(4):
                yb = n % 2
                n += 1
                tt = tb * 4 + t4
                for q in range(4):
                    bank = 4 + (n * 4 + q) % 4
                    for j in range(4):
                        kc = q * 4 + j
                        P.pe(lambda e, b=b, kc=kc, bank=bank, j=j, t4=t4: e.transpose(
                            out=ps[bank][:, j * 128:(j + 1) * 128], in_=st[b][:, kc * 512 + t4 * 128: kc * 512 + (t4 + 1) * 128],
                            identity=ident[:]), r=['nst%d' % b, 'ident'], w=['ps%d' % bank])
                    if q % 2 == 0:
                        P.act(lambda e, yb=yb, q=q, bank=bank: e.activation(out=yo[yb][:, q * 512:(q + 1) * 512], in_=ps[bank][:], func=AF.Copy),
                              r=['ps%d' % bank], w=['yo%d_%d' % (yb, q)])
                    else:
                        P.dve(lambda e, yb=yb, q=q, bank=bank: e.tensor_copy(out=yo[yb][:, q * 512:(q + 1) * 512], in_=ps[bank][:]),
                              r=['ps%d' % bank], w=['yo%d_%d' % (yb, q)])
                P.dma('sp', lambda e, yb=yb, tt=tt: e.dma_start(out=y[tt * 128:(tt + 1) * 128, :], in_=yo[yb]),
                      r=['yo%d_%d' % (yb, q) for q in range(4)], w=['y'], stream='yo%d' % yb)

    phase_load_x()
    for kind, l in plan:
        if kind == 'fox':
            phase_fox(l)
        elif kind == 'even':
            phase_even(l)
        else:
            phase_mlp(l)
    phase_final()
    P.emit(final_wait_streams=['yo0', 'yo1'])
    return nc, P


SM_GAIN = 0
SM_CONV = 144
SM_DAN = SM_CONV + 192
SM_GDN = SM_DAN + 2
SM_ALOG = SM_GDN + 2
SM_DTB = SM_ALOG + 2
SM_FOXB = SM_DTB + 2
SM_LAM = SM_FOXB + 2
SM_COLS = SM_LAM + 512


def pack_small(inp):
    sm = np.zeros((128, SM_COLS), np.float32)
    gains = [inp['norm_mix'][l] for l in range(4)] + [inp['norm_mlp'][l] for l in range(4)] + [inp['norm_final']]
    for n, g in enumerate(gains):
        sm[:, SM_GAIN + n * 16: SM_GAIN + (n + 1) * 16] = np.asarray(g).reshape(16, 128).T
    cw = np.asarray(inp['conv_w'])
    for i in range(2):
        for c in range(24):
            for j in range(4):
                sm[:, SM_CONV + (i * 24 + c) * 4 + j] = cw[i, j, c * 128:(c + 1) * 128]
    for i in range(2):
        sm[:, SM_DAN + i] = inp['da_norm'][i]
        sm[:, SM_GDN + i] = inp['gdn_norm'][i]
        sm[0:8, SM_ALOG + i] = inp['gdn_a_log'][i]
        sm[0:8, SM_DTB + i] = inp['gdn_dt_bias'][i]
        sm[0:16, SM_FOXB + i] = inp['fox_b_f'][i]
        for j, nm in enumerate(['lam_q1', 'lam_k1', 'lam_q2', 'lam_k2']):
            sm[0, SM_LAM + (i * 4 + j) * 64: SM_LAM + (i * 4 + j + 1) * 64] = inp[nm][i]
    return sm


def make_consts():
    c = {}
    c['c_ident'] = np.eye(128, dtype=np.float32)
    s = np.arange(128)
    c['c_maskneg'] = np.where(s[:, None] > s[None, :], NEG, 0.0).astype(np.float32)
    t = np.arange(T)
    al = np.zeros((4, 2 * T), np.float32)
    al[0, :T] = (t // 128) * 128
    al[1, :T] = t % 128
    al[2, :T] = 1
    al[3, :T] = 1
    al[0, T:] = 1
    al[1, T:] = 1
    al[2, T:] = -((t // 128) * 128)
    al[3, T:] = -(t % 128)
    c['c_alibi'] = al
    j = np.arange(64)
    tri = np.zeros((64, 3, 8, 64), np.float32)
    tri[:, 0] = (j[None, :] >= j[:, None]).astype(np.float32)[:, None, :]
    tri[:, 1] = (j[None, :] > j[:, None]).astype(np.float32)[:, None, :]
    tri[:, 2] = (j[None, :] < j[:, None]).astype(np.float32)[:, None, :]
    c['c_tri'] = tri.reshape(64, 3 * 512)
    cm = np.ones((8, T), np.float32)
    cm[:, ::64] = 0
    c['c_cmask'] = cm
    return c


_CACHE = {}


def kernel(**inputs):
    inp = {k: np.asarray(v) for k, v in inputs.items()}
    if 'nc' not in _CACHE:
        _CACHE['nc'] = build_program()[0]
    nc = _CACHE['nc']
    sm = pack_small(inp)
    consts = make_consts()
    shared = dict(w_in_even=inp['w_in_even'], w_out_even=inp['w_out_even'], w_in_odd=inp['w_in_odd'],
                  w_out_odd=inp['w_out_odd'], w_up=inp['w_up'], w_down=inp['w_down'], sm=sm, **consts)
    in_maps = []
    for c in range(8):
        m = dict(shared)
        m['x'] = np.ascontiguousarray(inp['x'][c % 4])
        in_maps.append(m)
    res = run_bass_kernel_spmd(nc, in_maps, core_ids=list(range(8)))
    out = np.stack([res.results[b]['y'] for b in range(4)], axis=0)
    return out.astype(np.float32)
```

```python
import contextlib
import math
import numpy as np
import concourse.bass as bass
import concourse.mybir as mybir
from concourse.bass_utils import run_bass_kernel_spmd

F32 = mybir.dt.float32
BF16 = mybir.dt.bfloat16
AF = mybir.ActivationFunctionType
ALU = mybir.AluOpType

ENGS = ('pe', 'act', 'dve', 'pool', 'sp')
T = 2048
D = 2048
DFF = 8192
EVEN_IN = 7184
ODD_IN = 6160
NEG = -30000.0


class Prog:
    def __init__(self, nc):
        self.nc = nc
        self.ops = []
        self.last_w = {}
        self.readers = {}
        self.stream_last = {}
        self.eng_last = {}
        self.pending_bar = {}
        self.stack = contextlib.ExitStack()

    def sbuf(self, name, shape, dtype):
        return self.stack.enter_context(self.nc.sbuf_tensor("sb_" + name, list(shape), dtype))

    def psum(self, name, shape, dtype):
        return self.stack.enter_context(self.nc.psum_tensor(name, list(shape), dtype))

    def add(self, eng, fn, r=(), w=(), stream=None):
        idx = len(self.ops)
        raw = set()
        oth = set()
        for k in r:
            lw = self.last_w.get(k)
            if lw is not None:
                raw.add(lw)
        for k in w:
            lw = self.last_w.get(k)
            if lw is not None:
                oth.add(lw)
            oth.update(self.readers.get(k, ()))
        if stream is not None:
            p = self.stream_last.get(stream)
            if p is not None:
                raw.add(p)
            self.stream_last[stream] = idx
        if eng in self.pending_bar:
            raw |= self.pending_bar.pop(eng)
        for k in r:
            self.readers.setdefault(k, []).append(idx)
        for k in w:
            self.last_w[k] = idx
            self.readers[k] = []
        raw.discard(idx)
        oth.discard(idx)
        self.ops.append(dict(eng=eng, fn=fn, raw=raw, oth=oth - raw, stream=stream, bar=False))
        if stream is None:
            self.eng_last[eng] = idx
        return idx

    def barrier(self):
        deps = set(self.eng_last.values()) | set(self.stream_last.values())
        for e in ENGS:
            self.pending_bar[e] = set(deps) | self.pending_bar.get(e, set())
        self.last_w = {}
        self.readers = {}

    def pe(self, fn, r=(), w=()):
        return self.add('pe', fn, r, w)

    def act(self, fn, r=(), w=()):
        return self.add('act', fn, r, w)

    def dve(self, fn, r=(), w=()):
        return self.add('dve', fn, r, w)

    def pool(self, fn, r=(), w=()):
        return self.add('pool', fn, r, w)

    def dma(self, eng, fn, r=(), w=(), stream=None):
        return self.add(eng, fn, r, w, stream=stream)

    def emit(self, final_wait_streams=()):
        nc = self.nc
        ops = self.ops
        for o in ops:
            deps = set()
            for d in o['raw']:
                if ops[d]['stream'] is None and ops[d]['eng'] == o['eng'] and o['eng'] == 'pe':
                    continue
                deps.add(d)
            for d in o['oth']:
                if ops[d]['stream'] is None and ops[d]['eng'] == o['eng']:
                    continue
                deps.add(d)
            best = {}
            for d in deps:
                k = ('s', ops[d]['stream']) if ops[d]['stream'] is not None else ('e', ops[d]['eng'])
                if k not in best or best[k] < d:
                    best[k] = d
            o['deps'] = set(best.values())
        needed = set()
        for o in ops:
            needed |= o['deps']
        cnt = {e: 0 for e in ENGS}
        scnt = {}
        for i, o in enumerate(ops):
            if o['stream'] is not None:
                s = o['stream']
                scnt[s] = scnt.get(s, 0) + 16
                o['done'] = (('dma', s), scnt[s])
            elif i in needed:
                cnt[o['eng']] += 1
                o['done'] = (('eng', o['eng']), cnt[o['eng']])
            else:
                o['done'] = None
        self.sem_counts = dict(cnt)
        sems = {}
        for e in ENGS:
            sems[('eng', e)] = self.stack.enter_context(nc.semaphore('s_' + e))
        for s in scnt:
            sems[('dma', s)] = self.stack.enter_context(nc.semaphore('d_' + str(s)))
        self.n_sems = len(sems)

        def run_engine(eng_name, engine):
            waited = {}
            for o in ops:
                if o['eng'] != eng_name:
                    continue
                need = {}
                for d in o['deps']:
                    semkey, val = ops[d]['done']
                    if need.get(semkey, 0) < val:
                        need[semkey] = val
                for semkey, val in need.items():
                    if waited.get(semkey, 0) >= val:
                        continue
                    engine.wait_ge(sems[semkey], val)
                    waited[semkey] = val
                ins = o['fn'](engine)
                if o['done'] is not None:
                    semkey, val = o['done']
                    ins.then_inc(sems[semkey], 16 if semkey[0] == 'dma' else 1)
            if eng_name == 'sp':
                for s in final_wait_streams:
                    engine.wait_ge(sems[('dma', s)], scnt[s])

        with nc.Block() as block:
            @block.tensor
            def _(e):
                run_engine('pe', e)

            @block.scalar
            def _(e):
                run_engine('act', e)

            @block.vector
            def _(e):
                run_engine('dve', e)

            @block.gpsimd
            def _(e):
                run_engine('pool', e)

            @block.sync
            def _(e):
                run_engine('sp', e)
        self.stack.close()


class Arena:
    def __init__(self, P, nbytes):
        self.t32 = P.sbuf("arena", [128, nbytes // 4], F32)
        self.t16 = self.t32.bitcast(BF16)
        self.nbytes = nbytes
        self.off = 0
        self.uid = 0

    def reset(self, off=0):
        self.off = off

    def _take(self, nbytes):
        o = self.off
        self.off += (nbytes + 31) // 32 * 32
        assert self.off <= self.nbytes, ("arena overflow", self.off, self.nbytes)
        return o

    def f32(self, n, parts=128):
        o = self._take(n * 4)
        return self.t32[0:parts, o // 4:o // 4 + n]

    def bf(self, n, parts=128):
        o = self._take(n * 2)
        return self.t16[0:parts, o // 2:o // 2 + n]


FULL_PLAN = [('even', 0), ('mlp', 0), ('fox', 1), ('mlp', 1), ('even', 2), ('mlp', 2), ('fox', 3), ('mlp', 3)]


def build_program(plan=None, dbg=False):
    plan = FULL_PLAN if plan is None else plan
    nc = bass.Bass("TRN2", target_bir_lowering=False)
    P = Prog(nc)

    def din(name, shape, dt=F32):
        return nc.dram_tensor(name, list(shape), dt, kind="ExternalInput").ap()

    x = din("x", [T, D])
    w_in_even = din("w_in_even", [2, D, EVEN_IN])
    w_out_even = din("w_out_even", [2, D, D])
    w_in_odd = din("w_in_odd", [2, D, ODD_IN])
    w_out_odd = din("w_out_odd", [2, D, D])
    w_up = din("w_up", [4, D, DFF])
    w_down = din("w_down", [4, DFF, D])
    sm_d = din("sm", [128, SM_COLS])
    c_ident = din("c_ident", [128, 128])
    c_maskneg = din("c_maskneg", [128, 128])
    c_alibi = din("c_alibi", [4, 2 * T])
    c_tri = din("c_tri", [64, 3 * 512])
    c_cmask = din("c_cmask", [8, T])
    y = nc.dram_tensor("y", [T, D], F32, kind="ExternalOutput").ap()
    hT = nc.dram_tensor("hT", [16, 128, T], F32).ap()
    oscr = nc.dram_tensor("oscr", [16, 128, T], BF16, **({"kind": "ExternalOutput"} if dbg else {})).ap()
    pscr = nc.dram_tensor("pscr", [32, 128, T], F32, **({"kind": "ExternalOutput"} if dbg else {})).ap()
    rows = nc.dram_tensor("rows", [10, 8, T], F32, **({"kind": "ExternalOutput"} if dbg else {})).ap()
    eglr = nc.dram_tensor("eglr", [8, 32], F32).ap()
    cbs = nc.dram_tensor("cbs", [6, 16, T], BF16).ap()

    sm = P.sbuf("sm", [128, SM_COLS], F32)
    ident = P.sbuf("ident", [128, 128], F32)
    identb = P.sbuf("identb", [128, 128], BF16)
    masknegb = P.sbuf("masknegb", [128, 128], BF16)
    ones32 = P.sbuf("ones32", [128, 128], F32)
    onesb = P.sbuf("onesb", [128, 128], BF16)
    ps = [P.psum("ps%d" % i, [128, 512], F32) for i in range(8)]
    AR = Arena(P, 202 * 1024)

    uid = [0]

    def U(s):
        uid[0] += 1
        return "%s_%d" % (s, uid[0])

    P.dma('sp', lambda e: e.dma_start(out=sm[:], in_=sm_d), w=['sm'], stream='c0')
    P.dma('sp', lambda e: e.dma_start(out=ident[:], in_=c_ident), w=['ident'], stream='c1')
    P.dma('pool', lambda e: e.dma_start(out=masknegb[:], in_=c_maskneg), w=['masknegb'], stream='c2')
    P.dve(lambda e: e.tensor_copy(out=identb[:], in_=ident[:]), r=['ident'], w=['identb'])
    P.dve(lambda e: e.memset(ones32[:], 1.0), w=['ones32'])
    P.dve(lambda e: e.memset(onesb[:], 1.0), w=['onesb'])
    P.barrier()

    wstate = dict(n=0, bufs=None)

    def wtile(src, ncols=512, nk=16):
        b = wstate['n'] % len(wstate['bufs'])
        wstate['n'] += 1
        buf = wstate['bufs'][b]
        view = buf[:, 0:nk * ncols].rearrange("p (k n) -> p k n", k=nk, n=ncols)
        key = 'wb%d' % b
        P.dma('pool', lambda e: e.dma_start(out=view, in_=src.rearrange("(k p) n -> p k n", p=128)),
              w=[key], stream=key)
        return view, key

    def phase_load_x():
        AR.reset()
        xs = [AR.f32(D) for _ in range(2)]
        xo = [AR.f32(2048) for _ in range(2)]
        for tt in range(16):
            b = tt % 2
            P.dma('sp', lambda e, b=b, tt=tt: e.dma_start(out=xs[b], in_=x[tt * 128:(tt + 1) * 128, :]),
                  w=['xs%d' % b], stream='xs%d' % b)
            for q in range(4):
                bank = (tt * 4 + q) % 8
                for j in range(4):
                    kc = q * 4 + j
                    P.pe(lambda e, b=b, kc=kc, bank=bank, j=j: e.transpose(
                        out=ps[bank][:, j * 128:(j + 1) * 128], in_=xs[b][:, kc * 128:(kc + 1) * 128], identity=ident[:]),
                        r=['xs%d' % b, 'ident'], w=['ps%d' % bank])
                eng = P.act if q % 2 == 0 else P.dve
                if q % 2 == 0:
                    P.act(lambda e, b=b, q=q, bank=bank: e.activation(out=xo[b][:, q * 512:(q + 1) * 512], in_=ps[bank][:], func=AF.Copy),
                          r=['ps%d' % bank], w=['xo%d_%d' % (b, q)])
                else:
                    P.dve(lambda e, b=b, q=q, bank=bank: e.tensor_copy(out=xo[b][:, q * 512:(q + 1) * 512], in_=ps[bank][:]),
                          r=['ps%d' % bank], w=['xo%d_%d' % (b, q)])
            P.dma('sp', lambda e, b=b, tt=tt: e.dma_start(
                out=hT[:, :, tt * 128:(tt + 1) * 128].rearrange("k p t -> p k t"),
                in_=xo[b].rearrange("p (k t) -> p k t", k=16, t=128)),
                r=['xo%d_%d' % (b, q) for q in range(4)], w=['hT'], stream='xo%d' % b)
        P.barrier()

    def phase_norm(norm_idx, hn):
        base = AR.off
        st1 = AR.f32(16 * 512)
        st = [st1, st1]
        sq = AR.bf(16 * 512)
        rs = AR.f32(512)
        for tb in range(4):
            b = 0
            P.dma('sp', lambda e, b=b, tb=tb: e.dma_start(
                out=st[b].rearrange("p (k t) -> p k t", k=16, t=512),
                in_=hT[:, :, tb * 512:(tb + 1) * 512].rearrange("k p t -> p k t")),
                w=['nst%d' % b], stream='nst%d' % b)
            P.act(lambda e, b=b: e.activation(out=sq, in_=st[b], func=AF.Square), r=['nst%d' % b], w=['nsq'])
            bank = tb % 2
            for kc in range(16):
                P.pe(lambda e, kc=kc, bank=bank: e.matmul(ps[bank][:], lhsT=onesb[:], rhs=sq[:, kc * 512:(kc + 1) * 512],
                                                           start=(kc == 0), stop=(kc == 15)),
                     r=['nsq', 'onesb'], w=['ps%d' % bank])
            P.act(lambda e, bank=bank: e.activation(out=rs, in_=ps[bank][:], func=AF.Sqrt, bias=1e-6, scale=1.0 / D),
                  r=['ps%d' % bank], w=['nrs'])
            P.dve(lambda e: e.reciprocal(out=rs, in_=rs), r=['nrs'], w=['nrs'])
            for kc in range(16):
                P.dve(lambda e, b=b, kc=kc, tb=tb: e.scalar_tensor_tensor(
                    out=hn[:, kc * T + tb * 512: kc * T + (tb + 1) * 512], in0=st[b][:, kc * 512:(kc + 1) * 512],
                    scalar=sm[:, norm_idx * 16 + kc: norm_idx * 16 + kc + 1], in1=rs, op0=ALU.mult, op1=ALU.mult),
                    r=['nst%d' % b, 'nrs', 'sm'], w=['hn'])
        AR.reset(base)
        P.barrier()

    att = dict(n=0, sc=0)

    pj = dict(n=0)

    def proj_fm(hn, wv, wkey, c0, M, evac):
        for tb in range(4):
            bank = pj['n'] % 4 + 4
            pj['n'] += 1
            for kc in range(16):
                P.pe(lambda e, bank=bank, kc=kc, tb=tb: e.matmul(
                    ps[bank][0:M, :], lhsT=wv[:, kc, c0:c0 + M], rhs=hn[:, kc * T + tb * 512: kc * T + (tb + 1) * 512],
                    start=(kc == 0), stop=(kc == 15)), r=[wkey, 'hn'], w=['ps%d' % bank])
            evac(tb, bank)

    def phase_outproj(w_out):
        AR.reset()
        oall = AR.bf(16 * T)
        wstate['bufs'] = [AR.bf(16 * 512) for _ in range(3)]
        hst = [AR.f32(512) for _ in range(3)]
        for k4 in range(4):
            P.dma('sp', lambda e, k4=k4: e.dma_start(
                out=oall[:, k4 * 4 * T:(k4 + 1) * 4 * T].rearrange("p (k t) -> p k t", k=4, t=T),
                in_=oscr[k4 * 4:(k4 + 1) * 4].rearrange("k p t -> p k t")), w=['hn'], stream='oall%d' % k4)
        residual_matmul(oall, 16, lambda cg: w_out[:, cg * 512:(cg + 1) * 512], hst)
        P.barrier()

    rz = dict(n=0)

    def residual_matmul(act, nk, wsrc, hst):
        for cg in range(4):
            tiles = []
            for kg in range(nk // 16):
                src = wsrc(cg)
                tiles.append(wtile(src[kg * 2048:(kg + 1) * 2048, :]))
            for c4 in range(4):
                dc = cg * 4 + c4
                for tb in range(4):
                    n = rz['n']
                    rz['n'] += 1
                    bank = n % 8
                    hb = n % 3
                    P.dma('sp', lambda e, hb=hb, dc=dc, tb=tb: e.dma_start(out=hst[hb], in_=hT[dc, :, tb * 512:(tb + 1) * 512]),
                          r=['hT%d_%d' % (dc, tb)], w=['hst%d' % hb], stream='hst%d' % hb)
                    for kk in range(nk):
                        wv, wkey = tiles[kk // 16]
                        P.pe(lambda e, bank=bank, wv=wv, kk=kk, c4=c4, tb=tb: e.matmul(
                            ps[bank][:], lhsT=wv[:, kk % 16, c4 * 128:(c4 + 1) * 128],
                            rhs=act[:, kk * T + tb * 512: kk * T + (tb + 1) * 512], start=(kk == 0), stop=(kk == nk - 1)),
                            r=[wkey, 'hn'], w=['ps%d' % bank])
                    P.dve(lambda e, bank=bank, hb=hb: e.tensor_tensor(out=hst[hb], in0=ps[bank][:], in1=hst[hb], op=ALU.add),
                          r=['ps%d' % bank, 'hst%d' % hb], w=['hst%d' % hb])
                    P.dma('sp', lambda e, hb=hb, dc=dc, tb=tb: e.dma_start(out=hT[dc, :, tb * 512:(tb + 1) * 512], in_=hst[hb]),
                          r=['hst%d' % hb], w=['hT%d_%d' % (dc, tb)], stream='hst%d' % hb)

    def phase_mlp(l):
        AR.reset()
        hn = AR.bf(16 * T)
        wstate['bufs'] = [AR.bf(16 * 512) for _ in range(3)]
        phase_norm(4 + l, hn)
        aT = AR.bf(64 * 512)
        rl = [AR.f32(512) for _ in range(2)]
        hst = [AR.f32(512) for _ in range(3)]
        n = 0
        for tb in range(4):
            for fg in range(16):
                wv, wkey = wtile(w_up[l][:, fg * 512:(fg + 1) * 512])
                for c4 in range(4):
                    bank = n % 8
                    rb_ = n % 2
                    n += 1
                    for kc in range(16):
                        P.pe(lambda e, bank=bank, wv=wv, kc=kc, c4=c4, tb=tb: e.matmul(
                            ps[bank][:], lhsT=wv[:, kc, c4 * 128:(c4 + 1) * 128],
                            rhs=hn[:, kc * T + tb * 512: kc * T + (tb + 1) * 512], start=(kc == 0), stop=(kc == 15)),
                            r=[wkey, 'hn'], w=['ps%d' % bank])
                    P.act(lambda e, bank=bank, rb_=rb_: e.activation(out=rl[rb_], in_=ps[bank][:], func=AF.Relu),
                          r=['ps%d' % bank], w=['rl%d' % rb_])
                    fc = fg * 4 + c4
                    P.dve(lambda e, rb_=rb_, fc=fc: e.tensor_tensor(out=aT[:, fc * 512:(fc + 1) * 512], in0=rl[rb_], in1=rl[rb_], op=ALU.mult),
                          r=['rl%d' % rb_], w=['aT%d' % fc])
            for cg in range(4):
                banks = [(n + j) % 8 for j in range(4)]
                n += 4
                hbs = []
                for c4 in range(4):
                    dc = cg * 4 + c4
                    hb = rz['n'] % 3
                    rz['n'] += 1
                    hbs.append(hb)
                for fr in range(4):
                    wv, wkey = wtile(w_down[l][fr * 2048:(fr + 1) * 2048, cg * 512:(cg + 1) * 512])
                    for c4 in range(4):
                        for fc in range(16):
                            ff = fr * 16 + fc
                            P.pe(lambda e, bank=banks[c4], wv=wv, fc=fc, c4=c4, ff=ff, fr=fr: e.matmul(
                                ps[bank][:], lhsT=wv[:, fc, c4 * 128:(c4 + 1) * 128], rhs=aT[:, ff * 512:(ff + 1) * 512],
                                start=(fr == 0 and fc == 0), stop=(fr == 3 and fc == 15)),
                                r=[wkey, 'aT%d' % ff], w=['ps%d' % banks[c4]])
                for c4 in range(4):
                    dc = cg * 4 + c4
                    hb = hbs[c4]
                    bank = banks[c4]
                    P.dma('sp', lambda e, hb=hb, dc=dc, tb=tb: e.dma_start(out=hst[hb], in_=hT[dc, :, tb * 512:(tb + 1) * 512]),
                          r=['hT%d_%d' % (dc, tb)], w=['hst%d' % hb], stream='hst%d' % hb)
                    P.dve(lambda e, bank=bank, hb=hb: e.tensor_tensor(out=hst[hb], in0=ps[bank][:], in1=hst[hb], op=ALU.add),
                          r=['ps%d' % bank, 'hst%d' % hb], w=['hst%d' % hb])
                    P.dma('sp', lambda e, hb=hb, dc=dc, tb=tb: e.dma_start(out=hT[dc, :, tb * 512:(tb + 1) * 512], in_=hst[hb]),
                          r=['hst%d' % hb], w=['hT%d_%d' % (dc, tb)], stream='hst%d' % hb)
        P.barrier()

    def phase_fox(l):
        i = l // 2
        win = w_in_odd[i]
        AR.reset()
        hn = AR.bf(16 * T)
        wstate['bufs'] = [AR.bf(16 * 512) for _ in range(3)]
        phase_norm(l, hn)
        base0 = AR.off
        wf = AR.bf(16 * 16)
        frow = AR.f32(T, parts=16)
        c3 = [AR.f32(T, parts=16) for _ in range(2)]
        cb = [AR.bf(T, parts=16) for _ in range(6)]
        wfv = wf.rearrange("p (k n) -> p k n", k=16, n=16)
        P.dma('pool', lambda e: e.dma_start(out=wfv, in_=win[:, 6144:6160].rearrange("(k p) n -> p k n", p=128)),
              w=['wf'], stream='wf')
        P.dve(lambda e: e.tensor_scalar(out=c3[1][:, 0:1], in0=sm[0:16, SM_FOXB + i:SM_FOXB + i + 1], scalar1=-1.0, scalar2=None,
                                        op0=ALU.mult), r=['sm'], w=['nbf'])

        def ev_f(tb, bank):
            P.act(lambda e, tb=tb, bank=bank: e.activation(out=frow[:, tb * 512:(tb + 1) * 512], in_=ps[bank][0:16, :], func=AF.Exp,
                                                           bias=c3[1][:, 0:1], scale=-1.0), r=['ps%d' % bank, 'nbf'], w=['frow'])
        proj_fm(hn, wfv, 'wf', 0, 16, ev_f)
        P.act(lambda e: e.activation(out=frow, in_=frow, func=AF.Ln, bias=1.0, scale=1.0), r=['frow'], w=['frow'])
        P.dve(lambda e: e.memset(c3[1], 1.0), r=['frow'], w=['nbf'])
        P.dve(lambda e: e.tensor_tensor_scan(out=c3[0], data0=c3[1], data1=frow, initial=0.0, op0=ALU.mult, op1=ALU.add),
              r=['frow', 'nbf'], w=['cpos'])
        P.dve(lambda e: e.tensor_copy(out=cb[0], in_=c3[0]), r=['cpos'], w=['cb0'])
        P.dve(lambda e: e.tensor_tensor(out=c3[1], in0=c3[0], in1=cb[0], op=ALU.subtract), r=['cpos', 'cb0'], w=['nbf'])
        P.dve(lambda e: e.tensor_copy(out=cb[1], in_=c3[1]), r=['nbf'], w=['cb1'])
        P.dve(lambda e: e.tensor_tensor(out=c3[0], in0=c3[1], in1=cb[1], op=ALU.subtract), r=['nbf', 'cb1'], w=['cpos'])
        P.dve(lambda e: e.tensor_copy(out=cb[2], in_=c3[0]), r=['cpos'], w=['cb2'])
        for j in range(3):
            P.dve(lambda e, j=j: e.tensor_scalar(out=cb[3 + j], in0=cb[j], scalar1=-1.0, scalar2=None, op0=ALU.mult),
                  r=['cb%d' % j], w=['cb%d' % (3 + j)])
        for j in range(6):
            P.dma('sp', lambda e, j=j: e.dma_start(out=cbs[j], in_=cb[j]), r=['cb%d' % j], w=['cbs'], stream='cbs%d' % j)
        P.barrier()
        AR.reset(base0)
        QT = AR.bf(4 * T)
        KT = AR.bf(4 * T)
        V = AR.bf(16 * 512)
        pT = [AR.bf(512) for _ in range(3)]
        osb = [AR.bf(512) for _ in range(2)]
        rec = [AR.f32(512) for _ in range(2)]
        lb = [AR.bf(T, parts=6) for _ in range(2)]
        rb = [AR.bf(T, parts=6) for _ in range(2)]
        for b in range(2):
            P.dve(lambda e, b=b: e.memset(lb[b], 1.0), w=['lb%d' % b])
            P.dve(lambda e, b=b: e.memset(rb[b], 1.0), w=['rb%d' % b])
        scale = 128 ** -0.5
        for g in range(4):
            wq, kq = wtile(win[:, g * 512:(g + 1) * 512])
            for hh in range(4):
                def ev_q(tb, bank, hh=hh):
                    P.act(lambda e, tb=tb, bank=bank: e.activation(out=QT[:, hh * T + tb * 512: hh * T + (tb + 1) * 512], in_=ps[bank][:],
                                                                   func=AF.Copy, scale=scale), r=['ps%d' % bank], w=['QT'])
                proj_fm(hn, wq, kq, hh * 128, 128, ev_q)
            wk, kk_ = wtile(win[:, 2048 + g * 512:2048 + (g + 1) * 512])
            for hh in range(4):
                def ev_k(tb, bank, hh=hh):
                    P.dve(lambda e, tb=tb, bank=bank: e.tensor_copy(out=KT[:, hh * T + tb * 512: hh * T + (tb + 1) * 512], in_=ps[bank][:]),
                          r=['ps%d' % bank], w=['KT'])
                proj_fm(hn, wk, kk_, hh * 128, 128, ev_k)
            wv, kv = wtile(win[:, 4096 + g * 512:4096 + (g + 1) * 512])
            for tt in range(16):
                bank = pj['n'] % 4 + 4
                pj['n'] += 1
                for kc in range(16):
                    P.pe(lambda e, bank=bank, kc=kc, tt=tt, wv=wv: e.matmul(ps[bank][:], lhsT=hn[:, kc * T + tt * 128: kc * T + (tt + 1) * 128],
                                                                           rhs=wv[:, kc, :], start=(kc == 0), stop=(kc == 15)),
                         r=[kv, 'hn'], w=['ps%d' % bank])
                if tt % 2 == 0:
                    P.act(lambda e, bank=bank, tt=tt: e.activation(out=V[:, tt * 512:(tt + 1) * 512], in_=ps[bank][:], func=AF.Copy),
                          r=['ps%d' % bank], w=['V'])
                else:
                    P.dve(lambda e, bank=bank, tt=tt: e.tensor_copy(out=V[:, tt * 512:(tt + 1) * 512], in_=ps[bank][:]),
                          r=['ps%d' % bank], w=['V'])
            for hh in range(4):
                h = g * 4 + hh
                b = h % 2
                P.dma('sp', lambda e, b=b, h=h: e.dma_start(out=lb[b][3:6, :], in_=cbs[0:3, h, :]), r=['cbs'], w=['lb%d' % b], stream='lb%d' % b)
                P.dma('sp', lambda e, b=b, h=h: e.dma_start(out=rb[b][0:3, :], in_=cbs[3:6, h, :]), r=['cbs'], w=['rb%d' % b], stream='rb%d' % b)

                def osbf(qb, h=h):
                    ob_ = (h * 4 + qb) % 2
                    return osb[ob_], 'osb%d' % ob_, rec[ob_], 'rec%d' % ob_
                attention_with_store(QT[:, hh * T:(hh + 1) * T], KT[:, hh * T:(hh + 1) * T],
                                     lambda kt, hh=hh: V[:, kt * 512 + hh * 128: kt * 512 + (hh + 1) * 128],
                                     lb[b], rb[b], pT, osbf, ['QT', 'KT', 'V', 'lb%d' % b, 'rb%d' % b], h)
        P.barrier()
        phase_outproj(w_out_odd[i])

    def attention_with_store(QT, KT, Vfn, lb, rb, pT, osbf, rkeys, h):
        stores = []

        def osb2(qb):
            return osbf(qb)
        attention_qb(QT, KT, Vfn, lb, rb, pT, osb2, rkeys, lambda qb, dst, dkey: P.dma(
            'sp', lambda e: e.dma_start(out=oscr[h, :, qb * 512:(qb + 1) * 512], in_=dst), r=[dkey], w=['oscr'], stream=dkey))

    def attention_qb(QT, KT, Vfn, lb, rb, pT, osb, rkeys, after):
        steps = []
        for qb in range(4):
            nkt = 4 * (qb + 1)
            for kt in range(nkt):
                steps.append((qb, kt, nkt))
        banks = {}

        def stage_a(qb, kt, nkt):
            if kt == 0:
                par = att['n'] % 2
                att['n'] += 1
                banks[qb] = (par, 2 + par)
            i = kt - 4 * qb
            q0 = qb * 512 + max(i, 0) * 128
            N = (qb + 1) * 512 - q0
            off = q0 - qb * 512
            sbk = 4 + att['sc'] % 4
            pb = att['sc'] % 3
            att['sc'] += 1
            P.pe(lambda e: e.matmul(ps[sbk][:, 0:N], lhsT=KT[:, kt * 128:(kt + 1) * 128], rhs=QT[:, q0:q0 + N], start=True, stop=False),
                 r=rkeys, w=['ps%d' % sbk])
            P.pe(lambda e: e.matmul(ps[sbk][:, 0:N], lhsT=lb[:, kt * 128:(kt + 1) * 128], rhs=rb[:, q0:q0 + N], start=False, stop=(i < 0)),
                 r=rkeys, w=['ps%d' % sbk])
            if i >= 0:
                P.pe(lambda e: e.matmul(ps[sbk][:, 0:128], lhsT=identb[:], rhs=masknegb[:], start=False, stop=True),
                     r=['identb', 'masknegb'], w=['ps%d' % sbk])
            P.act(lambda e: e.activation(out=pT[pb][:, 0:N], in_=ps[sbk][:, 0:N], func=AF.Exp), r=['ps%d' % sbk], w=['pT%d' % pb])
            return (qb, kt, nkt, N, off, pb)

        def stage_b(info):
            qb, kt, nkt, N, off, pb = info
            ob, sb_ = banks[qb]
            P.pe(lambda e: e.matmul(ps[ob][:, off:off + N], lhsT=Vfn(kt), rhs=pT[pb][:, 0:N], start=(kt == 0), stop=(kt == nkt - 1)),
                 r=rkeys + ['pT%d' % pb], w=['ps%d' % ob])
            P.pe(lambda e: e.matmul(ps[sb_][:, off:off + N], lhsT=onesb[:], rhs=pT[pb][:, 0:N], start=(kt == 0), stop=(kt == nkt - 1)),
                 r=['onesb', 'pT%d' % pb], w=['ps%d' % sb_])
            if kt == nkt - 1:
                dst, dkey, rec, rkey = osb(qb)
                P.dve(lambda e: e.reciprocal(out=rec, in_=ps[sb_][:]), r=['ps%d' % sb_], w=[rkey])
                P.dve(lambda e: e.tensor_tensor(out=dst, in0=ps[ob][:], in1=rec, op=ALU.mult), r=['ps%d' % ob, rkey], w=[dkey])
                after(qb, dst, dkey)

        prev = None
        for st_ in steps:
            cur = stage_a(*st_)
            if prev is not None:
                stage_b(prev)
            prev = cur
        stage_b(prev)

    def phase_even(l):
        i = l // 2
        win = w_in_even[i]
        lam_init = 0.8 - 0.6 * math.exp(-0.3 * l)
        AR.reset()
        hn = AR.bf(16 * T)
        wstate['bufs'] = [AR.bf(16 * 512) for _ in range(2)]
        phase_norm(l, hn)
        base0 = AR.off
        e1, Gc, eb, beta, lnb, EG, BEG, tmp, nGc = [AR.f32(T, parts=8) for _ in range(9)]
        cm = AR.f32(T, parts=8)
        sm8 = AR.f32(64, parts=8)
        wab = AR.bf(16 * 16)
        wabv = wab.rearrange("p (k n) -> p k n", k=16, n=16)
        P.dma('pool', lambda e: e.dma_start(out=wabv, in_=win[:, 7168:7184].rearrange("(k p) n -> p k n", p=128)),
              w=['wab'], stream='wf')
        P.dma('sp', lambda e: e.dma_start(out=cm, in_=c_cmask), w=['cm'], stream='cm')
        P.act(lambda e: e.activation(out=sm8[:, 0:1], in_=sm[0:8, SM_ALOG + i:SM_ALOG + i + 1], func=AF.Exp), r=['sm'], w=['nA'])
        P.dve(lambda e: e.tensor_scalar(out=sm8[:, 0:1], in0=sm8[:, 0:1], scalar1=-1.0, scalar2=None, op0=ALU.mult), r=['nA'], w=['nA'])

        def ev_a(tb, bank):
            P.act(lambda e: e.activation(out=e1[:, tb * 512:(tb + 1) * 512], in_=ps[bank][0:8, :], func=AF.Exp,
                                         bias=sm[0:8, SM_DTB + i:SM_DTB + i + 1], scale=1.0), r=['ps%d' % bank, 'sm'], w=['e1'])
        proj_fm(hn, wabv, 'wab', 0, 8, ev_a)

        def ev_b(tb, bank):
            P.act(lambda e: e.activation(out=eb[:, tb * 512:(tb + 1) * 512], in_=ps[bank][0:8, :], func=AF.Exp, scale=-1.0),
                  r=['ps%d' % bank], w=['eb'])
        proj_fm(hn, wabv, 'wab', 8, 8, ev_b)
        P.act(lambda e: e.activation(out=e1, in_=e1, func=AF.Ln, bias=1.0, scale=1.0), r=['e1'], w=['e1'])
        P.dve(lambda e: e.tensor_scalar(out=e1, in0=e1, scalar1=sm8[:, 0:1], scalar2=None, op0=ALU.mult), r=['e1', 'nA'], w=['e1'])
        P.dve(lambda e: e.tensor_tensor_scan(out=Gc, data0=cm, data1=e1, initial=0.0, op0=ALU.mult, op1=ALU.add),
              r=['cm', 'e1'], w=['Gc'])
        P.dve(lambda e: e.tensor_scalar(out=eb, in0=eb, scalar1=1.0, scalar2=None, op0=ALU.add), r=['eb'], w=['eb'])
        P.dve(lambda e: e.reciprocal(out=beta, in_=eb), r=['eb'], w=['beta'])
        P.act(lambda e: e.activation(out=lnb, in_=eb, func=AF.Ln), r=['eb'], w=['lnb'])
        P.dve(lambda e: e.tensor_tensor(out=lnb, in0=Gc, in1=lnb, op=ALU.subtract), r=['Gc', 'lnb'], w=['lnb'])
        P.act(lambda e: e.activation(out=EG, in_=Gc, func=AF.Exp), r=['Gc'], w=['EG'])
        P.dve(lambda e: e.tensor_tensor(out=BEG, in0=beta, in1=EG, op=ALU.mult), r=['beta', 'EG'], w=['BEG'])
        Gc3 = Gc.rearrange("p (n c) -> p n c", n=32, c=64)
        P.dve(lambda e: e.tensor_copy(out=tmp.rearrange("p (n c) -> p n c", n=32, c=64), in_=Gc3[:, :, 63:64].to_broadcast([8, 32, 64])),
              r=['Gc'], w=['tmp'])
        P.dve(lambda e: e.tensor_tensor(out=tmp, in0=tmp, in1=Gc, op=ALU.subtract), r=['tmp', 'Gc'], w=['tmp'])
        P.act(lambda e: e.activation(out=tmp, in_=tmp, func=AF.Exp), r=['tmp'], w=['tmp'])
        P.act(lambda e: e.activation(out=sm8[:, 8:40], in_=Gc3[:, :, 63], func=AF.Exp), r=['Gc'], w=['egl'])
        P.dve(lambda e: e.tensor_scalar(out=nGc, in0=Gc, scalar1=-1.0, scalar2=None, op0=ALU.mult), r=['Gc'], w=['nGc'])
        for q, (tl, k_) in enumerate(((beta, 'beta'), (EG, 'EG'), (BEG, 'BEG'), (tmp, 'tmp'), (Gc, 'Gc'), (nGc, 'nGc'), (lnb, 'lnb'))):
            P.dma('sp', lambda e, q=q, tl=tl: e.dma_start(out=rows[q], in_=tl), r=[k_], w=['rows'], stream='rows%d' % q)
        P.dma('sp', lambda e: e.dma_start(out=eglr, in_=sm8[:, 8:40]), r=['egl'], w=['eglr'], stream='rows7')
        P.barrier()
        AR.reset(base0)
        QT = AR.bf(4 * T)
        KT = AR.bf(4 * T)
        V = AR.bf(16 * 512)
        pT = [AR.bf(512) for _ in range(3)]
        o12 = [AR.f32(T), AR.f32(T)]
        rec = [AR.f32(512) for _ in range(2)]
        sqd = AR.bf(T)
        rsd = AR.f32(512)
        ob = [AR.bf(512) for _ in range(2)]
        aL = AR.bf(T, parts=68)
        aR0 = AR.f32(T, parts=68)
        aR = [AR.bf(T, parts=68) for _ in range(2)]
        lamv = AR.f32(80, parts=1)
        nlam = AR.f32(2)
        gcol = AR.f32(1)
        for pb in (0, 64):
            P.dma('pool', lambda e, pb=pb: e.dma_start(out=aL[pb:pb + 4, :], in_=c_alibi[:, 0:T]), w=['aL'], stream='aL%d' % pb)
            P.dma('sp', lambda e, pb=pb: e.dma_start(out=aR0[pb:pb + 4, :], in_=c_alibi[:, T:2 * T]), w=['aR0'], stream='aR0%d' % pb)
        lo = SM_LAM + i * 256
        for j in range(2):
            P.dve(lambda e, j=j: e.tensor_tensor(out=lamv[:, 8:72], in0=sm[0:1, lo + 2 * j * 64:lo + 2 * j * 64 + 64],
                                                 in1=sm[0:1, lo + (2 * j + 1) * 64:lo + (2 * j + 1) * 64 + 64], op=ALU.mult), r=['sm', 'lamj'], w=['lamt'])
            P.dve(lambda e, j=j: e.reduce_sum(out=lamv[:, j:j + 1], in_=lamv[:, 8:72], axis=mybir.AxisListType.X),
                  r=['lamt'], w=['lamj'])
        P.act(lambda e: e.activation(out=lamv[:, 0:2], in_=lamv[:, 0:2], func=AF.Exp), r=['lamj'], w=['lamj'])
        P.dve(lambda e: e.tensor_tensor(out=lamv[:, 2:3], in0=lamv[:, 1:2], in1=lamv[:, 0:1], op=ALU.subtract), r=['lamj'], w=['lamn'])
        P.dve(lambda e: e.tensor_scalar(out=lamv[:, 2:3], in0=lamv[:, 2:3], scalar1=-lam_init, scalar2=None, op0=ALU.add), r=['lamn'], w=['lamn'])
        P.dve(lambda e: e.tensor_copy(out=lamv[:, 3:4], in_=lamv[:, 2:3]), r=['lamn'], w=['lamn'])
        P.pe(lambda e: e.matmul(ps[0][:, 0:2], lhsT=ones32[0:1, :], rhs=lamv[0:1, 2:4], start=True, stop=True), r=['lamn', 'ones32'], w=['ps0'])
        P.dve(lambda e: e.tensor_copy(out=nlam, in_=ps[0][:, 0:2]), r=['ps0'], w=['nlam'])
        P.dve(lambda e: e.tensor_scalar(out=gcol, in0=sm[:, SM_DAN + i:SM_DAN + i + 1], scalar1=1.0 - lam_init, scalar2=None, op0=ALU.mult),
              r=['sm'], w=['gcol'])
        for g in range(2):
            wq, kq = wtile(win[:, g * 512:(g + 1) * 512])
            for hh in range(4):
                def ev_q(tb, bank, hh=hh):
                    P.act(lambda e: e.activation(out=QT[:, hh * T + tb * 512: hh * T + (tb + 1) * 512], in_=ps[bank][:],
                                                 func=AF.Copy, scale=0.125), r=['ps%d' % bank], w=['QT'])
                proj_fm(hn, wq, kq, hh * 128, 128, ev_q)
            wk, kk_ = wtile(win[:, 1024 + g * 512:1024 + (g + 1) * 512])
            for hh in range(4):
                def ev_k(tb, bank, hh=hh):
                    P.dve(lambda e: e.tensor_copy(out=KT[:, hh * T + tb * 512: hh * T + (tb + 1) * 512], in_=ps[bank][:]),
                          r=['ps%d' % bank], w=['KT'])
                proj_fm(hn, wk, kk_, hh * 128, 128, ev_k)
            wv, kv = wtile(win[:, 2048 + g * 512:2048 + (g + 1) * 512])
            for tt in range(16):
                bank = pj['n'] % 4 + 4
                pj['n'] += 1
                for kc in range(16):
                    P.pe(lambda e, bank=bank, kc=kc, tt=tt, wv=wv: e.matmul(ps[bank][:], lhsT=hn[:, kc * T + tt * 128: kc * T + (tt + 1) * 128],
                                                                           rhs=wv[:, kc, :], start=(kc == 0), stop=(kc == 15)),
                         r=[kv, 'hn'], w=['ps%d' % bank])
                if tt % 2 == 0:
                    P.act(lambda e, bank=bank, tt=tt: e.activation(out=V[:, tt * 512:(tt + 1) * 512], in_=ps[bank][:], func=AF.Copy),
                          r=['ps%d' % bank], w=['V'])
                else:
                    P.dve(lambda e, bank=bank, tt=tt: e.tensor_copy(out=V[:, tt * 512:(tt + 1) * 512], in_=ps[bank][:]),
                          r=['ps%d' % bank], w=['V'])
            for hh in range(4):
                h = g * 4 + hh
                b = h % 2
                slope = 2.0 ** (-(h + 1))
                for pb in (0, 64):
                    P.dve(lambda e, b=b, slope=slope, pb=pb: e.tensor_scalar(out=aR[b][pb:pb + 4, :], in0=aR0[pb:pb + 4, :], scalar1=slope,
                                                                             scalar2=None, op0=ALU.mult), r=['aR0'], w=['aR%d' % b])
                for m in range(2):
                    def osbf(qb, m=m):
                        return o12[m][:, qb * 512:(qb + 1) * 512], 'o%d_%d' % (m, qb), rec[qb % 2], 'rec%d' % (qb % 2)
                    attention_qb(QT[64 * m:64 * m + 64, hh * T:(hh + 1) * T], KT[64 * m:64 * m + 64, hh * T:(hh + 1) * T],
                                 lambda kt, hh=hh: V[:, kt * 512 + hh * 128: kt * 512 + (hh + 1) * 128],
                                 aL[64 * m:64 * m + 4, :], aR[b][64 * m:64 * m + 4, :], pT, osbf, ['QT', 'KT', 'V', 'aL', 'aR%d' % b],
                                 lambda qb, dst, dkey: None)
                okeys = ['o%d_%d' % (m, qb) for m in range(2) for qb in range(4)]
                P.dve(lambda e: e.scalar_tensor_tensor(out=o12[0], in0=o12[1], scalar=nlam[:, 0:1], in1=o12[0], op0=ALU.mult, op1=ALU.add),
                      r=okeys + ['nlam'], w=okeys)
                P.act(lambda e: e.activation(out=sqd, in_=o12[0], func=AF.Square), r=okeys, w=['sqd'])
                for tb in range(4):
                    bank = 4 + tb
                    P.pe(lambda e, bank=bank, tb=tb: e.matmul(ps[bank][:], lhsT=onesb[:], rhs=sqd[:, tb * 512:(tb + 1) * 512], start=True, stop=True),
                         r=['sqd', 'onesb'], w=['ps%d' % bank])
                    P.act(lambda e, bank=bank: e.activation(out=rsd, in_=ps[bank][:], func=AF.Sqrt, bias=1e-6, scale=1.0 / 128),
                          r=['ps%d' % bank], w=['rsd'])
                    P.dve(lambda e: e.reciprocal(out=rsd, in_=rsd), r=['rsd'], w=['rsd'])
                    ob_ = tb % 2
                    P.dve(lambda e, tb=tb, ob_=ob_: e.scalar_tensor_tensor(out=ob[ob_], in0=o12[0][:, tb * 512:(tb + 1) * 512], scalar=gcol[:, 0:1],
                                                                          in1=rsd, op0=ALU.mult, op1=ALU.mult),
                          r=okeys + ['rsd', 'gcol'], w=['ob%d' % ob_])
                    P.dma('sp', lambda e, tb=tb, ob_=ob_, h=h: e.dma_start(out=oscr[h, :, tb * 512:(tb + 1) * 512], in_=ob[ob_]),
                          r=['ob%d' % ob_], w=['oscr'], stream='ob%d' % ob_)
        P.barrier()
        AR.reset(base0)
        stg = [AR.f32(512) for _ in range(3)]
        sn = 0
        for j in range(8):
            wg, kg = wtile(win[:, 3072 + j * 512:3072 + (j + 1) * 512])
            for c4 in range(4):
                c = j * 4 + c4

                def ev_g(tb, bank, c=c):
                    nonlocal sn
                    k3 = sn % 3
                    sn += 1
                    if sn % 2 == 0:
                        P.act(lambda e: e.activation(out=stg[k3], in_=ps[bank][:], func=AF.Copy), r=['ps%d' % bank], w=['stg%d' % k3])
                    else:
                        P.dve(lambda e: e.tensor_copy(out=stg[k3], in_=ps[bank][:]), r=['ps%d' % bank], w=['stg%d' % k3])
                    P.dma('sp', lambda e: e.dma_start(out=pscr[c, :, tb * 512:(tb + 1) * 512], in_=stg[k3]),
                          r=['stg%d' % k3], w=['pscr'], stream='stg%d' % k3)
                proj_fm(hn, wg, kg, c4 * 128, 128, ev_g)
        P.barrier()
        phase_gdn(l)
        phase_outproj(w_out_even[i])

    def phase_gdn(l):
        i = l // 2
        AR.reset()
        tri = AR.f32(1536, parts=64)
        eye = AR.f32(512, parts=64)
        tL = AR.f32(T, parts=66)
        tR = AR.f32(T, parts=66)
        S = [AR.f32(128) for _ in range(2)]
        vn = [AR.f32(128, parts=64) for _ in range(2)]
        eglh = AR.f32(32)
        rsd = AR.f32(512)
        sqd = AR.bf(T)
        ob = [AR.bf(512) for _ in range(2)]
        qT, kT, qdT = AR.f32(T), AR.f32(T), AR.f32(T)
        vb, kbg, kd = [AR.f32(32 * 128, parts=64) for _ in range(3)]
        intraT = AR.f32(T, parts=64)
        nwT = AR.f32(T)
        Xbase = AR.off
        P.dma('sp', lambda e: e.dma_start(out=tri, in_=c_tri), w=['tri'], stream='tri')
        P.dve(lambda e: e.tensor_tensor(out=eye, in0=tri[:, 0:512], in1=tri[:, 512:1024], op=ALU.subtract), r=['tri'], w=['eye'])
        P.dve(lambda e: e.memset(tL, 1.0), w=['tL'])
        P.dve(lambda e: e.memset(tR, 1.0), w=['tR'])
        for h in range(8):
            AR.reset(Xbase)
            xp = [AR.f32(T + 3) for _ in range(2)]
            B = [AR.f32(T) for _ in range(4)]
            vT, kbgT, kdT = AR.f32(T), AR.f32(T), AR.f32(T)
            for q in range(4):
                P.dma('sp', lambda e, q=q, h=h: e.dma_start(out=B[q], in_=rows[q, h, :].partition_broadcast(128)), w=['B%d' % q], stream='B%d' % q)
            P.dma('sp', lambda e, h=h: e.dma_start(out=eglh, in_=eglr[h, :].partition_broadcast(128)), w=['eglh'], stream='eglh')
            for (tt_, prow, q) in ((tL, 0, 5), (tL, 32, 5), (tL, 64, 6), (tR, 1, 4), (tR, 33, 6), (tR, 65, 5)):
                nm = 'tL' if tt_ is tL else 'tR'
                P.dma('sp', lambda e, tt_=tt_, prow=prow, q=q, h=h: e.dma_start(out=tt_[prow:prow + 1, :], in_=rows[q, h:h + 1, :]),
                      w=[nm], stream='%s%d' % (nm, prow))
            for which, c, dst, dk in ((0, h, qT, 'qT'), (1, 8 + h, kT, 'kT'), (2, 16 + h, vT, 'vT')):
                b = which % 2
                P.dve(lambda e, b=b: e.memset(xp[b][:, 0:3], 0.0), w=['xp%d' % b])
                P.dma('sp', lambda e, b=b, c=c: e.dma_start(out=xp[b][:, 3:3 + T], in_=pscr[c]), w=['xp%d' % b], stream='xp%d' % b)
                wc = SM_CONV + (i * 24 + c) * 4
                P.dve(lambda e, b=b, dst=dst, wc=wc: e.tensor_scalar(out=dst, in0=xp[b][:, 0:T], scalar1=sm[:, wc:wc + 1], scalar2=None, op0=ALU.mult),
                      r=['xp%d' % b, 'sm'], w=[dk])
                for j in range(1, 4):
                    P.dve(lambda e, b=b, dst=dst, wc=wc, j=j: e.scalar_tensor_tensor(out=dst, in0=xp[b][:, j:j + T], scalar=sm[:, wc + j:wc + j + 1],
                                                                                    in1=dst, op0=ALU.mult, op1=ALU.add),
                          r=['xp%d' % b, 'sm', dk], w=[dk])
                P.act(lambda e, dst=dst: e.activation(out=dst, in_=dst, func=AF.Silu), r=[dk], w=[dk])
                if which < 2:
                    P.act(lambda e, dst=dst: e.activation(out=sqd, in_=dst, func=AF.Square), r=[dk], w=['sqd'])
                    qs = (128 ** -0.5) if which == 0 else 1.0
                    for tb in range(4):
                        bank = 4 + tb
                        P.pe(lambda e, bank=bank, tb=tb: e.matmul(ps[bank][:], lhsT=onesb[:], rhs=sqd[:, tb * 512:(tb + 1) * 512], start=True, stop=True),
                             r=['sqd', 'onesb'], w=['ps%d' % bank])
                        P.act(lambda e, bank=bank: e.activation(out=rsd, in_=ps[bank][:], func=AF.Sqrt, bias=1e-6, scale=1.0),
                              r=['ps%d' % bank], w=['rsd'])
                        P.dve(lambda e: e.reciprocal(out=rsd, in_=rsd), r=['rsd'], w=['rsd'])
                        P.dve(lambda e, dst=dst, tb=tb, qs=qs: e.scalar_tensor_tensor(out=dst[:, tb * 512:(tb + 1) * 512], in0=dst[:, tb * 512:(tb + 1) * 512],
                                                                                      scalar=qs, in1=rsd, op0=ALU.mult, op1=ALU.mult),
                              r=[dk, 'rsd'], w=[dk])
            P.dve(lambda e: e.tensor_tensor(out=qdT, in0=qT, in1=B[1], op=ALU.mult), r=['qT', 'B1'], w=['qdT'])
            P.dve(lambda e: e.tensor_tensor(out=vT, in0=vT, in1=B[0], op=ALU.mult), r=['vT', 'B0'], w=['vT'])
            P.dve(lambda e: e.tensor_tensor(out=kbgT, in0=kT, in1=B[2], op=ALU.mult), r=['kT', 'B2'], w=['kbgT'])
            P.dve(lambda e: e.tensor_tensor(out=kdT, in0=kT, in1=B[3], op=ALU.mult), r=['kT', 'B3'], w=['kdT'])
            tn = 0
            for X_, Xk, Y_, Yk in ((vT, 'vT', vb, 'vb'), (kbgT, 'kbgT', kbg, 'kbg'), (kdT, 'kdT', kd, 'kd')):
                for n4 in range(8):
                    bank = tn % 8
                    tn += 1
                    for j in range(4):
                        n = n4 * 4 + j
                        P.pe(lambda e, bank=bank, j=j, n=n, X_=X_: e.transpose(out=ps[bank][0:64, j * 128:(j + 1) * 128],
                                                                               in_=X_[:, n * 64:(n + 1) * 64], identity=ident[:]),
                             r=[Xk, 'ident'], w=['ps%d' % bank])
                    if tn % 2 == 0:
                        P.act(lambda e, bank=bank, n4=n4, Y_=Y_: e.activation(out=Y_[:, n4 * 512:(n4 + 1) * 512], in_=ps[bank][0:64, :], func=AF.Copy),
                              r=['ps%d' % bank], w=[Yk])
                    else:
                        P.dve(lambda e, bank=bank, n4=n4, Y_=Y_: e.tensor_copy(out=Y_[:, n4 * 512:(n4 + 1) * 512], in_=ps[bank][0:64, :]),
                              r=['ps%d' % bank], w=[Yk])
            P.barrier()
            AR.reset(Xbase)
            DT, DTb, DTbT = [AR.f32(T, parts=64) for _ in range(3)]
            Pm = [AR.f32(T, parts=64) for _ in range(2)]
            PTm = [AR.f32(T, parts=64) for _ in range(2)]
            Rm = [AR.f32(T, parts=64) for _ in range(2)]
            bn = 0
            for vi, (pb, dst, dk) in enumerate(((0, DT, 'DT'), (32, DTb, 'DTb'), (64, DTbT, 'DTbT'))):
                for n8 in range(4):
                    bank = bn % 8
                    bn += 1
                    for j in range(8):
                        n = n8 * 8 + j
                        P.pe(lambda e, bank=bank, j=j, n=n, pb=pb: e.matmul(ps[bank][0:64, j * 64:(j + 1) * 64], lhsT=tL[pb:pb + 2, n * 64:(n + 1) * 64],
                                                                            rhs=tR[pb:pb + 2, n * 64:(n + 1) * 64], start=True, stop=True),
                             r=['tL', 'tR'], w=['ps%d' % bank])
                    blk = slice(n8 * 512, (n8 + 1) * 512)
                    P.dve(lambda e, bank=bank, dst=dst, blk=blk: e.tensor_scalar(out=dst[:, blk], in0=ps[bank][0:64, :], scalar1=0.0, scalar2=None, op0=ALU.min),
                          r=['ps%d' % bank], w=[dk])
                    P.act(lambda e, dst=dst, blk=blk: e.activation(out=dst[:, blk], in_=dst[:, blk], func=AF.Exp), r=[dk], w=[dk])
                    P.dve(lambda e, dst=dst, blk=blk, vi=vi: e.tensor_tensor(out=dst[:, blk], in0=dst[:, blk], in1=tri[:, vi * 512:(vi + 1) * 512], op=ALU.mult),
                          r=[dk, 'tri'], w=[dk])
            for n8 in range(4):
                blk = slice(n8 * 512, (n8 + 1) * 512)
                bank = bn % 8
                bn += 1
                for j in range(8):
                    n = n8 * 8 + j
                    P.pe(lambda e, bank=bank, j=j, n=n: e.matmul(ps[bank][0:64, j * 64:(j + 1) * 64], lhsT=kT[:, n * 64:(n + 1) * 64],
                                                                 rhs=kT[:, n * 64:(n + 1) * 64], start=True, stop=True), r=['kT'], w=['ps%d' % bank])
                P.dve(lambda e, bank=bank, blk=blk: e.scalar_tensor_tensor(out=Pm[0][:, blk], in0=ps[bank][0:64, :], scalar=-1.0, in1=DTb[:, blk],
                                                                           op0=ALU.mult, op1=ALU.mult), r=['ps%d' % bank, 'DTb'], w=['P0'])
                P.dve(lambda e, bank=bank, blk=blk: e.scalar_tensor_tensor(out=PTm[0][:, blk], in0=ps[bank][0:64, :], scalar=-1.0, in1=DTbT[:, blk],
                                                                           op0=ALU.mult, op1=ALU.mult), r=['ps%d' % bank, 'DTbT'], w=['PT0'])
                bank2 = bn % 8
                bn += 1
                for j in range(8):
                    n = n8 * 8 + j
                    P.pe(lambda e, bank2=bank2, j=j, n=n: e.matmul(ps[bank2][0:64, j * 64:(j + 1) * 64], lhsT=kT[:, n * 64:(n + 1) * 64],
                                                                   rhs=qT[:, n * 64:(n + 1) * 64], start=True, stop=True), r=['kT', 'qT'], w=['ps%d' % bank2])
                P.dve(lambda e, bank2=bank2, blk=blk: e.tensor_tensor(out=intraT[:, blk], in0=ps[bank2][0:64, :], in1=DT[:, blk], op=ALU.mult),
                      r=['ps%d' % bank2, 'DT'], w=['intraT'])
                P.dve(lambda e, blk=blk: e.tensor_tensor(out=Rm[0][:, blk], in0=Pm[0][:, blk], in1=eye, op=ALU.add), r=['P0', 'eye'], w=['R0'])
            cur = 0
            for m in range(1, 6):
                nxt = 1 - cur
                for n8 in range(4):
                    blk = slice(n8 * 512, (n8 + 1) * 512)
                    if m < 5:
                        bA = bn % 8
                        bn += 1
                        for j in range(8):
                            n = n8 * 8 + j
                            P.pe(lambda e, bA=bA, j=j, n=n, cur=cur: e.matmul(ps[bA][0:64, j * 64:(j + 1) * 64], lhsT=PTm[cur][:, n * 64:(n + 1) * 64],
                                                                              rhs=Pm[cur][:, n * 64:(n + 1) * 64], start=True, stop=True),
                                 r=['P%d' % cur, 'PT%d' % cur], w=['ps%d' % bA])
                        P.act(lambda e, bA=bA, blk=blk, nxt=nxt: e.activation(out=Pm[nxt][:, blk], in_=ps[bA][0:64, :], func=AF.Copy),
                              r=['ps%d' % bA], w=['P%d_%d' % (nxt, n8)])
                    bB = bn % 8
                    bn += 1
                    for j in range(8):
                        n = n8 * 8 + j
                        P.pe(lambda e, bB=bB, j=j, n=n, cur=cur: e.matmul(ps[bB][0:64, j * 64:(j + 1) * 64], lhsT=Pm[cur][:, n * 64:(n + 1) * 64],
                                                                          rhs=PTm[cur][:, n * 64:(n + 1) * 64], start=True, stop=True),
                             r=['P%d' % cur, 'PT%d' % cur], w=['ps%d' % bB])
                    P.dve(lambda e, bB=bB, blk=blk, nxt=nxt: e.tensor_copy(out=PTm[nxt][:, blk], in_=ps[bB][0:64, :]),
                          r=['ps%d' % bB], w=['PT%d_%d' % (nxt, n8)])
                for n8 in range(4):
                    blk = slice(n8 * 512, (n8 + 1) * 512)
                    bC = bn % 8
                    bn += 1
                    for j in range(8):
                        n = n8 * 8 + j
                        P.pe(lambda e, bC=bC, j=j, n=n, cur=cur, nxt=nxt: e.matmul(ps[bC][0:64, j * 64:(j + 1) * 64], lhsT=PTm[nxt][:, n * 64:(n + 1) * 64],
                                                                                   rhs=Rm[cur][:, n * 64:(n + 1) * 64], start=True, stop=True),
                             r=['PT%d_%d' % (nxt, n8), 'R%d' % cur], w=['ps%d' % bC])
                    P.dve(lambda e, bC=bC, blk=blk, cur=cur, nxt=nxt: e.tensor_tensor(out=Rm[nxt][:, blk], in0=ps[bC][0:64, :], in1=Rm[cur][:, blk], op=ALU.add),
                          r=['ps%d' % bC, 'R%d' % cur], w=['R%d' % nxt])
                P.dve(lambda e: e.memset(vn[0][:, 0:1], 0.0), r=['P%d_%d' % (nxt, n8) for n8 in range(4)] + ['PT%d_%d' % (nxt, n8) for n8 in range(4)],
                      w=['P%d' % nxt, 'PT%d' % nxt])
                cur = nxt
            Rf = Rm[cur]
            rk = 'R%d' % cur
            for n8 in range(4):
                blk = slice(n8 * 512, (n8 + 1) * 512)
                bank = bn % 8
                bn += 1
                for j in range(8):
                    n = n8 * 8 + j
                    P.pe(lambda e, bank=bank, j=j, n=n: e.matmul(ps[bank][:, j * 64:(j + 1) * 64], lhsT=kbg[:, n * 128:(n + 1) * 128],
                                                                 rhs=Rf[:, n * 64:(n + 1) * 64], start=True, stop=True), r=['kbg', rk], w=['ps%d' % bank])
                P.act(lambda e, bank=bank, blk=blk: e.activation(out=nwT[:, blk], in_=ps[bank][:], func=AF.Copy, scale=-1.0),
                      r=['ps%d' % bank], w=['nwT'])
            P.barrier()
            AR.reset(Xbase)
            oT = AR.f32(T)
            zs = AR.f32(T)
            P.dma('sp', lambda e, h=h: e.dma_start(out=zs, in_=pscr[24 + h]), w=['zs'], stream='zs')
            P.act(lambda e: e.activation(out=zs, in_=zs, func=AF.Silu), r=['zs'], w=['zs'])
            P.dve(lambda e: e.memset(S[0], 0.0), w=['S0'])
            for n in range(32):
                c_, x_ = n % 2, (n + 1) % 2
                bv, bs, bo = 4 + n % 2, 6 + n % 2, (n // 8) % 2
                j = n % 8
                P.pe(lambda e, bv=bv, n=n: e.matmul(ps[bv][0:64, 0:128], lhsT=Rf[:, n * 64:(n + 1) * 64], rhs=vb[:, n * 128:(n + 1) * 128],
                                                    start=True, stop=False), r=[rk, 'vb'], w=['ps%d' % bv])
                P.pe(lambda e, bv=bv, n=n, c_=c_: e.matmul(ps[bv][0:64, 0:128], lhsT=nwT[:, n * 64:(n + 1) * 64], rhs=S[c_],
                                                           start=False, stop=True), r=['nwT', 'S%d' % c_], w=['ps%d' % bv])
                P.act(lambda e, bv=bv, c_=c_: e.activation(out=vn[c_], in_=ps[bv][0:64, 0:128], func=AF.Copy), r=['ps%d' % bv], w=['vn%d' % c_])
                P.pe(lambda e, bo=bo, j=j, n=n, c_=c_: e.matmul(ps[bo][:, j * 64:(j + 1) * 64], lhsT=S[c_], rhs=qdT[:, n * 64:(n + 1) * 64],
                                                                start=True, stop=False), r=['S%d' % c_, 'qdT'], w=['ps%d' % bo])
                P.pe(lambda e, bs=bs, n=n, c_=c_: e.matmul(ps[bs][:, 0:128], lhsT=kd[:, n * 128:(n + 1) * 128], rhs=vn[c_], start=True, stop=True),
                     r=['kd', 'vn%d' % c_], w=['ps%d' % bs])
                P.dve(lambda e, bs=bs, n=n, c_=c_, x_=x_: e.scalar_tensor_tensor(out=S[x_], in0=S[c_], scalar=eglh[:, n:n + 1], in1=ps[bs][:, 0:128],
                                                                                 op0=ALU.mult, op1=ALU.add),
                      r=['S%d' % c_, 'eglh', 'ps%d' % bs], w=['S%d' % x_])
                P.pe(lambda e, bo=bo, j=j, n=n, c_=c_: e.matmul(ps[bo][:, j * 64:(j + 1) * 64], lhsT=vn[c_], rhs=intraT[:, n * 64:(n + 1) * 64],
                                                                start=False, stop=True), r=['vn%d' % c_, 'intraT'], w=['ps%d' % bo])
                if j == 7:
                    P.act(lambda e, bo=bo, n=n: e.activation(out=oT[:, (n - 7) * 64:(n + 1) * 64], in_=ps[bo][:], func=AF.Copy),
                          r=['ps%d' % bo], w=['oT'])
            P.act(lambda e: e.activation(out=sqd, in_=oT, func=AF.Square), r=['oT'], w=['sqd'])
            for tb in range(4):
                bank = 2 + tb % 2
                P.pe(lambda e, bank=bank, tb=tb: e.matmul(ps[bank][:], lhsT=onesb[:], rhs=sqd[:, tb * 512:(tb + 1) * 512], start=True, stop=True),
                     r=['sqd', 'onesb'], w=['ps%d' % bank])
                P.act(lambda e, bank=bank: e.activation(out=rsd, in_=ps[bank][:], func=AF.Sqrt, bias=1e-6, scale=1.0 / 128),
                      r=['ps%d' % bank], w=['rsd'])
                P.dve(lambda e: e.reciprocal(out=rsd, in_=rsd), r=['rsd'], w=['rsd'])
                blk = slice(tb * 512, (tb + 1) * 512)
                P.dve(lambda e, blk=blk: e.scalar_tensor_tensor(out=oT[:, blk], in0=oT[:, blk], scalar=sm[:, SM_GDN + i:SM_GDN + i + 1], in1=rsd,
                                                                op0=ALU.mult, op1=ALU.mult), r=['oT', 'rsd', 'sm'], w=['oT'])
                ob_ = tb % 2
                P.dve(lambda e, blk=blk, ob_=ob_: e.tensor_tensor(out=ob[ob_], in0=oT[:, blk], in1=zs[:, blk], op=ALU.mult),
                      r=['oT', 'zs'], w=['gob%d' % ob_])
                P.dma('sp', lambda e, tb=tb, ob_=ob_, h=h: e.dma_start(out=oscr[8 + h, :, tb * 512:(tb + 1) * 512], in_=ob[ob_]),
                      r=['gob%d' % ob_], w=['oscr'], stream='gob%d' % ob_)
            P.barrier()

    def phase_final():
        AR.reset()
        st = [AR.f32(16 * 512) for _ in range(2)]
        sq = AR.bf(16 * 512)
        rs = AR.f32(512)
        yo = [AR.f32(D) for _ in range(2)]
        n = 0
        for tb in range(4):
            b = tb % 2
            P.dma('sp', lambda e, b=b, tb=tb: e.dma_start(
                out=st[b].rearrange("p (k t) -> p k t", k=16, t=512),
                in_=hT[:, :, tb * 512:(tb + 1) * 512].rearrange("k p t -> p k t")),
                w=['nst%d' % b], stream='nst%d' % b)
            P.act(lambda e, b=b: e.activation(out=sq, in_=st[b], func=AF.Square), r=['nst%d' % b], w=['nsq'])
            bank = tb % 2
            for kc in range(16):
                P.pe(lambda e, kc=kc, bank=bank: e.matmul(ps[bank][:], lhsT=onesb[:], rhs=sq[:, kc * 512:(kc + 1) * 512],
                                                           start=(kc == 0), stop=(kc == 15)),
                     r=['nsq', 'onesb'], w=['ps%d' % bank])
            P.act(lambda e, bank=bank: e.activation(out=rs, in_=ps[bank][:], func=AF.Sqrt, bias=1e-6, scale=1.0 / D),
                  r=['ps%d' % bank], w=['nrs'])
            P.dve(lambda e: e.reciprocal(out=rs, in_=rs), r=['nrs'], w=['nrs'])
            for kc in range(16):
                P.dve(lambda e, b=b, kc=kc: e.scalar_tensor_tensor(
                    out=st[b][:, kc * 512:(kc + 1) * 512], in0=st[b][:, kc * 512:(kc + 1) * 512],
                    scalar=sm[:, 8 * 16 + kc: 8 * 16 + kc + 1], in1=rs, op0=ALU.mult, op1=ALU.mult),
                    r=['nst%d' % b, 'nrs', 'sm'], w=['nst%d' % b])
            for t4 in range(4):
                yb = n % 2
                n += 1
                tt = tb * 4 + t4
                for q in range(4):
                    bank = 4 + (n * 4 + q) % 4
                    for j in range(4):
                        kc = q * 4 + j
                        P.pe(lambda e, b=b, kc=kc, bank=bank, j=j, t4=t4: e.transpose(
                            out=ps[bank][:, j * 128:(j + 1) * 128], in_=st[b][:, kc * 512 + t4 * 128: kc * 512 + (t4 + 1) * 128],
                            identity=ident[:]), r=['nst%d' % b, 'ident'], w=['ps%d' % bank])
                    if q % 2 == 0:
                        P.act(lambda e, yb=yb, q=q, bank=bank: e.activation(out=yo[yb][:, q * 512:(q + 1) * 512], in_=ps[bank][:], func=AF.Copy),
                              r=['ps%d' % bank], w=['yo%d_%d' % (yb, q)])
                    else:
                        P.dve(lambda e, yb=yb, q=q, bank=bank: e.tensor_copy(out=yo[yb][:, q * 512:(q + 1) * 512], in_=ps[bank][:]),
                              r=['ps%d' % bank], w=['yo%d_%d' % (yb, q)])
                P.dma('sp', lambda e, yb=yb, tt=tt: e.dma_start(out=y[tt * 128:(tt + 1) * 128, :], in_=yo[yb]),
                      r=['yo%d_%d' % (yb, q) for q in range(4)], w=['y'], stream='yo%d' % yb)

    phase_load_x()
    for kind, l in plan:
        if kind == 'fox':
            phase_fox(l)
        elif kind == 'even':
            phase_even(l)
        else:
            phase_mlp(l)
    phase_final()
    P.emit(final_wait_streams=['yo0', 'yo1'])
    return nc, P


SM_GAIN = 0
SM_CONV = 144
SM_DAN = SM_CONV + 192
SM_GDN = SM_DAN + 2
SM_ALOG = SM_GDN + 2
SM_DTB = SM_ALOG + 2
SM_FOXB = SM_DTB + 2
SM_LAM = SM_FOXB + 2
SM_COLS = SM_LAM + 512


def pack_small(inp):
    sm = np.zeros((128, SM_COLS), np.float32)
    gains = [inp['norm_mix'][l] for l in range(4)] + [inp['norm_mlp'][l] for l in range(4)] + [inp['norm_final']]
    for n, g in enumerate(gains):
        sm[:, SM_GAIN + n * 16: SM_GAIN + (n + 1) * 16] = np.asarray(g).reshape(16, 128).T
    cw = np.asarray(inp['conv_w'])
    for i in range(2):
        for c in range(24):
            for j in range(4):
                sm[:, SM_CONV + (i * 24 + c) * 4 + j] = cw[i, j, c * 128:(c + 1) * 128]
    for i in range(2):
        sm[:, SM_DAN + i] = inp['da_norm'][i]
        sm[:, SM_GDN + i] = inp['gdn_norm'][i]
        sm[0:8, SM_ALOG + i] = inp['gdn_a_log'][i]
        sm[0:8, SM_DTB + i] = inp['gdn_dt_bias'][i]
        sm[0:16, SM_FOXB + i] = inp['fox_b_f'][i]
        for j, nm in enumerate(['lam_q1', 'lam_k1', 'lam_q2', 'lam_k2']):
            sm[0, SM_LAM + (i * 4 + j) * 64: SM_LAM + (i * 4 + j + 1) * 64] = inp[nm][i]
    return sm


def make_consts():
    c = {}
    c['c_ident'] = np.eye(128, dtype=np.float32)
    s = np.arange(128)
    c['c_maskneg'] = np.where(s[:, None] > s[None, :], NEG, 0.0).astype(np.float32)
    t = np.arange(T)
    al = np.zeros((4, 2 * T), np.float32)
    al[0, :T] = (t // 128) * 128
    al[1, :T] = t % 128
    al[2, :T] = 1
    al[3, :T] = 1
    al[0, T:] = 1
    al[1, T:] = 1
    al[2, T:] = -((t // 128) * 128)
    al[3, T:] = -(t % 128)
    c['c_alibi'] = al
    j = np.arange(64)
    tri = np.zeros((64, 3, 8, 64), np.float32)
    tri[:, 0] = (j[None, :] >= j[:, None]).astype(np.float32)[:, None, :]
    tri[:, 1] = (j[None, :] > j[:, None]).astype(np.float32)[:, None, :]
    tri[:, 2] = (j[None, :] < j[:, None]).astype(np.float32)[:, None, :]
    c['c_tri'] = tri.reshape(64, 3 * 512)
    cm = np.ones((8, T), np.float32)
    cm[:, ::64] = 0
    c['c_cmask'] = cm
    return c


_CACHE = {}


def kernel(**inputs):
    inp = {k: np.asarray(v) for k, v in inputs.items()}
    if 'nc' not in _CACHE:
        _CACHE['nc'] = build_program()[0]
    nc = _CACHE['nc']
    sm = pack_small(inp)
    consts = make_consts()
    shared = dict(w_in_even=inp['w_in_even'], w_out_even=inp['w_out_even'], w_in_odd=inp['w_in_odd'],
                  w_out_odd=inp['w_out_odd'], w_up=inp['w_up'], w_down=inp['w_down'], sm=sm, **consts)
    in_maps = []
    for c in range(8):
        m = dict(shared)
        m['x'] = np.ascontiguousarray(inp['x'][c % 4])
        in_maps.append(m)
    res = run_bass_kernel_spmd(nc, in_maps, core_ids=list(range(8)))
    out = np.stack([res.results[b]['y'] for b in range(4)], axis=0)
    return out.astype(np.float32)
```

```python
import contextlib
import math
import numpy as np
import concourse.bass as bass
import concourse.mybir as mybir
from concourse.bass_utils import run_bass_kernel_spmd

F32 = mybir.dt.float32
BF16 = mybir.dt.bfloat16
AF = mybir.ActivationFunctionType
ALU = mybir.AluOpType

ENGS = ('pe', 'act', 'dve', 'pool', 'sp')
T = 2048
D = 2048
DFF = 8192
EVEN_IN = 7184
ODD_IN = 6160
NEG = -30000.0


class Prog:
    def __init__(self, nc):
        self.nc = nc
        self.ops = []
        self.last_w = {}
        self.readers = {}
        self.stream_last = {}
        self.eng_last = {}
        self.pending_bar = {}
        self.stack = contextlib.ExitStack()

    def sbuf(self, name, shape, dtype):
        return self.stack.enter_context(self.nc.sbuf_tensor("sb_" + name, list(shape), dtype))

    def psum(self, name, shape, dtype):
        return self.stack.enter_context(self.nc.psum_tensor(name, list(shape), dtype))

    def add(self, eng, fn, r=(), w=(), stream=None):
        idx = len(self.ops)
        raw = set()
        oth = set()
        for k in r:
            lw = self.last_w.get(k)
            if lw is not None:
                raw.add(lw)
        for k in w:
            lw = self.last_w.get(k)
            if lw is not None:
                oth.add(lw)
            oth.update(self.readers.get(k, ()))
        if stream is not None:
            p = self.stream_last.get(stream)
            if p is not None:
                raw.add(p)
            self.stream_last[stream] = idx
        if eng in self.pending_bar:
            raw |= self.pending_bar.pop(eng)
        for k in r:
            self.readers.setdefault(k, []).append(idx)
        for k in w:
            self.last_w[k] = idx
            self.readers[k] = []
        raw.discard(idx)
        oth.discard(idx)
        self.ops.append(dict(eng=eng, fn=fn, raw=raw, oth=oth - raw, stream=stream, bar=False))
        if stream is None:
            self.eng_last[eng] = idx
        return idx

    def barrier(self):
        deps = set(self.eng_last.values()) | set(self.stream_last.values())
        for e in ENGS:
            self.pending_bar[e] = set(deps) | self.pending_bar.get(e, set())
        self.last_w = {}
        self.readers = {}

    def pe(self, fn, r=(), w=()):
        return self.add('pe', fn, r, w)

    def act(self, fn, r=(), w=()):
        return self.add('act', fn, r, w)

    def dve(self, fn, r=(), w=()):
        return self.add('dve', fn, r, w)

    def pool(self, fn, r=(), w=()):
        return self.add('pool', fn, r, w)

    def dma(self, eng, fn, r=(), w=(), stream=None):
        return self.add(eng, fn, r, w, stream=stream)

    def emit(self, final_wait_streams=()):
        nc = self.nc
        ops = self.ops
        for o in ops:
            deps = set()
            for d in o['raw']:
                if ops[d]['stream'] is None and ops[d]['eng'] == o['eng'] and o['eng'] == 'pe':
                    continue
                deps.add(d)
            for d in o['oth']:
                if ops[d]['stream'] is None and ops[d]['eng'] == o['eng']:
                    continue
                deps.add(d)
            best = {}
            for d in deps:
                k = ('s', ops[d]['stream']) if ops[d]['stream'] is not None else ('e', ops[d]['eng'])
                if k not in best or best[k] < d:
                    best[k] = d
            o['deps'] = set(best.values())
        needed = set()
        for o in ops:
            needed |= o['deps']
        cnt = {e: 0 for e in ENGS}
        scnt = {}
        for i, o in enumerate(ops):
            if o['stream'] is not None:
                s = o['stream']
                scnt[s] = scnt.get(s, 0) + 16
                o['done'] = (('dma', s), scnt[s])
            elif i in needed:
                cnt[o['eng']] += 1
                o['done'] = (('eng', o['eng']), cnt[o['eng']])
            else:
                o['done'] = None
        self.sem_counts = dict(cnt)
        sems = {}
        for e in ENGS:
            sems[('eng', e)] = self.stack.enter_context(nc.semaphore('s_' + e))
        for s in scnt:
            sems[('dma', s)] = self.stack.enter_context(nc.semaphore('d_' + str(s)))
        self.n_sems = len(sems)

        def run_engine(eng_name, engine):
            waited = {}
            for o in ops:
                if o['eng'] != eng_name:
                    continue
                need = {}
                for d in o['deps']:
                    semkey, val = ops[d]['done']
                    if need.get(semkey, 0) < val:
                        need[semkey] = val
                for semkey, val in need.items():
                    if waited.get(semkey, 0) >= val:
                        continue
                    engine.wait_ge(sems[semkey], val)
                    waited[semkey] = val
                ins = o['fn'](engine)
                if o['done'] is not None:
                    semkey, val = o['done']
                    ins.then_inc(sems[semkey], 16 if semkey[0] == 'dma' else 1)
            if eng_name == 'sp':
                for s in final_wait_streams:
                    engine.wait_ge(sems[('dma', s)], scnt[s])

        with nc.Block() as block:
            @block.tensor
            def _(e):
                run_engine('pe', e)

            @block.scalar
            def _(e):
                run_engine('act', e)

            @block.vector
            def _(e):
                run_engine('dve', e)

            @block.gpsimd
            def _(e):
                run_engine('pool', e)

            @block.sync
            def _(e):
                run_engine('sp', e)
        self.stack.close()


class Arena:
    def __init__(self, P, nbytes):
        self.t32 = P.sbuf("arena", [128, nbytes // 4], F32)
        self.t16 = self.t32.bitcast(BF16)
        self.nbytes = nbytes
        self.off = 0
        self.uid = 0

    def reset(self, off=0):
        self.off = off

    def _take(self, nbytes):
        o = self.off
        self.off += (nbytes + 31) // 32 * 32
        assert self.off <= self.nbytes, ("arena overflow", self.off, self.nbytes)
        return o

    def f32(self, n, parts=128):
        o = self._take(n * 4)
        return self.t32[0:parts, o // 4:o // 4 + n]

    def bf(self, n, parts=128):
        o = self._take(n * 2)
        return self.t16[0:parts, o // 2:o // 2 + n]


FULL_PLAN = [('even', 0), ('mlp', 0), ('fox', 1), ('mlp', 1), ('even', 2), ('mlp', 2), ('fox', 3), ('mlp', 3)]


def build_program(plan=None, dbg=False):
    plan = FULL_PLAN if plan is None else plan
    nc = bass.Bass("TRN2", target_bir_lowering=False)
    P = Prog(nc)

    def din(name, shape, dt=F32):
        return nc.dram_tensor(name, list(shape), dt, kind="ExternalInput").ap()

    x = din("x", [T, D])
    w_in_even = din("w_in_even", [2, D, EVEN_IN])
    w_out_even = din("w_out_even", [2, D, D])
    w_in_odd = din("w_in_odd", [2, D, ODD_IN])
    w_out_odd = din("w_out_odd", [2, D, D])
    w_up = din("w_up", [4, D, DFF])
    w_down = din("w_down", [4, DFF, D])
    sm_d = din("sm", [128, SM_COLS])
    c_ident = din("c_ident", [128, 128])
    c_maskneg = din("c_maskneg", [128, 128])
    c_alibi = din("c_alibi", [4, 2 * T])
    c_tri = din("c_tri", [64, 3 * 512])
    c_cmask = din("c_cmask", [8, T])
    y = nc.dram_tensor("y", [T, D], F32, kind="ExternalOutput").ap()
    hT = nc.dram_tensor("hT", [16, 128, T], F32).ap()
    oscr = nc.dram_tensor("oscr", [16, 128, T], BF16, **({"kind": "ExternalOutput"} if dbg else {})).ap()
    pscr = nc.dram_tensor("pscr", [32, 128, T], F32, **({"kind": "ExternalOutput"} if dbg else {})).ap()
    rows = nc.dram_tensor("rows", [10, 8, T], F32, **({"kind": "ExternalOutput"} if dbg else {})).ap()
    eglr = nc.dram_tensor("eglr", [8, 32], F32).ap()
    cbs = nc.dram_tensor("cbs", [6, 16, T], BF16).ap()

    sm = P.sbuf("sm", [128, SM_COLS], F32)
    ident = P.sbuf("ident", [128, 128], F32)
    identb = P.sbuf("identb", [128, 128], BF16)
    masknegb = P.sbuf("masknegb", [128, 128], BF16)
    ones32 = P.sbuf("ones32", [128, 128], F32)
    onesb = P.sbuf("onesb", [128, 128], BF16)
    ps = [P.psum("ps%d" % i, [128, 512], F32) for i in range(8)]
    AR = Arena(P, 202 * 1024)

    uid = [0]

    def U(s):
        uid[0] += 1
        return "%s_%d" % (s, uid[0])

    P.dma('sp', lambda e: e.dma_start(out=sm[:], in_=sm_d), w=['sm'], stream='c0')
    P.dma('sp', lambda e: e.dma_start(out=ident[:], in_=c_ident), w=['ident'], stream='c1')
    P.dma('pool', lambda e: e.dma_start(out=masknegb[:], in_=c_maskneg), w=['masknegb'], stream='c2')
    P.dve(lambda e: e.tensor_copy(out=identb[:], in_=ident[:]), r=['ident'], w=['identb'])
    P.dve(lambda e: e.memset(ones32[:], 1.0), w=['ones32'])
    P.dve(lambda e: e.memset(onesb[:], 1.0), w=['onesb'])
    P.barrier()

    wstate = dict(n=0, bufs=None)

    def wtile(src, ncols=512, nk=16):
        b = wstate['n'] % len(wstate['bufs'])
        wstate['n'] += 1
        buf = wstate['bufs'][b]
        view = buf[:, 0:nk * ncols].rearrange("p (k n) -> p k n", k=nk, n=ncols)
        key = 'wb%d' % b
        P.dma('pool', lambda e: e.dma_start(out=view, in_=src.rearrange("(k p) n -> p k n", p=128)),
              w=[key], stream=key)
        return view, key

    def phase_load_x():
        AR.reset()
        xs = [AR.f32(D) for _ in range(2)]
        xo = [AR.f32(2048) for _ in range(2)]
        for tt in range(16):
            b = tt % 2
            P.dma('sp', lambda e, b=b, tt=tt: e.dma_start(out=xs[b], in_=x[tt * 128:(tt + 1) * 128, :]),
                  w=['xs%d' % b], stream='xs%d' % b)
            for q in range(4):
                bank = (tt * 4 + q) % 8
                for j in range(4):
                    kc = q * 4 + j
                    P.pe(lambda e, b=b, kc=kc, bank=bank, j=j: e.transpose(
                        out=ps[bank][:, j * 128:(j + 1) * 128], in_=xs[b][:, kc * 128:(kc + 1) * 128], identity=ident[:]),
                        r=['xs%d' % b, 'ident'], w=['ps%d' % bank])
                eng = P.act if q % 2 == 0 else P.dve
                if q % 2 == 0:
                    P.act(lambda e, b=b, q=q, bank=bank: e.activation(out=xo[b][:, q * 512:(q + 1) * 512], in_=ps[bank][:], func=AF.Copy),
                          r=['ps%d' % bank], w=['xo%d_%d' % (b, q)])
                else:
                    P.dve(lambda e, b=b, q=q, bank=bank: e.tensor_copy(out=xo[b][:, q * 512:(q + 1) * 512], in_=ps[bank][:]),
                          r=['ps%d' % bank], w=['xo%d_%d' % (b, q)])
            P.dma('sp', lambda e, b=b, tt=tt: e.dma_start(
                out=hT[:, :, tt * 128:(tt + 1) * 128].rearrange("k p t -> p k t"),
                in_=xo[b].rearrange("p (k t) -> p k t", k=16, t=128)),
                r=['xo%d_%d' % (b, q) for q in range(4)], w=['hT'], stream='xo%d' % b)
        P.barrier()

    def phase_norm(norm_idx, hn):
        base = AR.off
        st1 = AR.f32(16 * 512)
        st = [st1, st1]
        sq = AR.bf(16 * 512)
        rs = AR.f32(512)
        for tb in range(4):
            b = 0
            P.dma('sp', lambda e, b=b, tb=tb: e.dma_start(
                out=st[b].rearrange("p (k t) -> p k t", k=16, t=512),
                in_=hT[:, :, tb * 512:(tb + 1) * 512].rearrange("k p t -> p k t")),
                w=['nst%d' % b], stream='nst%d' % b)
            P.act(lambda e, b=b: e.activation(out=sq, in_=st[b], func=AF.Square), r=['nst%d' % b], w=['nsq'])
            bank = tb % 2
            for kc in range(16):
                P.pe(lambda e, kc=kc, bank=bank: e.matmul(ps[bank][:], lhsT=onesb[:], rhs=sq[:, kc * 512:(kc + 1) * 512],
                                                           start=(kc == 0), stop=(kc == 15)),
                     r=['nsq', 'onesb'], w=['ps%d' % bank])
            P.act(lambda e, bank=bank: e.activation(out=rs, in_=ps[bank][:], func=AF.Sqrt, bias=1e-6, scale=1.0 / D),
                  r=['ps%d' % bank], w=['nrs'])
            P.dve(lambda e: e.reciprocal(out=rs, in_=rs), r=['nrs'], w=['nrs'])
            for kc in range(16):
                P.dve(lambda e, b=b, kc=kc, tb=tb: e.scalar_tensor_tensor(
                    out=hn[:, kc * T + tb * 512: kc * T + (tb + 1) * 512], in0=st[b][:, kc * 512:(kc + 1) * 512],
                    scalar=sm[:, norm_idx * 16 + kc: norm_idx * 16 + kc + 1], in1=rs, op0=ALU.mult, op1=ALU.mult),
                    r=['nst%d' % b, 'nrs', 'sm'], w=['hn'])
        AR.reset(base)
        P.barrier()

    att = dict(n=0, sc=0)

    pj = dict(n=0)

    def proj_fm(hn, wv, wkey, c0, M, evac):
        for tb in range(4):
            bank = pj['n'] % 4 + 4
            pj['n'] += 1
            for kc in range(16):
                P.pe(lambda e, bank=bank, kc=kc, tb=tb: e.matmul(
                    ps[bank][0:M, :], lhsT=wv[:, kc, c0:c0 + M], rhs=hn[:, kc * T + tb * 512: kc * T + (tb + 1) * 512],
                    start=(kc == 0), stop=(kc == 15)), r=[wkey, 'hn'], w=['ps%d' % bank])
            evac(tb, bank)

    def phase_outproj(w_out):
        AR.reset()
        oall = AR.bf(16 * T)
        wstate['bufs'] = [AR.bf(16 * 512) for _ in range(3)]
        hst = [AR.f32(512) for _ in range(3)]
        for k4 in range(4):
            P.dma('sp', lambda e, k4=k4: e.dma_start(
                out=oall[:, k4 * 4 * T:(k4 + 1) * 4 * T].rearrange("p (k t) -> p k t", k=4, t=T),
                in_=oscr[k4 * 4:(k4 + 1) * 4].rearrange("k p t -> p k t")), w=['hn'], stream='oall%d' % k4)
        residual_matmul(oall, 16, lambda cg: w_out[:, cg * 512:(cg + 1) * 512], hst)
        P.barrier()

    rz = dict(n=0)

    def residual_matmul(act, nk, wsrc, hst):
        for cg in range(4):
            tiles = []
            for kg in range(nk // 16):
                src = wsrc(cg)
                tiles.append(wtile(src[kg * 2048:(kg + 1) * 2048, :]))
            for c4 in range(4):
                dc = cg * 4 + c4
                for tb in range(4):
                    n = rz['n']
                    rz['n'] += 1
                    bank = n % 8
                    hb = n % 3
                    P.dma('sp', lambda e, hb=hb, dc=dc, tb=tb: e.dma_start(out=hst[hb], in_=hT[dc, :, tb * 512:(tb + 1) * 512]),
                          r=['hT%d_%d' % (dc, tb)], w=['hst%d' % hb], stream='hst%d' % hb)
                    for kk in range(nk):
                        wv, wkey = tiles[kk // 16]
                        P.pe(lambda e, bank=bank, wv=wv, kk=kk, c4=c4, tb=tb: e.matmul(
                            ps[bank][:], lhsT=wv[:, kk % 16, c4 * 128:(c4 + 1) * 128],
                            rhs=act[:, kk * T + tb * 512: kk * T + (tb + 1) * 512], start=(kk == 0), stop=(kk == nk - 1)),
                            r=[wkey, 'hn'], w=['ps%d' % bank])
                    P.dve(lambda e, bank=bank, hb=hb: e.tensor_tensor(out=hst[hb], in0=ps[bank][:], in1=hst[hb], op=ALU.add),
                          r=['ps%d' % bank, 'hst%d' % hb], w=['hst%d' % hb])
                    P.dma('sp', lambda e, hb=hb, dc=dc, tb=tb: e.dma_start(out=hT[dc, :, tb * 512:(tb + 1) * 512], in_=hst[hb]),
                          r=['hst%d' % hb], w=['hT%d_%d' % (dc, tb)], stream='hst%d' % hb)

    def phase_mlp(l):
        AR.reset()
        hn = AR.bf(16 * T)
        wstate['bufs'] = [AR.bf(16 * 512) for _ in range(3)]
        phase_norm(4 + l, hn)
        aT = AR.bf(64 * 512)
        rl = [AR.f32(512) for _ in range(2)]
        hst = [AR.f32(512) for _ in range(3)]
        n = 0
        for tb in range(4):
            for fg in range(16):
                wv, wkey = wtile(w_up[l][:, fg * 512:(fg + 1) * 512])
                for c4 in range(4):
                    bank = n % 8
                    rb_ = n % 2
                    n += 1
                    for kc in range(16):
                        P.pe(lambda e, bank=bank, wv=wv, kc=kc, c4=c4, tb=tb: e.matmul(
                            ps[bank][:], lhsT=wv[:, kc, c4 * 128:(c4 + 1) * 128],
                            rhs=hn[:, kc * T + tb * 512: kc * T + (tb + 1) * 512], start=(kc == 0), stop=(kc == 15)),
                            r=[wkey, 'hn'], w=['ps%d' % bank])
                    P.act(lambda e, bank=bank, rb_=rb_: e.activation(out=rl[rb_], in_=ps[bank][:], func=AF.Relu),
                          r=['ps%d' % bank], w=['rl%d' % rb_])
                    fc = fg * 4 + c4
                    P.dve(lambda e, rb_=rb_, fc=fc: e.tensor_tensor(out=aT[:, fc * 512:(fc + 1) * 512], in0=rl[rb_], in1=rl[rb_], op=ALU.mult),
                          r=['rl%d' % rb_], w=['aT%d' % fc])
            for cg in range(4):
                banks = [(n + j) % 8 for j in range(4)]
                n += 4
                hbs = []
                for c4 in range(4):
                    dc = cg * 4 + c4
                    hb = rz['n'] % 3
                    rz['n'] += 1
                    hbs.append(hb)
                for fr in range(4):
                    wv, wkey = wtile(w_down[l][fr * 2048:(fr + 1) * 2048, cg * 512:(cg + 1) * 512])
                    for c4 in range(4):
                        for fc in range(16):
                            ff = fr * 16 + fc
                            P.pe(lambda e, bank=banks[c4], wv=wv, fc=fc, c4=c4, ff=ff, fr=fr: e.matmul(
                                ps[bank][:], lhsT=wv[:, fc, c4 * 128:(c4 + 1) * 128], rhs=aT[:, ff * 512:(ff + 1) * 512],
                                start=(fr == 0 and fc == 0), stop=(fr == 3 and fc == 15)),
                                r=[wkey, 'aT%d' % ff], w=['ps%d' % banks[c4]])
                for c4 in range(4):
                    dc = cg * 4 + c4
                    hb = hbs[c4]
                    bank = banks[c4]
                    P.dma('sp', lambda e, hb=hb, dc=dc, tb=tb: e.dma_start(out=hst[hb], in_=hT[dc, :, tb * 512:(tb + 1) * 512]),
                          r=['hT%d_%d' % (dc, tb)], w=['hst%d' % hb], stream='hst%d' % hb)
                    P.dve(lambda e, bank=bank, hb=hb: e.tensor_tensor(out=hst[hb], in0=ps[bank][:], in1=hst[hb], op=ALU.add),
                          r=['ps%d' % bank, 'hst%d' % hb], w=['hst%d' % hb])
                    P.dma('sp', lambda e, hb=hb, dc=dc, tb=tb: e.dma_start(out=hT[dc, :, tb * 512:(tb + 1) * 512], in_=hst[hb]),
                          r=['hst%d' % hb], w=['hT%d_%d' % (dc, tb)], stream='hst%d' % hb)
        P.barrier()

    def phase_fox(l):
        i = l // 2
        win = w_in_odd[i]
        AR.reset()
        hn = AR.bf(16 * T)
        wstate['bufs'] = [AR.bf(16 * 512) for _ in range(3)]
        phase_norm(l, hn)
        base0 = AR.off
        wf = AR.bf(16 * 16)
        frow = AR.f32(T, parts=16)
        c3 = [AR.f32(T, parts=16) for _ in range(2)]
        cb = [AR.bf(T, parts=16) for _ in range(6)]
        wfv = wf.rearrange("p (k n) -> p k n", k=16, n=16)
        P.dma('pool', lambda e: e.dma_start(out=wfv, in_=win[:, 6144:6160].rearrange("(k p) n -> p k n", p=128)),
              w=['wf'], stream='wf')
        P.dve(lambda e: e.tensor_scalar(out=c3[1][:, 0:1], in0=sm[0:16, SM_FOXB + i:SM_FOXB + i + 1], scalar1=-1.0, scalar2=None,
                                        op0=ALU.mult), r=['sm'], w=['nbf'])

        def ev_f(tb, bank):
            P.act(lambda e, tb=tb, bank=bank: e.activation(out=frow[:, tb * 512:(tb + 1) * 512], in_=ps[bank][0:16, :], func=AF.Exp,
                                                           bias=c3[1][:, 0:1], scale=-1.0), r=['ps%d' % bank, 'nbf'], w=['frow'])
        proj_fm(hn, wfv, 'wf', 0, 16, ev_f)
        P.act(lambda e: e.activation(out=frow, in_=frow, func=AF.Ln, bias=1.0, scale=1.0), r=['frow'], w=['frow'])
        P.dve(lambda e: e.memset(c3[1], 1.0), r=['frow'], w=['nbf'])
        P.dve(lambda e: e.tensor_tensor_scan(out=c3[0], data0=c3[1], data1=frow, initial=0.0, op0=ALU.mult, op1=ALU.add),
              r=['frow', 'nbf'], w=['cpos'])
        P.dve(lambda e: e.tensor_copy(out=cb[0], in_=c3[0]), r=['cpos'], w=['cb0'])
        P.dve(lambda e: e.tensor_tensor(out=c3[1], in0=c3[0], in1=cb[0], op=ALU.subtract), r=['cpos', 'cb0'], w=['nbf'])
        P.dve(lambda e: e.tensor_copy(out=cb[1], in_=c3[1]), r=['nbf'], w=['cb1'])
        P.dve(lambda e: e.tensor_tensor(out=c3[0], in0=c3[1], in1=cb[1], op=ALU.subtract), r=['nbf', 'cb1'], w=['cpos'])
        P.dve(lambda e: e.tensor_copy(out=cb[2], in_=c3[0]), r=['cpos'], w=['cb2'])
        for j in range(3):
            P.dve(lambda e, j=j: e.tensor_scalar(out=cb[3 + j], in0=cb[j], scalar1=-1.0, scalar2=None, op0=ALU.mult),
                  r=['cb%d' % j], w=['cb%d' % (3 + j)])
        for j in range(6):
            P.dma('sp', lambda e, j=j: e.dma_start(out=cbs[j], in_=cb[j]), r=['cb%d' % j], w=['cbs'], stream='cbs%d' % j)
        P.barrier()
        AR.reset(base0)
        QT = AR.bf(4 * T)
        KT = AR.bf(4 * T)
        V = AR.bf(16 * 512)
        pT = [AR.bf(512) for _ in range(3)]
        osb = [AR.bf(512) for _ in range(2)]
        rec = [AR.f32(512) for _ in range(2)]
        lb = [AR.bf(T, parts=6) for _ in range(2)]
        rb = [AR.bf(T, parts=6) for _ in range(2)]
        for b in range(2):
            P.dve(lambda e, b=b: e.memset(lb[b], 1.0), w=['lb%d' % b])
            P.dve(lambda e, b=b: e.memset(rb[b], 1.0), w=['rb%d' % b])
        scale = 128 ** -0.5
        for g in range(4):
            wq, kq = wtile(win[:, g * 512:(g + 1) * 512])
            for hh in range(4):
                def ev_q(tb, bank, hh=hh):
                    P.act(lambda e, tb=tb, bank=bank: e.activation(out=QT[:, hh * T + tb * 512: hh * T + (tb + 1) * 512], in_=ps[bank][:],
                                                                   func=AF.Copy, scale=scale), r=['ps%d' % bank], w=['QT'])
                proj_fm(hn, wq, kq, hh * 128, 128, ev_q)
            wk, kk_ = wtile(win[:, 2048 + g * 512:2048 + (g + 1) * 512])
            for hh in range(4):
                def ev_k(tb, bank, hh=hh):
                    P.dve(lambda e, tb=tb, bank=bank: e.tensor_copy(out=KT[:, hh * T + tb * 512: hh * T + (tb + 1) * 512], in_=ps[bank][:]),
                          r=['ps%d' % bank], w=['KT'])
                proj_fm(hn, wk, kk_, hh * 128, 128, ev_k)
            wv, kv = wtile(win[:, 4096 + g * 512:4096 + (g + 1) * 512])
            for tt in range(16):
                bank = pj['n'] % 4 + 4
                pj['n'] += 1
                for kc in range(16):
                    P.pe(lambda e, bank=bank, kc=kc, tt=tt, wv=wv: e.matmul(ps[bank][:], lhsT=hn[:, kc * T + tt * 128: kc * T + (tt + 1) * 128],
                                                                           rhs=wv[:, kc, :], start=(kc == 0), stop=(kc == 15)),
                         r=[kv, 'hn'], w=['ps%d' % bank])
                if tt % 2 == 0:
                    P.act(lambda e, bank=bank, tt=tt: e.activation(out=V[:, tt * 512:(tt + 1) * 512], in_=ps[bank][:], func=AF.Copy),
                          r=['ps%d' % bank], w=['V'])
                else:
                    P.dve(lambda e, bank=bank, tt=tt: e.tensor_copy(out=V[:, tt * 512:(tt + 1) * 512], in_=ps[bank][:]),
                          r=['ps%d' % bank], w=['V'])
            for hh in range(4):
                h = g * 4 + hh
                b = h % 2
                P.dma('sp', lambda e, b=b, h=h: e.dma_start(out=lb[b][3:6, :], in_=cbs[0:3, h, :]), r=['cbs'], w=['lb%d' % b], stream='lb%d' % b)
                P.dma('sp', lambda e, b=b, h=h: e.dma_start(out=rb[b][0:3, :], in_=cbs[3:6, h, :]), r=['cbs'], w=['rb%d' % b], stream='rb%d' % b)

                def osbf(qb, h=h):
                    ob_ = (h * 4 + qb) % 2
                    return osb[ob_], 'osb%d' % ob_, rec[ob_], 'rec%d' % ob_
                attention_with_store(QT[:, hh * T:(hh + 1) * T], KT[:, hh * T:(hh + 1) * T],
                                     lambda kt, hh=hh: V[:, kt * 512 + hh * 128: kt * 512 + (hh + 1) * 128],
                                     lb[b], rb[b], pT, osbf, ['QT', 'KT', 'V', 'lb%d' % b, 'rb%d' % b], h)
        P.barrier()
        phase_outproj(w_out_odd[i])

    def attention_with_store(QT, KT, Vfn, lb, rb, pT, osbf, rkeys, h):
        stores = []

        def osb2(qb):
            return osbf(qb)
        attention_qb(QT, KT, Vfn, lb, rb, pT, osb2, rkeys, lambda qb, dst, dkey: P.dma(
            'sp', lambda e: e.dma_start(out=oscr[h, :, qb * 512:(qb + 1) * 512], in_=dst), r=[dkey], w=['oscr'], stream=dkey))

    def attention_qb(QT, KT, Vfn, lb, rb, pT, osb, rkeys, after):
        steps = []
        for qb in range(4):
            nkt = 4 * (qb + 1)
            for kt in range(nkt):
                steps.append((qb, kt, nkt))
        banks = {}

        def stage_a(qb, kt, nkt):
            if kt == 0:
                par = att['n'] % 2
                att['n'] += 1
                banks[qb] = (par, 2 + par)
            i = kt - 4 * qb
            q0 = qb * 512 + max(i, 0) * 128
            N = (qb + 1) * 512 - q0
            off = q0 - qb * 512
            sbk = 4 + att['sc'] % 4
            pb = att['sc'] % 3
            att['sc'] += 1
            P.pe(lambda e: e.matmul(ps[sbk][:, 0:N], lhsT=KT[:, kt * 128:(kt + 1) * 128], rhs=QT[:, q0:q0 + N], start=True, stop=False),
                 r=rkeys, w=['ps%d' % sbk])
            P.pe(lambda e: e.matmul(ps[sbk][:, 0:N], lhsT=lb[:, kt * 128:(kt + 1) * 128], rhs=rb[:, q0:q0 + N], start=False, stop=(i < 0)),
                 r=rkeys, w=['ps%d' % sbk])
            if i >= 0:
                P.pe(lambda e: e.matmul(ps[sbk][:, 0:128], lhsT=identb[:], rhs=masknegb[:], start=False, stop=True),
                     r=['identb', 'masknegb'], w=['ps%d' % sbk])
            P.act(lambda e: e.activation(out=pT[pb][:, 0:N], in_=ps[sbk][:, 0:N], func=AF.Exp), r=['ps%d' % sbk], w=['pT%d' % pb])
            return (qb, kt, nkt, N, off, pb)

        def stage_b(info):
            qb, kt, nkt, N, off, pb = info
            ob, sb_ = banks[qb]
            P.pe(lambda e: e.matmul(ps[ob][:, off:off + N], lhsT=Vfn(kt), rhs=pT[pb][:, 0:N], start=(kt == 0), stop=(kt == nkt - 1)),
                 r=rkeys + ['pT%d' % pb], w=['ps%d' % ob])
            P.pe(lambda e: e.matmul(ps[sb_][:, off:off + N], lhsT=onesb[:], rhs=pT[pb][:, 0:N], start=(kt == 0), stop=(kt == nkt - 1)),
                 r=['onesb', 'pT%d' % pb], w=['ps%d' % sb_])
            if kt == nkt - 1:
                dst, dkey, rec, rkey = osb(qb)
                P.dve(lambda e: e.reciprocal(out=rec, in_=ps[sb_][:]), r=['ps%d' % sb_], w=[rkey])
                P.dve(lambda e: e.tensor_tensor(out=dst, in0=ps[ob][:], in1=rec, op=ALU.mult), r=['ps%d' % ob, rkey], w=[dkey])
                after(qb, dst, dkey)

        prev = None
        for st_ in steps:
            cur = stage_a(*st_)
            if prev is not None:
                stage_b(prev)
            prev = cur
        stage_b(prev)

    def phase_even(l):
        i = l // 2
        win = w_in_even[i]
        lam_init = 0.8 - 0.6 * math.exp(-0.3 * l)
        AR.reset()
        hn = AR.bf(16 * T)
        wstate['bufs'] = [AR.bf(16 * 512) for _ in range(2)]
        phase_norm(l, hn)
        base0 = AR.off
        e1, Gc, eb, beta, lnb, EG, BEG, tmp, nGc = [AR.f32(T, parts=8) for _ in range(9)]
        cm = AR.f32(T, parts=8)
        sm8 = AR.f32(64, parts=8)
        wab = AR.bf(16 * 16)
        wabv = wab.rearrange("p (k n) -> p k n", k=16, n=16)
        P.dma('pool', lambda e: e.dma_start(out=wabv, in_=win[:, 7168:7184].rearrange("(k p) n -> p k n", p=128)),
              w=['wab'], stream='wf')
        P.dma('sp', lambda e: e.dma_start(out=cm, in_=c_cmask), w=['cm'], stream='cm')
        P.act(lambda e: e.activation(out=sm8[:, 0:1], in_=sm[0:8, SM_ALOG + i:SM_ALOG + i + 1], func=AF.Exp), r=['sm'], w=['nA'])
        P.dve(lambda e: e.tensor_scalar(out=sm8[:, 0:1], in0=sm8[:, 0:1], scalar1=-1.0, scalar2=None, op0=ALU.mult), r=['nA'], w=['nA'])

        def ev_a(tb, bank):
            P.act(lambda e: e.activation(out=e1[:, tb * 512:(tb + 1) * 512], in_=ps[bank][0:8, :], func=AF.Exp,
                                         bias=sm[0:8, SM_DTB + i:SM_DTB + i + 1], scale=1.0), r=['ps%d' % bank, 'sm'], w=['e1'])
        proj_fm(hn, wabv, 'wab', 0, 8, ev_a)

        def ev_b(tb, bank):
            P.act(lambda e: e.activation(out=eb[:, tb * 512:(tb + 1) * 512], in_=ps[bank][0:8, :], func=AF.Exp, scale=-1.0),
                  r=['ps%d' % bank], w=['eb'])
        proj_fm(hn, wabv, 'wab', 8, 8, ev_b)
        P.act(lambda e: e.activation(out=e1, in_=e1, func=AF.Ln, bias=1.0, scale=1.0), r=['e1'], w=['e1'])
        P.dve(lambda e: e.tensor_scalar(out=e1, in0=e1, scalar1=sm8[:, 0:1], scalar2=None, op0=ALU.mult), r=['e1', 'nA'], w=['e1'])
        P.dve(lambda e: e.tensor_tensor_scan(out=Gc, data0=cm, data1=e1, initial=0.0, op0=ALU.mult, op1=ALU.add),
              r=['cm', 'e1'], w=['Gc'])
        P.dve(lambda e: e.tensor_scalar(out=eb, in0=eb, scalar1=1.0, scalar2=None, op0=ALU.add), r=['eb'], w=['eb'])
        P.dve(lambda e: e.reciprocal(out=beta, in_=eb), r=['eb'], w=['beta'])
        P.act(lambda e: e.activation(out=lnb, in_=eb, func=AF.Ln), r=['eb'], w=['lnb'])
        P.dve(lambda e: e.tensor_tensor(out=lnb, in0=Gc, in1=lnb, op=ALU.subtract), r=['Gc', 'lnb'], w=['lnb'])
        P.act(lambda e: e.activation(out=EG, in_=Gc, func=AF.Exp), r=['Gc'], w=['EG'])
        P.dve(lambda e: e.tensor_tensor(out=BEG, in0=beta, in1=EG, op=ALU.mult), r=['beta', 'EG'], w=['BEG'])
        Gc3 = Gc.rearrange("p (n c) -> p n c", n=32, c=64)
        P.dve(lambda e: e.tensor_copy(out=tmp.rearrange("p (n c) -> p n c", n=32, c=64), in_=Gc3[:, :, 63:64].to_broadcast([8, 32, 64])),
              r=['Gc'], w=['tmp'])
        P.dve(lambda e: e.tensor_tensor(out=tmp, in0=tmp, in1=Gc, op=ALU.subtract), r=['tmp', 'Gc'], w=['tmp'])
        P.act(lambda e: e.activation(out=tmp, in_=tmp, func=AF.Exp), r=['tmp'], w=['tmp'])
        P.act(lambda e: e.activation(out=sm8[:, 8:40], in_=Gc3[:, :, 63], func=AF.Exp), r=['Gc'], w=['egl'])
        P.dve(lambda e: e.tensor_scalar(out=nGc, in0=Gc, scalar1=-1.0, scalar2=None, op0=ALU.mult), r=['Gc'], w=['nGc'])
        for q, (tl, k_) in enumerate(((beta, 'beta'), (EG, 'EG'), (BEG, 'BEG'), (tmp, 'tmp'), (Gc, 'Gc'), (nGc, 'nGc'), (lnb, 'lnb'))):
            P.dma('sp', lambda e, q=q, tl=tl: e.dma_start(out=rows[q], in_=tl), r=[k_], w=['rows'], stream='rows%d' % q)
        P.dma('sp', lambda e: e.dma_start(out=eglr, in_=sm8[:, 8:40]), r=['egl'], w=['eglr'], stream='rows7')
        P.barrier()
        AR.reset(base0)
        QT = AR.bf(4 * T)
        KT = AR.bf(4 * T)
        V = AR.bf(16 * 512)
        pT = [AR.bf(512) for _ in range(3)]
        o12 = [AR.f32(T), AR.f32(T)]
        rec = [AR.f32(512) for _ in range(2)]
        sqd = AR.bf(T)
        rsd = AR.f32(512)
        ob = [AR.bf(512) for _ in range(2)]
        aL = AR.bf(T, parts=68)
        aR0 = AR.f32(T, parts=68)
        aR = [AR.bf(T, parts=68) for _ in range(2)]
        lamv = AR.f32(80, parts=1)
        nlam = AR.f32(2)
        gcol = AR.f32(1)
        for pb in (0, 64):
            P.dma('pool', lambda e, pb=pb: e.dma_start(out=aL[pb:pb + 4, :], in_=c_alibi[:, 0:T]), w=['aL'], stream='aL%d' % pb)
            P.dma('sp', lambda e, pb=pb: e.dma_start(out=aR0[pb:pb + 4, :], in_=c_alibi[:, T:2 * T]), w=['aR0'], stream='aR0%d' % pb)
        lo = SM_LAM + i * 256
        for j in range(2):
            P.dve(lambda e, j=j: e.tensor_tensor(out=lamv[:, 8:72], in0=sm[0:1, lo + 2 * j * 64:lo + 2 * j * 64 + 64],
                                                 in1=sm[0:1, lo + (2 * j + 1) * 64:lo + (2 * j + 1) * 64 + 64], op=ALU.mult), r=['sm', 'lamj'], w=['lamt'])
            P.dve(lambda e, j=j: e.reduce_sum(out=lamv[:, j:j + 1], in_=lamv[:, 8:72], axis=mybir.AxisListType.X),
                  r=['lamt'], w=['lamj'])
        P.act(lambda e: e.activation(out=lamv[:, 0:2], in_=lamv[:, 0:2], func=AF.Exp), r=['lamj'], w=['lamj'])
        P.dve(lambda e: e.tensor_tensor(out=lamv[:, 2:3], in0=lamv[:, 1:2], in1=lamv[:, 0:1], op=ALU.subtract), r=['lamj'], w=['lamn'])
        P.dve(lambda e: e.tensor_scalar(out=lamv[:, 2:3], in0=lamv[:, 2:3], scalar1=-lam_init, scalar2=None, op0=ALU.add), r=['lamn'], w=['lamn'])
        P.dve(lambda e: e.tensor_copy(out=lamv[:, 3:4], in_=lamv[:, 2:3]), r=['lamn'], w=['lamn'])
        P.pe(lambda e: e.matmul(ps[0][:, 0:2], lhsT=ones32[0:1, :], rhs=lamv[0:1, 2:4], start=True, stop=True), r=['lamn', 'ones32'], w=['ps0'])
        P.dve(lambda e: e.tensor_copy(out=nlam, in_=ps[0][:, 0:2]), r=['ps0'], w=['nlam'])
        P.dve(lambda e: e.tensor_scalar(out=gcol, in0=sm[:, SM_DAN + i:SM_DAN + i + 1], scalar1=1.0 - lam_init, scalar2=None, op0=ALU.mult),
              r=['sm'], w=['gcol'])
        for g in range(2):
            wq, kq = wtile(win[:, g * 512:(g + 1) * 512])
            for hh in range(4):
                def ev_q(tb, bank, hh=hh):
                    P.act(lambda e: e.activation(out=QT[:, hh * T + tb * 512: hh * T + (tb + 1) * 512], in_=ps[bank][:],
                                                 func=AF.Copy, scale=0.125), r=['ps%d' % bank], w=['QT'])
                proj_fm(hn, wq, kq, hh * 128, 128, ev_q)
            wk, kk_ = wtile(win[:, 1024 + g * 512:1024 + (g + 1) * 512])
            for hh in range(4):
                def ev_k(tb, bank, hh=hh):
                    P.dve(lambda e: e.tensor_copy(out=KT[:, hh * T + tb * 512: hh * T + (tb + 1) * 512], in_=ps[bank][:]),
                          r=['ps%d' % bank], w=['KT'])
                proj_fm(hn, wk, kk_, hh * 128, 128, ev_k)
            wv, kv = wtile(win[:, 2048 + g * 512:2048 + (g + 1) * 512])
            for tt in range(16):
                bank = pj['n'] % 4 + 4
                pj['n'] += 1
                for kc in range(16):
                    P.pe(lambda e, bank=bank, kc=kc, tt=tt, wv=wv: e.matmul(ps[bank][:], lhsT=hn[:, kc * T + tt * 128: kc * T + (tt + 1) * 128],
                                                                           rhs=wv[:, kc, :], start=(kc == 0), stop=(kc == 15)),
                         r=[kv, 'hn'], w=['ps%d' % bank])
                if tt % 2 == 0:
                    P.act(lambda e, bank=bank, tt=tt: e.activation(out=V[:, tt * 512:(tt + 1) * 512], in_=ps[bank][:], func=AF.Copy),
                          r=['ps%d' % bank], w=['V'])
                else:
                    P.dve(lambda e, bank=bank, tt=tt: e.tensor_copy(out=V[:, tt * 512:(tt + 1) * 512], in_=ps[bank][:]),
                          r=['ps%d' % bank], w=['V'])
            for hh in range(4):
                h = g * 4 + hh
                b = h % 2
                slope = 2.0 ** (-(h + 1))
                for pb in (0, 64):
                    P.dve(lambda e, b=b, slope=slope, pb=pb: e.tensor_scalar(out=aR[b][pb:pb + 4, :], in0=aR0[pb:pb + 4, :], scalar1=slope,
                                                                             scalar2=None, op0=ALU.mult), r=['aR0'], w=['aR%d' % b])
                for m in range(2):
                    def osbf(qb, m=m):
                        return o12[m][:, qb * 512:(qb + 1) * 512], 'o%d_%d' % (m, qb), rec[qb % 2], 'rec%d' % (qb % 2)
                    attention_qb(QT[64 * m:64 * m + 64, hh * T:(hh + 1) * T], KT[64 * m:64 * m + 64, hh * T:(hh + 1) * T],
                                 lambda kt, hh=hh: V[:, kt * 512 + hh * 128: kt * 512 + (hh + 1) * 128],
                                 aL[64 * m:64 * m + 4, :], aR[b][64 * m:64 * m + 4, :], pT, osbf, ['QT', 'KT', 'V', 'aL', 'aR%d' % b],
                                 lambda qb, dst, dkey: None)
                okeys = ['o%d_%d' % (m, qb) for m in range(2) for qb in range(4)]
                P.dve(lambda e: e.scalar_tensor_tensor(out=o12[0], in0=o12[1], scalar=nlam[:, 0:1], in1=o12[0], op0=ALU.mult, op1=ALU.add),
                      r=okeys + ['nlam'], w=okeys)
                P.act(lambda e: e.activation(out=sqd, in_=o12[0], func=AF.Square), r=okeys, w=['sqd'])
                for tb in range(4):
                    bank = 4 + tb
                    P.pe(lambda e, bank=bank, tb=tb: e.matmul(ps[bank][:], lhsT=onesb[:], rhs=sqd[:, tb * 512:(tb + 1) * 512], start=True, stop=True),
                         r=['sqd', 'onesb'], w=['ps%d' % bank])
                    P.act(lambda e, bank=bank: e.activation(out=rsd, in_=ps[bank][:], func=AF.Sqrt, bias=1e-6, scale=1.0 / 128),
                          r=['ps%d' % bank], w=['rsd'])
                    P.dve(lambda e: e.reciprocal(out=rsd, in_=rsd), r=['rsd'], w=['rsd'])
                    ob_ = tb % 2
                    P.dve(lambda e, tb=tb, ob_=ob_: e.scalar_tensor_tensor(out=ob[ob_], in0=o12[0][:, tb * 512:(tb + 1) * 512], scalar=gcol[:, 0:1],
                                                                          in1=rsd, op0=ALU.mult, op1=ALU.mult),
                          r=okeys + ['rsd', 'gcol'], w=['ob%d' % ob_])
                    P.dma('sp', lambda e, tb=tb, ob_=ob_, h=h: e.dma_start(out=oscr[h, :, tb * 512:(tb + 1) * 512], in_=ob[ob_]),
                          r=['ob%d' % ob_], w=['oscr'], stream='ob%d' % ob_)
        P.barrier()
        AR.reset(base0)
        stg = [AR.f32(512) for _ in range(3)]
        sn = 0
        for j in range(8):
            wg, kg = wtile(win[:, 3072 + j * 512:3072 + (j + 1) * 512])
            for c4 in range(4):
                c = j * 4 + c4

                def ev_g(tb, bank, c=c):
                    nonlocal sn
                    k3 = sn % 3
                    sn += 1
                    if sn % 2 == 0:
                        P.act(lambda e: e.activation(out=stg[k3], in_=ps[bank][:], func=AF.Copy), r=['ps%d' % bank], w=['stg%d' % k3])
                    else:
                        P.dve(lambda e: e.tensor_copy(out=stg[k3], in_=ps[bank][:]), r=['ps%d' % bank], w=['stg%d' % k3])
                    P.dma('sp', lambda e: e.dma_start(out=pscr[c, :, tb * 512:(tb + 1) * 512], in_=stg[k3]),
                          r=['stg%d' % k3], w=['pscr'], stream='stg%d' % k3)
                proj_fm(hn, wg, kg, c4 * 128, 128, ev_g)
        P.barrier()
        phase_gdn(l)
        phase_outproj(w_out_even[i])

    def phase_gdn(l):
        i = l // 2
        AR.reset()
        tri = AR.f32(1536, parts=64)
        eye = AR.f32(512, parts=64)
        tL = AR.f32(T, parts=66)
        tR = AR.f32(T, parts=66)
        S = [AR.f32(128) for _ in range(2)]
        vn = [AR.f32(128, parts=64) for _ in range(2)]
        eglh = AR.f32(32)
        rsd = AR.f32(512)
        sqd = AR.bf(T)
        ob = [AR.bf(512) for _ in range(2)]
        qT, kT, qdT = AR.f32(T), AR.f32(T), AR.f32(T)
        vb, kbg, kd = [AR.f32(32 * 128, parts=64) for _ in range(3)]
        intraT = AR.f32(T, parts=64)
        nwT = AR.f32(T)
        Xbase = AR.off
        P.dma('sp', lambda e: e.dma_start(out=tri, in_=c_tri), w=['tri'], stream='tri')
        P.dve(lambda e: e.tensor_tensor(out=eye, in0=tri[:, 0:512], in1=tri[:, 512:1024], op=ALU.subtract), r=['tri'], w=['eye'])
        P.dve(lambda e: e.memset(tL, 1.0), w=['tL'])
        P.dve(lambda e: e.memset(tR, 1.0), w=['tR'])
        for h in range(8):
            AR.reset(Xbase)
            xp = [AR.f32(T + 3) for _ in range(2)]
            B = [AR.f32(T) for _ in range(4)]
            vT, kbgT, kdT = AR.f32(T), AR.f32(T), AR.f32(T)
            for q in range(4):
                P.dma('sp', lambda e, q=q, h=h: e.dma_start(out=B[q], in_=rows[q, h, :].partition_broadcast(128)), w=['B%d' % q], stream='B%d' % q)
            P.dma('sp', lambda e, h=h: e.dma_start(out=eglh, in_=eglr[h, :].partition_broadcast(128)), w=['eglh'], stream='eglh')
            for (tt_, prow, q) in ((tL, 0, 5), (tL, 32, 5), (tL, 64, 6), (tR, 1, 4), (tR, 33, 6), (tR, 65, 5)):
                nm = 'tL' if tt_ is tL else 'tR'
                P.dma('sp', lambda e, tt_=tt_, prow=prow, q=q, h=h: e.dma_start(out=tt_[prow:prow + 1, :], in_=rows[q, h:h + 1, :]),
                      w=[nm], stream='%s%d' % (nm, prow))
            for which, c, dst, dk in ((0, h, qT, 'qT'), (1, 8 + h, kT, 'kT'), (2, 16 + h, vT, 'vT')):
                b = which % 2
                P.dve(lambda e, b=b: e.memset(xp[b][:, 0:3], 0.0), w=['xp%d' % b])
                P.dma('sp', lambda e, b=b, c=c: e.dma_start(out=xp[b][:, 3:3 + T], in_=pscr[c]), w=['xp%d' % b], stream='xp%d' % b)
                wc = SM_CONV + (i * 24 + c) * 4
                P.dve(lambda e, b=b, dst=dst, wc=wc: e.tensor_scalar(out=dst, in0=xp[b][:, 0:T], scalar1=sm[:, wc:wc + 1], scalar2=None, op0=ALU.mult),
                      r=['xp%d' % b, 'sm'], w=[dk])
                for j in range(1, 4):
                    P.dve(lambda e, b=b, dst=dst, wc=wc, j=j: e.scalar_tensor_tensor(out=dst, in0=xp[b][:, j:j + T], scalar=sm[:, wc + j:wc + j + 1],
                                                                                    in1=dst, op0=ALU.mult, op1=ALU.add),
                          r=['xp%d' % b, 'sm', dk], w=[dk])
                P.act(lambda e, dst=dst: e.activation(out=dst, in_=dst, func=AF.Silu), r=[dk], w=[dk])
                if which < 2:
                    P.act(lambda e, dst=dst: e.activation(out=sqd, in_=dst, func=AF.Square), r=[dk], w=['sqd'])
                    qs = (128 ** -0.5) if which == 0 else 1.0
                    for tb in range(4):
                        bank = 4 + tb
                        P.pe(lambda e, bank=bank, tb=tb: e.matmul(ps[bank][:], lhsT=onesb[:], rhs=sqd[:, tb * 512:(tb + 1) * 512], start=True, stop=True),
                             r=['sqd', 'onesb'], w=['ps%d' % bank])
                        P.act(lambda e, bank=bank: e.activation(out=rsd, in_=ps[bank][:], func=AF.Sqrt, bias=1e-6, scale=1.0),
                              r=['ps%d' % bank], w=['rsd'])
                        P.dve(lambda e: e.reciprocal(out=rsd, in_=rsd), r=['rsd'], w=['rsd'])
                        P.dve(lambda e, dst=dst, tb=tb, qs=qs: e.scalar_tensor_tensor(out=dst[:, tb * 512:(tb + 1) * 512], in0=dst[:, tb * 512:(tb + 1) * 512],
                                                                                      scalar=qs, in1=rsd, op0=ALU.mult, op1=ALU.mult),
                              r=[dk, 'rsd'], w=[dk])
            P.dve(lambda e: e.tensor_tensor(out=qdT, in0=qT, in1=B[1], op=ALU.mult), r=['qT', 'B1'], w=['qdT'])
            P.dve(lambda e: e.tensor_tensor(out=vT, in0=vT, in1=B[0], op=ALU.mult), r=['vT', 'B0'], w=['vT'])
            P.dve(lambda e: e.tensor_tensor(out=kbgT, in0=kT, in1=B[2], op=ALU.mult), r=['kT', 'B2'], w=['kbgT'])
            P.dve(lambda e: e.tensor_tensor(out=kdT, in0=kT, in1=B[3], op=ALU.mult), r=['kT', 'B3'], w=['kdT'])
            tn = 0
            for X_, Xk, Y_, Yk in ((vT, 'vT', vb, 'vb'), (kbgT, 'kbgT', kbg, 'kbg'), (kdT, 'kdT', kd, 'kd')):
                for n4 in range(8):
                    bank = tn % 8
                    tn += 1
                    for j in range(4):
                        n = n4 * 4 + j
                        P.pe(lambda e, bank=bank, j=j, n=n, X_=X_: e.transpose(out=ps[bank][0:64, j * 128:(j + 1) * 128],
                                                                               in_=X_[:, n * 64:(n + 1) * 64], identity=ident[:]),
                             r=[Xk, 'ident'], w=['ps%d' % bank])
                    if tn % 2 == 0:
                        P.act(lambda e, bank=bank, n4=n4, Y_=Y_: e.activation(out=Y_[:, n4 * 512:(n4 + 1) * 512], in_=ps[bank][0:64, :], func=AF.Copy),
                              r=['ps%d' % bank], w=[Yk])
                    else:
                        P.dve(lambda e, bank=bank, n4=n4, Y_=Y_: e.tensor_copy(out=Y_[:, n4 * 512:(n4 + 1) * 512], in_=ps[bank][0:64, :]),
                              r=['ps%d' % bank], w=[Yk])
            P.barrier()
            AR.reset(Xbase)
            DT, DTb, DTbT = [AR.f32(T, parts=64) for _ in range(3)]
            Pm = [AR.f32(T, parts=64) for _ in range(2)]
            PTm = [AR.f32(T, parts=64) for _ in range(2)]
            Rm = [AR.f32(T, parts=64) for _ in range(2)]
            bn = 0
            for vi, (pb, dst, dk) in enumerate(((0, DT, 'DT'), (32, DTb, 'DTb'), (64, DTbT, 'DTbT'))):
                for n8 in range(4):
                    bank = bn % 8
                    bn += 1
                    for j in range(8):
                        n = n8 * 8 + j
                        P.pe(lambda e, bank=bank, j=j, n=n, pb=pb: e.matmul(ps[bank][0:64, j * 64:(j + 1) * 64], lhsT=tL[pb:pb + 2, n * 64:(n + 1) * 64],
                                                                            rhs=tR[pb:pb + 2, n * 64:(n + 1) * 64], start=True, stop=True),
                             r=['tL', 'tR'], w=['ps%d' % bank])
                    blk = slice(n8 * 512, (n8 + 1) * 512)
                    P.dve(lambda e, bank=bank, dst=dst, blk=blk: e.tensor_scalar(out=dst[:, blk], in0=ps[bank][0:64, :], scalar1=0.0, scalar2=None, op0=ALU.min),
                          r=['ps%d' % bank], w=[dk])
                    P.act(lambda e, dst=dst, blk=blk: e.activation(out=dst[:, blk], in_=dst[:, blk], func=AF.Exp), r=[dk], w=[dk])
                    P.dve(lambda e, dst=dst, blk=blk, vi=vi: e.tensor_tensor(out=dst[:, blk], in0=dst[:, blk], in1=tri[:, vi * 512:(vi + 1) * 512], op=ALU.mult),
                          r=[dk, 'tri'], w=[dk])
            for n8 in range(4):
                blk = slice(n8 * 512, (n8 + 1) * 512)
                bank = bn % 8
                bn += 1
                for j in range(8):
                    n = n8 * 8 + j
                    P.pe(lambda e, bank=bank, j=j, n=n: e.matmul(ps[bank][0:64, j * 64:(j + 1) * 64], lhsT=kT[:, n * 64:(n + 1) * 64],
                                                                 rhs=kT[:, n * 64:(n + 1) * 64], start=True, stop=True), r=['kT'], w=['ps%d' % bank])
                P.dve(lambda e, bank=bank, blk=blk: e.scalar_tensor_tensor(out=Pm[0][:, blk], in0=ps[bank][0:64, :], scalar=-1.0, in1=DTb[:, blk],
                                                                           op0=ALU.mult, op1=ALU.mult), r=['ps%d' % bank, 'DTb'], w=['P0'])
                P.dve(lambda e, bank=bank, blk=blk: e.scalar_tensor_tensor(out=PTm[0][:, blk], in0=ps[bank][0:64, :], scalar=-1.0, in1=DTbT[:, blk],
                                                                           op0=ALU.mult, op1=ALU.mult), r=['ps%d' % bank, 'DTbT'], w=['PT0'])
                bank2 = bn % 8
                bn += 1
                for j in range(8):
                    n = n8 * 8 + j
                    P.pe(lambda e, bank2=bank2, j=j, n=n: e.matmul(ps[bank2][0:64, j * 64:(j + 1) * 64], lhsT=kT[:, n * 64:(n + 1) * 64],
                                                                   rhs=qT[:, n * 64:(n + 1) * 64], start=True, stop=True), r=['kT', 'qT'], w=['ps%d' % bank2])
                P.dve(lambda e, bank2=bank2, blk=blk: e.tensor_tensor(out=intraT[:, blk], in0=ps[bank2][0:64, :], in1=DT[:, blk], op=ALU.mult),
                      r=['ps%d' % bank2, 'DT'], w=['intraT'])
                P.dve(lambda e, blk=blk: e.tensor_tensor(out=Rm[0][:, blk], in0=Pm[0][:, blk], in1=eye, op=ALU.add), r=['P0', 'eye'], w=['R0'])
            cur = 0
            for m in range(1, 6):
                nxt = 1 - cur
                for n8 in range(4):
                    blk = slice(n8 * 512, (n8 + 1) * 512)
                    if m < 5:
                        bA = bn % 8
                        bn += 1
                        for j in range(8):
                            n = n8 * 8 + j
                            P.pe(lambda e, bA=bA, j=j, n=n, cur=cur: e.matmul(ps[bA][0:64, j * 64:(j + 1) * 64], lhsT=PTm[cur][:, n * 64:(n + 1) * 64],
                                                                              rhs=Pm[cur][:, n * 64:(n + 1) * 64], start=True, stop=True),
                                 r=['P%d' % cur, 'PT%d' % cur], w=['ps%d' % bA])
                        P.act(lambda e, bA=bA, blk=blk, nxt=nxt: e.activation(out=Pm[nxt][:, blk], in_=ps[bA][0:64, :], func=AF.Copy),
                              r=['ps%d' % bA], w=['P%d_%d' % (nxt, n8)])
                    bB = bn % 8
                    bn += 1
                    for j in range(8):
                        n = n8 * 8 + j
                        P.pe(lambda e, bB=bB, j=j, n=n, cur=cur: e.matmul(ps[bB][0:64, j * 64:(j + 1) * 64], lhsT=Pm[cur][:, n * 64:(n + 1) * 64],
                                                                          rhs=PTm[cur][:, n * 64:(n + 1) * 64], start=True, stop=True),
                             r=['P%d' % cur, 'PT%d' % cur], w=['ps%d' % bB])
                    P.dve(lambda e, bB=bB, blk=blk, nxt=nxt: e.tensor_copy(out=PTm[nxt][:, blk], in_=ps[bB][0:64, :]),
                          r=['ps%d' % bB], w=['PT%d_%d' % (nxt, n8)])
                for n8 in range(4):
                    blk = slice(n8 * 512, (n8 + 1) * 512)
                    bC = bn % 8
                    bn += 1
                    for j in range(8):
                        n = n8 * 8 + j
                        P.pe(lambda e, bC=bC, j=j, n=n, cur=cur, nxt=nxt: e.matmul(ps[bC][0:64, j * 64:(j + 1) * 64], lhsT=PTm[nxt][:, n * 64:(n + 1) * 64],
                                                                                   rhs=Rm[cur][:, n * 64:(n + 1) * 64], start=True, stop=True),
                             r=['PT%d_%d' % (nxt, n8), 'R%d' % cur], w=['ps%d' % bC])
                    P.dve(lambda e, bC=bC, blk=blk, cur=cur, nxt=nxt: e.tensor_tensor(out=Rm[nxt][:, blk], in0=ps[bC][0:64, :], in1=Rm[cur][:, blk], op=ALU.add),
                          r=['ps%d' % bC, 'R%d' % cur], w=['R%d' % nxt])
                P.dve(lambda e: e.memset(vn[0][:, 0:1], 0.0), r=['P%d_%d' % (nxt, n8) for n8 in range(4)] + ['PT%d_%d' % (nxt, n8) for n8 in range(4)],
                      w=['P%d' % nxt, 'PT%d' % nxt])
                cur = nxt
            Rf = Rm[cur]
            rk = 'R%d' % cur
            for n8 in range(4):
                blk = slice(n8 * 512, (n8 + 1) * 512)
                bank = bn % 8
                bn += 1
                for j in range(8):
                    n = n8 * 8 + j
                    P.pe(lambda e, bank=bank, j=j, n=n: e.matmul(ps[bank][:, j * 64:(j + 1) * 64], lhsT=kbg[:, n * 128:(n + 1) * 128],
                                                                 rhs=Rf[:, n * 64:(n + 1) * 64], start=True, stop=True), r=['kbg', rk], w=['ps%d' % bank])
                P.act(lambda e, bank=bank, blk=blk: e.activation(out=nwT[:, blk], in_=ps[bank][:], func=AF.Copy, scale=-1.0),
                      r=['ps%d' % bank], w=['nwT'])
            P.barrier()
            AR.reset(Xbase)
            oT = AR.f32(T)
            zs = AR.f32(T)
            P.dma('sp', lambda e, h=h: e.dma_start(out=zs, in_=pscr[24 + h]), w=['zs'], stream='zs')
            P.act(lambda e: e.activation(out=zs, in_=zs, func=AF.Silu), r=['zs'], w=['zs'])
            P.dve(lambda e: e.memset(S[0], 0.0), w=['S0'])
            for n in range(32):
                c_, x_ = n % 2, (n + 1) % 2
                bv, bs, bo = 4 + n % 2, 6 + n % 2, (n // 8) % 2
                j = n % 8
                P.pe(lambda e, bv=bv, n=n: e.matmul(ps[bv][0:64, 0:128], lhsT=Rf[:, n * 64:(n + 1) * 64], rhs=vb[:, n * 128:(n + 1) * 128],
                                                    start=True, stop=False), r=[rk, 'vb'], w=['ps%d' % bv])
                P.pe(lambda e, bv=bv, n=n, c_=c_: e.matmul(ps[bv][0:64, 0:128], lhsT=nwT[:, n * 64:(n + 1) * 64], rhs=S[c_],
                                                           start=False, stop=True), r=['nwT', 'S%d' % c_], w=['ps%d' % bv])
                P.act(lambda e, bv=bv, c_=c_: e.activation(out=vn[c_], in_=ps[bv][0:64, 0:128], func=AF.Copy), r=['ps%d' % bv], w=['vn%d' % c_])
                P.pe(lambda e, bo=bo, j=j, n=n, c_=c_: e.matmul(ps[bo][:, j * 64:(j + 1) * 64], lhsT=S[c_], rhs=qdT[:, n * 64:(n + 1) * 64],
                                                                start=True, stop=False), r=['S%d' % c_, 'qdT'], w=['ps%d' % bo])
                P.pe(lambda e, bs=bs, n=n, c_=c_: e.matmul(ps[bs][:, 0:128], lhsT=kd[:, n * 128:(n + 1) * 128], rhs=vn[c_], start=True, stop=True),
                     r=['kd', 'vn%d' % c_], w=['ps%d' % bs])
                P.dve(lambda e, bs=bs, n=n, c_=c_, x_=x_: e.scalar_tensor_tensor(out=S[x_], in0=S[c_], scalar=eglh[:, n:n + 1], in1=ps[bs][:, 0:128],
                                                                                 op0=ALU.mult, op1=ALU.add),
                      r=['S%d' % c_, 'eglh', 'ps%d' % bs], w=['S%d' % x_])
                P.pe(lambda e, bo=bo, j=j, n=n, c_=c_: e.matmul(ps[bo][:, j * 64:(j + 1) * 64], lhsT=vn[c_], rhs=intraT[:, n * 64:(n + 1) * 64],
                                                                start=False, stop=True), r=['vn%d' % c_, 'intraT'], w=['ps%d' % bo])
                if j == 7:
                    P.act(lambda e, bo=bo, n=n: e.activation(out=oT[:, (n - 7) * 64:(n + 1) * 64], in_=ps[bo][:], func=AF.Copy),
                          r=['ps%d' % bo], w=['oT'])
            P.act(lambda e: e.activation(out=sqd, in_=oT, func=AF.Square), r=['oT'], w=['sqd'])
            for tb in range(4):
                bank = 2 + tb % 2
                P.pe(lambda e, bank=bank, tb=tb: e.matmul(ps[bank][:], lhsT=onesb[:], rhs=sqd[:, tb * 512:(tb + 1) * 512], start=True, stop=True),
                     r=['sqd', 'onesb'], w=['ps%d' % bank])
                P.act(lambda e, bank=bank: e.activation(out=rsd, in_=ps[bank][:], func=AF.Sqrt, bias=1e-6, scale=1.0 / 128),
                      r=['ps%d' % bank], w=['rsd'])
                P.dve(lambda e: e.reciprocal(out=rsd, in_=rsd), r=['rsd'], w=['rsd'])
                blk = slice(tb * 512, (tb + 1) * 512)
                P.dve(lambda e, blk=blk: e.scalar_tensor_tensor(out=oT[:, blk], in0=oT[:, blk], scalar=sm[:, SM_GDN + i:SM_GDN + i + 1], in1=rsd,
                                                                op0=ALU.mult, op1=ALU.mult), r=['oT', 'rsd', 'sm'], w=['oT'])
                ob_ = tb % 2
                P.dve(lambda e, blk=blk, ob_=ob_: e.tensor_tensor(out=ob[ob_], in0=oT[:, blk], in1=zs[:, blk], op=ALU.mult),
                      r=['oT', 'zs'], w=['gob%d' % ob_])
                P.dma('sp', lambda e, tb=tb, ob_=ob_, h=h: e.dma_start(out=oscr[8 + h, :, tb * 512:(tb + 1) * 512], in_=ob[ob_]),
                      r=['gob%d' % ob_], w=['oscr'], stream='gob%d' % ob_)
            P.barrier()

    def phase_final():
        AR.reset()
        st = [AR.f32(16 * 512) for _ in range(2)]
        sq = AR.bf(16 * 512)
        rs = AR.f32(512)
        yo = [AR.f32(D) for _ in range(2)]
        n = 0
        for tb in range(4):
            b = tb % 2
            P.dma('sp', lambda e, b=b, tb=tb: e.dma_start(
                out=st[b].rearrange("p (k t) -> p k t", k=16, t=512),
                in_=hT[:, :, tb * 512:(tb + 1) * 512].rearrange("k p t -> p k t")),
                w=['nst%d' % b], stream='nst%d' % b)
            P.act(lambda e, b=b: e.activation(out=sq, in_=st[b], func=AF.Square), r=['nst%d' % b], w=['nsq'])
            bank = tb % 2
            for kc in range(16):
                P.pe(lambda e, kc=kc, bank=bank: e.matmul(ps[bank][:], lhsT=onesb[:], rhs=sq[:, kc * 512:(kc + 1) * 512],
                                                           start=(kc == 0), stop=(kc == 15)),
                     r=['nsq', 'onesb'], w=['ps%d' % bank])
            P.act(lambda e, bank=bank: e.activation(out=rs, in_=ps[bank][:], func=AF.Sqrt, bias=1e-6, scale=1.0 / D),
                  r=['ps%d' % bank], w=['nrs'])
            P.dve(lambda e: e.reciprocal(out=rs, in_=rs), r=['nrs'], w=['nrs'])
            for kc in range(16):
                P.dve(lambda e, b=b, kc=kc: e.scalar_tensor_tensor(
                    out=st[b][:, kc * 512:(kc + 1) * 512], in0=st[b][:, kc * 512:(kc + 1) * 512],
                    scalar=sm[:, 8 * 16 + kc: 8 * 16 + kc + 1], in1=rs, op0=ALU.mult, op1=ALU.mult),
                    r=['nst%d' % b, 'nrs', 'sm'], w=['nst%d' % b])
            for t4 in range(4):
                yb = n % 2
                n += 1
                tt = tb * 4 + t4
                for q in range(4):
                    bank = 4 + (n * 4 + q) % 4
                    for j in range(4):
                        kc = q * 4 + j
                        P.pe(lambda e, b=b, kc=kc, bank=bank, j=j, t4=t4: e.transpose(
                            out=ps[bank][:, j * 128:(j + 1) * 128], in_=st[b][:, kc * 512 + t4 * 128: kc * 512 + (t4 + 1) * 128],
                            identity=ident[:]), r=['nst%d' % b, 'ident'], w=['ps%d' % bank])
                    if q % 2 == 0:
                        P.act(lambda e, yb=yb, q=q, bank=bank: e.activation(out=yo[yb][:, q * 512:(q + 1) * 512], in_=ps[bank][:], func=AF.Copy),
                              r=['ps%d' % bank], w=['yo%d_%d' % (yb, q)])
                    else:
                        P.dve(lambda e, yb=yb, q=q, bank=bank: e.tensor_copy(out=yo[yb][:, q * 512:(q + 1) * 512], in_=ps[bank][:]),
                              r=['ps%d' % bank], w=['yo%d_%d' % (yb, q)])
                P.dma('sp', lambda e, yb=yb, tt=tt: e.dma_start(out=y[tt * 128:(tt + 1) * 128, :], in_=yo[yb]),
                      r=['yo%d_%d' % (yb, q) for q in range(4)], w=['y'], stream='yo%d' % yb)

    phase_load_x()
    for kind, l in plan:
        if kind == 'fox':
            phase_fox(l)
        elif kind == 'even':
            phase_even(l)
        else:
            phase_mlp(l)
    phase_final()
    P.emit(final_wait_streams=['yo0', 'yo1'])
    return nc, P


SM_GAIN = 0
SM_CONV = 144
SM_DAN = SM_CONV + 192
SM_GDN = SM_DAN + 2
SM_ALOG = SM_GDN + 2
SM_DTB = SM_ALOG + 2
SM_FOXB = SM_DTB + 2
SM_LAM = SM_FOXB + 2
SM_COLS = SM_LAM + 512


def pack_small(inp):
    sm = np.zeros((128, SM_COLS), np.float32)
    gains = [inp['norm_mix'][l] for l in range(4)] + [inp['norm_mlp'][l] for l in range(4)] + [inp['norm_final']]
    for n, g in enumerate(gains):
        sm[:, SM_GAIN + n * 16: SM_GAIN + (n + 1) * 16] = np.asarray(g).reshape(16, 128).T
    cw = np.asarray(inp['conv_w'])
    for i in range(2):
        for c in range(24):
            for j in range(4):
                sm[:, SM_CONV + (i * 24 + c) * 4 + j] = cw[i, j, c * 128:(c + 1) * 128]
    for i in range(2):
        sm[:, SM_DAN + i] = inp['da_norm'][i]
        sm[:, SM_GDN + i] = inp['gdn_norm'][i]
        sm[0:8, SM_ALOG + i] = inp['gdn_a_log'][i]
        sm[0:8, SM_DTB + i] = inp['gdn_dt_bias'][i]
        sm[0:16, SM_FOXB + i] = inp['fox_b_f'][i]
        for j, nm in enumerate(['lam_q1', 'lam_k1', 'lam_q2', 'lam_k2']):
            sm[0, SM_LAM + (i * 4 + j) * 64: SM_LAM + (i * 4 + j + 1) * 64] = inp[nm][i]
    return sm


def make_consts():
    c = {}
    c['c_ident'] = np.eye(128, dtype=np.float32)
    s = np.arange(128)
    c['c_maskneg'] = np.where(s[:, None] > s[None, :], NEG, 0.0).astype(np.float32)
    t = np.arange(T)
    al = np.zeros((4, 2 * T), np.float32)
    al[0, :T] = (t // 128) * 128
    al[1, :T] = t % 128
    al[2, :T] = 1
    al[3, :T] = 1
    al[0, T:] = 1
    al[1, T:] = 1
    al[2, T:] = -((t // 128) * 128)
    al[3, T:] = -(t % 128)
    c['c_alibi'] = al
    j = np.arange(64)
    tri = np.zeros((64, 3, 8, 64), np.float32)
    tri[:, 0] = (j[None, :] >= j[:, None]).astype(np.float32)[:, None, :]
    tri[:, 1] = (j[None, :] > j[:, None]).astype(np.float32)[:, None, :]
    tri[:, 2] = (j[None, :] < j[:, None]).astype(np.float32)[:, None, :]
    c['c_tri'] = tri.reshape(64, 3 * 512)
    cm = np.ones((8, T), np.float32)
    cm[:, ::64] = 0
    c['c_cmask'] = cm
    return c


_CACHE = {}
REAL_CORES = [0, 1, 4, 5]


def kernel(**inputs):
    inp = {k: np.asarray(v) for k, v in inputs.items()}
    if 'nc' not in _CACHE:
        _CACHE['nc'] = build_program()[0]
    nc = _CACHE['nc']
    sm = pack_small(inp)
    consts = make_consts()
    shared = dict(w_in_even=inp['w_in_even'], w_out_even=inp['w_out_even'], w_in_odd=inp['w_in_odd'],
                  w_out_odd=inp['w_out_odd'], w_up=inp['w_up'], w_down=inp['w_down'], sm=sm, **consts)
    zeros = {k: np.zeros_like(v) for k, v in shared.items()}
    zeros['x'] = np.zeros_like(inp['x'][0])
    in_maps = []
    for c in range(8):
        if c in REAL_CORES:
            m = dict(shared)
            m['x'] = np.ascontiguousarray(inp['x'][REAL_CORES.index(c)])
        else:
            m = zeros
        in_maps.append(m)
    res = run_bass_kernel_spmd(nc, in_maps, core_ids=list(range(8)))
    out = np.stack([res.results[c]['y'] for c in REAL_CORES], axis=0)
    return out.astype(np.float32)
```

```python
import contextlib
import math
import numpy as np
import concourse.bass as bass
import concourse.mybir as mybir
from concourse.bass_utils import run_bass_kernel_spmd

F32 = mybir.dt.float32
BF16 = mybir.dt.bfloat16
AF = mybir.ActivationFunctionType
ALU = mybir.AluOpType

ENGS = ('pe', 'act', 'dve', 'pool', 'sp')
T = 2048
D = 2048
DFF = 8192
EVEN_IN = 7184
ODD_IN = 6160
NEG = -30000.0


class Prog:
    def __init__(self, nc):
        self.nc = nc
        self.ops = []
        self.last_w = {}
        self.readers = {}
        self.stream_last = {}
        self.eng_last = {}
        self.pending_bar = {}
        self.stack = contextlib.ExitStack()

    def sbuf(self, name, shape, dtype):
        return self.stack.enter_context(self.nc.sbuf_tensor("sb_" + name, list(shape), dtype))

    def psum(self, name, shape, dtype):
        return self.stack.enter_context(self.nc.psum_tensor(name, list(shape), dtype))

    def add(self, eng, fn, r=(), w=(), stream=None):
        idx = len(self.ops)
        raw = set()
        oth = set()
        for k in r:
            lw = self.last_w.get(k)
            if lw is not None:
                raw.add(lw)
        for k in w:
            lw = self.last_w.get(k)
            if lw is not None:
                oth.add(lw)
            oth.update(self.readers.get(k, ()))
        if stream is not None:
            p = self.stream_last.get(stream)
            if p is not None:
                raw.add(p)
            self.stream_last[stream] = idx
        if eng in self.pending_bar:
            raw |= self.pending_bar.pop(eng)
        for k in r:
            self.readers.setdefault(k, []).append(idx)
        for k in w:
            self.last_w[k] = idx
            self.readers[k] = []
        raw.discard(idx)
        oth.discard(idx)
        self.ops.append(dict(eng=eng, fn=fn, raw=raw, oth=oth - raw, stream=stream, bar=False))
        if stream is None:
            self.eng_last[eng] = idx
        return idx

    def barrier(self):
        deps = set(self.eng_last.values()) | set(self.stream_last.values())
        for e in ENGS:
            self.pending_bar[e] = set(deps) | self.pending_bar.get(e, set())
        self.last_w = {}
        self.readers = {}

    def pe(self, fn, r=(), w=()):
        return self.add('pe', fn, r, w)

    def act(self, fn, r=(), w=()):
        return self.add('act', fn, r, w)

    def dve(self, fn, r=(), w=()):
        return self.add('dve', fn, r, w)

    def pool(self, fn, r=(), w=()):
        return self.add('pool', fn, r, w)

    def dma(self, eng, fn, r=(), w=(), stream=None):
        return self.add(eng, fn, r, w, stream=stream)

    def emit(self, final_wait_streams=()):
        nc = self.nc
        ops = self.ops
        for o in ops:
            deps = set()
            for d in o['raw']:
                if ops[d]['stream'] is None and ops[d]['eng'] == o['eng'] and o['eng'] == 'pe':
                    continue
                deps.add(d)
            for d in o['oth']:
                if ops[d]['stream'] is None and ops[d]['eng'] == o['eng']:
                    continue
                deps.add(d)
            best = {}
            for d in deps:
                k = ('s', ops[d]['stream']) if ops[d]['stream'] is not None else ('e', ops[d]['eng'])
                if k not in best or best[k] < d:
                    best[k] = d
            o['deps'] = set(best.values())
        needed = set()
        for o in ops:
            needed |= o['deps']
        cnt = {e: 0 for e in ENGS}
        scnt = {}
        for i, o in enumerate(ops):
            if o['stream'] is not None:
                s = o['stream']
                scnt[s] = scnt.get(s, 0) + 16
                o['done'] = (('dma', s), scnt[s])
            elif i in needed:
                cnt[o['eng']] += 1
                o['done'] = (('eng', o['eng']), cnt[o['eng']])
            else:
                o['done'] = None
        self.sem_counts = dict(cnt)
        sems = {}
        for e in ENGS:
            sems[('eng', e)] = self.stack.enter_context(nc.semaphore('s_' + e))
        for s in scnt:
            sems[('dma', s)] = self.stack.enter_context(nc.semaphore('d_' + str(s)))
        self.n_sems = len(sems)

        def run_engine(eng_name, engine):
            waited = {}
            for o in ops:
                if o['eng'] != eng_name:
                    continue
                need = {}
                for d in o['deps']:
                    semkey, val = ops[d]['done']
                    if need.get(semkey, 0) < val:
                        need[semkey] = val
                for semkey, val in need.items():
                    if waited.get(semkey, 0) >= val:
                        continue
                    engine.wait_ge(sems[semkey], val)
                    waited[semkey] = val
                ins = o['fn'](engine)
                if o['done'] is not None:
                    semkey, val = o['done']
                    ins.then_inc(sems[semkey], 16 if semkey[0] == 'dma' else 1)
            if eng_name == 'sp':
                for s in final_wait_streams:
                    engine.wait_ge(sems[('dma', s)], scnt[s])

        with nc.Block() as block:
            @block.tensor
            def _(e):
                run_engine('pe', e)

            @block.scalar
            def _(e):
                run_engine('act', e)

            @block.vector
            def _(e):
                run_engine('dve', e)

            @block.gpsimd
            def _(e):
                run_engine('pool', e)

            @block.sync
            def _(e):
                run_engine('sp', e)
        self.stack.close()


class Arena:
    def __init__(self, P, nbytes):
        self.t32 = P.sbuf("arena", [128, nbytes // 4], F32)
        self.t16 = self.t32.bitcast(BF16)
        self.nbytes = nbytes
        self.off = 0
        self.uid = 0

    def reset(self, off=0):
        self.off = off

    def _take(self, nbytes):
        o = self.off
        self.off += (nbytes + 31) // 32 * 32
        assert self.off <= self.nbytes, ("arena overflow", self.off, self.nbytes)
        return o

    def f32(self, n, parts=128):
        o = self._take(n * 4)
        return self.t32[0:parts, o // 4:o // 4 + n]

    def bf(self, n, parts=128):
        o = self._take(n * 2)
        return self.t16[0:parts, o // 2:o // 2 + n]


FULL_PLAN = [('even', 0), ('mlp', 0), ('fox', 1), ('mlp', 1), ('even', 2), ('mlp', 2), ('fox', 3), ('mlp', 3)]


def build_program(plan=None, dbg=False):
    plan = FULL_PLAN if plan is None else plan
    nc = bass.Bass("TRN2", target_bir_lowering=False)
    P = Prog(nc)

    def din(name, shape, dt=F32):
        return nc.dram_tensor(name, list(shape), dt, kind="ExternalInput").ap()

    x = din("x", [T, D])
    w_in_even = din("w_in_even", [2, D, EVEN_IN])
    w_out_even = din("w_out_even", [2, D, D])
    w_in_odd = din("w_in_odd", [2, D, ODD_IN])
    w_out_odd = din("w_out_odd", [2, D, D])
    w_up = din("w_up", [4, D, DFF])
    w_down = din("w_down", [4, DFF, D])
    sm_d = din("sm", [128, SM_COLS])
    c_ident = din("c_ident", [128, 128])
    c_maskneg = din("c_maskneg", [128, 128])
    c_alibi = din("c_alibi", [4, 2 * T])
    c_tri = din("c_tri", [64, 3 * 512])
    c_cmask = din("c_cmask", [8, T])
    y = nc.dram_tensor("y", [T, D], F32, kind="ExternalOutput").ap()
    hT = nc.dram_tensor("hT", [16, 128, T], F32).ap()
    oscr = nc.dram_tensor("oscr", [16, 128, T], BF16, **({"kind": "ExternalOutput"} if dbg else {})).ap()
    pscr = nc.dram_tensor("pscr", [32, 128, T], F32, **({"kind": "ExternalOutput"} if dbg else {})).ap()
    rows = nc.dram_tensor("rows", [10, 8, T], F32, **({"kind": "ExternalOutput"} if dbg else {})).ap()
    eglr = nc.dram_tensor("eglr", [8, 32], F32).ap()
    cbs = nc.dram_tensor("cbs", [6, 16, T], BF16).ap()

    sm = P.sbuf("sm", [128, SM_COLS], F32)
    ident = P.sbuf("ident", [128, 128], F32)
    identb = P.sbuf("identb", [128, 128], BF16)
    masknegb = P.sbuf("masknegb", [128, 128], BF16)
    ones32 = P.sbuf("ones32", [128, 128], F32)
    onesb = P.sbuf("onesb", [128, 128], BF16)
    ps = [P.psum("ps%d" % i, [128, 512], F32) for i in range(8)]
    AR = Arena(P, 202 * 1024)

    uid = [0]

    def U(s):
        uid[0] += 1
        return "%s_%d" % (s, uid[0])

    P.dma('sp', lambda e: e.dma_start(out=sm[:], in_=sm_d), w=['sm'], stream='c0')
    P.dma('sp', lambda e: e.dma_start(out=ident[:], in_=c_ident), w=['ident'], stream='c1')
    P.dma('pool', lambda e: e.dma_start(out=masknegb[:], in_=c_maskneg), w=['masknegb'], stream='c2')
    P.dve(lambda e: e.tensor_copy(out=identb[:], in_=ident[:]), r=['ident'], w=['identb'])
    P.dve(lambda e: e.memset(ones32[:], 1.0), w=['ones32'])
    P.dve(lambda e: e.memset(onesb[:], 1.0), w=['onesb'])
    P.barrier()

    wstate = dict(n=0, bufs=None)

    def wtile(src, ncols=512, nk=16):
        b = wstate['n'] % len(wstate['bufs'])
        wstate['n'] += 1
        buf = wstate['bufs'][b]
        view = buf[:, 0:nk * ncols].rearrange("p (k n) -> p k n", k=nk, n=ncols)
        key = 'wb%d' % b
        P.dma('pool', lambda e: e.dma_start(out=view, in_=src.rearrange("(k p) n -> p k n", p=128)),
              w=[key], stream=key)
        return view, key

    def phase_load_x():
        AR.reset()
        xs = [AR.f32(D) for _ in range(2)]
        xo = [AR.f32(2048) for _ in range(2)]
        for tt in range(16):
            b = tt % 2
            P.dma('sp', lambda e, b=b, tt=tt: e.dma_start(out=xs[b], in_=x[tt * 128:(tt + 1) * 128, :]),
                  w=['xs%d' % b], stream='xs%d' % b)
            for q in range(4):
                bank = (tt * 4 + q) % 8
                for j in range(4):
                    kc = q * 4 + j
                    P.pe(lambda e, b=b, kc=kc, bank=bank, j=j: e.transpose(
                        out=ps[bank][:, j * 128:(j + 1) * 128], in_=xs[b][:, kc * 128:(kc + 1) * 128], identity=ident[:]),
                        r=['xs%d' % b, 'ident'], w=['ps%d' % bank])
                eng = P.act if q % 2 == 0 else P.dve
                if q % 2 == 0:
                    P.act(lambda e, b=b, q=q, bank=bank: e.activation(out=xo[b][:, q * 512:(q + 1) * 512], in_=ps[bank][:], func=AF.Copy),
                          r=['ps%d' % bank], w=['xo%d_%d' % (b, q)])
                else:
                    P.dve(lambda e, b=b, q=q, bank=bank: e.tensor_copy(out=xo[b][:, q * 512:(q + 1) * 512], in_=ps[bank][:]),
                          r=['ps%d' % bank], w=['xo%d_%d' % (b, q)])
            P.dma('sp', lambda e, b=b, tt=tt: e.dma_start(
                out=hT[:, :, tt * 128:(tt + 1) * 128].rearrange("k p t -> p k t"),
                in_=xo[b].rearrange("p (k t) -> p k t", k=16, t=128)),
                r=['xo%d_%d' % (b, q) for q in range(4)], w=['hT'], stream='xo%d' % b)
        P.barrier()

    def phase_norm(norm_idx, hn):
        base = AR.off
        st = [AR.f32(16 * 512) for _ in range(2)]
        sq = AR.bf(16 * 512)
        rs = AR.f32(512)
        for tb in range(4):
            b = tb % 2
            P.dma('sp', lambda e, b=b, tb=tb: e.dma_start(
                out=st[b].rearrange("p (k t) -> p k t", k=16, t=512),
                in_=hT[:, :, tb * 512:(tb + 1) * 512].rearrange("k p t -> p k t")),
                w=['nst%d' % b], stream='nst%d' % b)
            P.act(lambda e, b=b: e.activation(out=sq, in_=st[b], func=AF.Square), r=['nst%d' % b], w=['nsq'])
            bank = tb % 2
            for kc in range(16):
                P.pe(lambda e, kc=kc, bank=bank: e.matmul(ps[bank][:], lhsT=onesb[:], rhs=sq[:, kc * 512:(kc + 1) * 512],
                                                           start=(kc == 0), stop=(kc == 15)),
                     r=['nsq', 'onesb'], w=['ps%d' % bank])
            P.act(lambda e, bank=bank: e.activation(out=rs, in_=ps[bank][:], func=AF.Sqrt, bias=1e-6, scale=1.0 / D),
                  r=['ps%d' % bank], w=['nrs'])
            P.dve(lambda e: e.reciprocal(out=rs, in_=rs), r=['nrs'], w=['nrs'])
            for kc in range(16):
                P.dve(lambda e, b=b, kc=kc, tb=tb: e.scalar_tensor_tensor(
                    out=hn[:, kc * T + tb * 512: kc * T + (tb + 1) * 512], in0=st[b][:, kc * 512:(kc + 1) * 512],
                    scalar=sm[:, norm_idx * 16 + kc: norm_idx * 16 + kc + 1], in1=rs, op0=ALU.mult, op1=ALU.mult),
                    r=['nst%d' % b, 'nrs', 'sm'], w=['hn'])
        AR.reset(base)
        P.barrier()

    att = dict(n=0, sc=0)

    pj = dict(n=0)

    def proj_fm(hn, wv, wkey, c0, M, evac):
        for tb in range(4):
            bank = pj['n'] % 4 + 4
            pj['n'] += 1
            for kc in range(16):
                P.pe(lambda e, bank=bank, kc=kc, tb=tb: e.matmul(
                    ps[bank][0:M, :], lhsT=wv[:, kc, c0:c0 + M], rhs=hn[:, kc * T + tb * 512: kc * T + (tb + 1) * 512],
                    start=(kc == 0), stop=(kc == 15)), r=[wkey, 'hn'], w=['ps%d' % bank])
            evac(tb, bank)

    def phase_outproj(w_out):
        AR.reset()
        oall = AR.bf(16 * T)
        wstate['bufs'] = [AR.bf(16 * 512) for _ in range(3)]
        hst = [AR.f32(512) for _ in range(3)]
        for k4 in range(4):
            P.dma('sp', lambda e, k4=k4: e.dma_start(
                out=oall[:, k4 * 4 * T:(k4 + 1) * 4 * T].rearrange("p (k t) -> p k t", k=4, t=T),
                in_=oscr[k4 * 4:(k4 + 1) * 4].rearrange("k p t -> p k t")), w=['hn'], stream='oall%d' % k4)
        residual_matmul(oall, 16, lambda cg: w_out[:, cg * 512:(cg + 1) * 512], hst)
        P.barrier()

    rz = dict(n=0)

    def residual_matmul(act, nk, wsrc, hst):
        for cg in range(4):
            tiles = []
            for kg in range(nk // 16):
                src = wsrc(cg)
                tiles.append(wtile(src[kg * 2048:(kg + 1) * 2048, :]))
            for c4 in range(4):
                dc = cg * 4 + c4
                for tb in range(4):
                    n = rz['n']
                    rz['n'] += 1
                    bank = n % 8
                    hb = n % 3
                    P.dma('sp', lambda e, hb=hb, dc=dc, tb=tb: e.dma_start(out=hst[hb], in_=hT[dc, :, tb * 512:(tb + 1) * 512]),
                          r=['hT%d_%d' % (dc, tb)], w=['hst%d' % hb], stream='hst%d' % hb)
                    for kk in range(nk):
                        wv, wkey = tiles[kk // 16]
                        P.pe(lambda e, bank=bank, wv=wv, kk=kk, c4=c4, tb=tb: e.matmul(
                            ps[bank][:], lhsT=wv[:, kk % 16, c4 * 128:(c4 + 1) * 128],
                            rhs=act[:, kk * T + tb * 512: kk * T + (tb + 1) * 512], start=(kk == 0), stop=(kk == nk - 1)),
                            r=[wkey, 'hn'], w=['ps%d' % bank])
                    P.dve(lambda e, bank=bank, hb=hb: e.tensor_tensor(out=hst[hb], in0=ps[bank][:], in1=hst[hb], op=ALU.add),
                          r=['ps%d' % bank, 'hst%d' % hb], w=['hst%d' % hb])
                    P.dma('sp', lambda e, hb=hb, dc=dc, tb=tb: e.dma_start(out=hT[dc, :, tb * 512:(tb + 1) * 512], in_=hst[hb]),
                          r=['hst%d' % hb], w=['hT%d_%d' % (dc, tb)], stream='hst%d' % hb)

    def phase_mlp(l):
        AR.reset()
        hn = AR.bf(16 * T)
        wstate['bufs'] = [AR.bf(16 * 512) for _ in range(3)]
        phase_norm(4 + l, hn)
        aT = AR.bf(64 * 512)
        rl = [AR.f32(512) for _ in range(2)]
        hst = [AR.f32(512) for _ in range(3)]
        n = 0
        for tb in range(4):
            for fg in range(16):
                wv, wkey = wtile(w_up[l][:, fg * 512:(fg + 1) * 512])
                for c4 in range(4):
                    bank = n % 8
                    rb_ = n % 2
                    n += 1
                    for kc in range(16):
                        P.pe(lambda e, bank=bank, wv=wv, kc=kc, c4=c4, tb=tb: e.matmul(
                            ps[bank][:], lhsT=wv[:, kc, c4 * 128:(c4 + 1) * 128],
                            rhs=hn[:, kc * T + tb * 512: kc * T + (tb + 1) * 512], start=(kc == 0), stop=(kc == 15)),
                            r=[wkey, 'hn'], w=['ps%d' % bank])
                    P.act(lambda e, bank=bank, rb_=rb_: e.activation(out=rl[rb_], in_=ps[bank][:], func=AF.Relu),
                          r=['ps%d' % bank], w=['rl%d' % rb_])
                    fc = fg * 4 + c4
                    P.dve(lambda e, rb_=rb_, fc=fc: e.tensor_tensor(out=aT[:, fc * 512:(fc + 1) * 512], in0=rl[rb_], in1=rl[rb_], op=ALU.mult),
                          r=['rl%d' % rb_], w=['aT%d' % fc])
            for cg in range(4):
                banks = [(n + j) % 8 for j in range(4)]
                n += 4
                hbs = []
                for c4 in range(4):
                    dc = cg * 4 + c4
                    hb = rz['n'] % 3
                    rz['n'] += 1
                    hbs.append(hb)
                for fr in range(4):
                    wv, wkey = wtile(w_down[l][fr * 2048:(fr + 1) * 2048, cg * 512:(cg + 1) * 512])
                    for c4 in range(4):
                        for fc in range(16):
                            ff = fr * 16 + fc
                            P.pe(lambda e, bank=banks[c4], wv=wv, fc=fc, c4=c4, ff=ff, fr=fr: e.matmul(
                                ps[bank][:], lhsT=wv[:, fc, c4 * 128:(c4 + 1) * 128], rhs=aT[:, ff * 512:(ff + 1) * 512],
                                start=(fr == 0 and fc == 0), stop=(fr == 3 and fc == 15)),
                                r=[wkey, 'aT%d' % ff], w=['ps%d' % banks[c4]])
                for c4 in range(4):
                    dc = cg * 4 + c4
                    hb = hbs[c4]
                    bank = banks[c4]
                    P.dma('sp', lambda e, hb=hb, dc=dc, tb=tb: e.dma_start(out=hst[hb], in_=hT[dc, :, tb * 512:(tb + 1) * 512]),
                          r=['hT%d_%d' % (dc, tb)], w=['hst%d' % hb], stream='hst%d' % hb)
                    P.dve(lambda e, bank=bank, hb=hb: e.tensor_tensor(out=hst[hb], in0=ps[bank][:], in1=hst[hb], op=ALU.add),
                          r=['ps%d' % bank, 'hst%d' % hb], w=['hst%d' % hb])
                    P.dma('sp', lambda e, hb=hb, dc=dc, tb=tb: e.dma_start(out=hT[dc, :, tb * 512:(tb + 1) * 512], in_=hst[hb]),
                          r=['hst%d' % hb], w=['hT%d_%d' % (dc, tb)], stream='hst%d' % hb)
        P.barrier()

    def phase_fox(l):
        i = l // 2
        win = w_in_odd[i]
        AR.reset()
        hn = AR.bf(16 * T)
        wstate['bufs'] = [AR.bf(16 * 512) for _ in range(3)]
        phase_norm(l, hn)
        base0 = AR.off
        wf = AR.bf(16 * 16)
        frow = AR.f32(T, parts=16)
        c3 = [AR.f32(T, parts=16) for _ in range(2)]
        cb = [AR.bf(T, parts=16) for _ in range(6)]
        wfv = wf.rearrange("p (k n) -> p k n", k=16, n=16)
        P.dma('pool', lambda e: e.dma_start(out=wfv, in_=win[:, 6144:6160].rearrange("(k p) n -> p k n", p=128)),
              w=['wf'], stream='wf')
        P.dve(lambda e: e.tensor_scalar(out=c3[1][:, 0:1], in0=sm[0:16, SM_FOXB + i:SM_FOXB + i + 1], scalar1=-1.0, scalar2=None,
                                        op0=ALU.mult), r=['sm'], w=['nbf'])

        def ev_f(tb, bank):
            P.act(lambda e, tb=tb, bank=bank: e.activation(out=frow[:, tb * 512:(tb + 1) * 512], in_=ps[bank][0:16, :], func=AF.Exp,
                                                           bias=c3[1][:, 0:1], scale=-1.0), r=['ps%d' % bank, 'nbf'], w=['frow'])
        proj_fm(hn, wfv, 'wf', 0, 16, ev_f)
        P.act(lambda e: e.activation(out=frow, in_=frow, func=AF.Ln, bias=1.0, scale=1.0), r=['frow'], w=['frow'])
        P.dve(lambda e: e.memset(c3[1], 1.0), r=['frow'], w=['nbf'])
        P.dve(lambda e: e.tensor_tensor_scan(out=c3[0], data0=c3[1], data1=frow, initial=0.0, op0=ALU.mult, op1=ALU.add),
              r=['frow', 'nbf'], w=['cpos'])
        P.dve(lambda e: e.tensor_copy(out=cb[0], in_=c3[0]), r=['cpos'], w=['cb0'])
        P.dve(lambda e: e.tensor_tensor(out=c3[1], in0=c3[0], in1=cb[0], op=ALU.subtract), r=['cpos', 'cb0'], w=['nbf'])
        P.dve(lambda e: e.tensor_copy(out=cb[1], in_=c3[1]), r=['nbf'], w=['cb1'])
        P.dve(lambda e: e.tensor_tensor(out=c3[0], in0=c3[1], in1=cb[1], op=ALU.subtract), r=['nbf', 'cb1'], w=['cpos'])
        P.dve(lambda e: e.tensor_copy(out=cb[2], in_=c3[0]), r=['cpos'], w=['cb2'])
        for j in range(3):
            P.dve(lambda e, j=j: e.tensor_scalar(out=cb[3 + j], in0=cb[j], scalar1=-1.0, scalar2=None, op0=ALU.mult),
                  r=['cb%d' % j], w=['cb%d' % (3 + j)])
        for j in range(6):
            P.dma('sp', lambda e, j=j: e.dma_start(out=cbs[j], in_=cb[j]), r=['cb%d' % j], w=['cbs'], stream='cbs%d' % j)
        P.barrier()
        AR.reset(base0)
        QT = AR.bf(4 * T)
        KT = AR.bf(4 * T)
        V = AR.bf(16 * 512)
        pT = [AR.bf(512) for _ in range(3)]
        osb = [AR.bf(512) for _ in range(2)]
        rec = [AR.f32(512) for _ in range(2)]
        lb = [AR.bf(T, parts=6) for _ in range(2)]
        rb = [AR.bf(T, parts=6) for _ in range(2)]
        for b in range(2):
            P.dve(lambda e, b=b: e.memset(lb[b], 1.0), w=['lb%d' % b])
            P.dve(lambda e, b=b: e.memset(rb[b], 1.0), w=['rb%d' % b])
        scale = 128 ** -0.5
        for g in range(4):
            wq, kq = wtile(win[:, g * 512:(g + 1) * 512])
            for hh in range(4):
                def ev_q(tb, bank, hh=hh):
                    P.act(lambda e, tb=tb, bank=bank: e.activation(out=QT[:, hh * T + tb * 512: hh * T + (tb + 1) * 512], in_=ps[bank][:],
                                                                   func=AF.Copy, scale=scale), r=['ps%d' % bank], w=['QT'])
                proj_fm(hn, wq, kq, hh * 128, 128, ev_q)
            wk, kk_ = wtile(win[:, 2048 + g * 512:2048 + (g + 1) * 512])
            for hh in range(4):
                def ev_k(tb, bank, hh=hh):
                    P.dve(lambda e, tb=tb, bank=bank: e.tensor_copy(out=KT[:, hh * T + tb * 512: hh * T + (tb + 1) * 512], in_=ps[bank][:]),
                          r=['ps%d' % bank], w=['KT'])
                proj_fm(hn, wk, kk_, hh * 128, 128, ev_k)
            wv, kv = wtile(win[:, 4096 + g * 512:4096 + (g + 1) * 512])
            for tt in range(16):
                bank = pj['n'] % 4 + 4
                pj['n'] += 1
                for kc in range(16):
                    P.pe(lambda e, bank=bank, kc=kc, tt=tt, wv=wv: e.matmul(ps[bank][:], lhsT=hn[:, kc * T + tt * 128: kc * T + (tt + 1) * 128],
                                                                           rhs=wv[:, kc, :], start=(kc == 0), stop=(kc == 15)),
                         r=[kv, 'hn'], w=['ps%d' % bank])
                if tt % 2 == 0:
                    P.act(lambda e, bank=bank, tt=tt: e.activation(out=V[:, tt * 512:(tt + 1) * 512], in_=ps[bank][:], func=AF.Copy),
                          r=['ps%d' % bank], w=['V'])
                else:
                    P.dve(lambda e, bank=bank, tt=tt: e.tensor_copy(out=V[:, tt * 512:(tt + 1) * 512], in_=ps[bank][:]),
                          r=['ps%d' % bank], w=['V'])
            for hh in range(4):
                h = g * 4 + hh
                b = h % 2
                P.dma('sp', lambda e, b=b, h=h: e.dma_start(out=lb[b][3:6, :], in_=cbs[0:3, h, :]), r=['cbs'], w=['lb%d' % b], stream='lb%d' % b)
                P.dma('sp', lambda e, b=b, h=h: e.dma_start(out=rb[b][0:3, :], in_=cbs[3:6, h, :]), r=['cbs'], w=['rb%d' % b], stream='rb%d' % b)

                def osbf(qb, h=h):
                    ob_ = (h * 4 + qb) % 2
                    return osb[ob_], 'osb%d' % ob_, rec[ob_], 'rec%d' % ob_
                attention_with_store(QT[:, hh * T:(hh + 1) * T], KT[:, hh * T:(hh + 1) * T],
                                     lambda kt, hh=hh: V[:, kt * 512 + hh * 128: kt * 512 + (hh + 1) * 128],
                                     lb[b], rb[b], pT, osbf, ['QT', 'KT', 'V', 'lb%d' % b, 'rb%d' % b], h)
        P.barrier()
        phase_outproj(w_out_odd[i])

    def attention_with_store(QT, KT, Vfn, lb, rb, pT, osbf, rkeys, h):
        stores = []

        def osb2(qb):
            return osbf(qb)
        attention_qb(QT, KT, Vfn, lb, rb, pT, osb2, rkeys, lambda qb, dst, dkey: P.dma(
            'sp', lambda e: e.dma_start(out=oscr[h, :, qb * 512:(qb + 1) * 512], in_=dst), r=[dkey], w=['oscr'], stream=dkey))

    def attention_qb(QT, KT, Vfn, lb, rb, pT, osb, rkeys, after):
        steps = []
        for qb in range(4):
            nkt = 4 * (qb + 1)
            for kt in range(nkt):
                steps.append((qb, kt, nkt))
        banks = {}

        def stage_a(qb, kt, nkt):
            if kt == 0:
                par = att['n'] % 2
                att['n'] += 1
                banks[qb] = (par, 2 + par)
            i = kt - 4 * qb
            q0 = qb * 512 + max(i, 0) * 128
            N = (qb + 1) * 512 - q0
            off = q0 - qb * 512
            sbk = 4 + att['sc'] % 4
            pb = att['sc'] % 3
            att['sc'] += 1
            P.pe(lambda e: e.matmul(ps[sbk][:, 0:N], lhsT=KT[:, kt * 128:(kt + 1) * 128], rhs=QT[:, q0:q0 + N], start=True, stop=False),
                 r=rkeys, w=['ps%d' % sbk])
            P.pe(lambda e: e.matmul(ps[sbk][:, 0:N], lhsT=lb[:, kt * 128:(kt + 1) * 128], rhs=rb[:, q0:q0 + N], start=False, stop=(i < 0)),
                 r=rkeys, w=['ps%d' % sbk])
            if i >= 0:
                P.pe(lambda e: e.matmul(ps[sbk][:, 0:128], lhsT=identb[:], rhs=masknegb[:], start=False, stop=True),
                     r=['identb', 'masknegb'], w=['ps%d' % sbk])
            P.act(lambda e: e.activation(out=pT[pb][:, 0:N], in_=ps[sbk][:, 0:N], func=AF.Exp), r=['ps%d' % sbk], w=['pT%d' % pb])
            return (qb, kt, nkt, N, off, pb)

        def stage_b(info):
            qb, kt, nkt, N, off, pb = info
            ob, sb_ = banks[qb]
            P.pe(lambda e: e.matmul(ps[ob][:, off:off + N], lhsT=Vfn(kt), rhs=pT[pb][:, 0:N], start=(kt == 0), stop=(kt == nkt - 1)),
                 r=rkeys + ['pT%d' % pb], w=['ps%d' % ob])
            P.pe(lambda e: e.matmul(ps[sb_][:, off:off + N], lhsT=onesb[:], rhs=pT[pb][:, 0:N], start=(kt == 0), stop=(kt == nkt - 1)),
                 r=['onesb', 'pT%d' % pb], w=['ps%d' % sb_])
            if kt == nkt - 1:
                dst, dkey, rec, rkey = osb(qb)
                P.dve(lambda e: e.reciprocal(out=rec, in_=ps[sb_][:]), r=['ps%d' % sb_], w=[rkey])
                P.dve(lambda e: e.tensor_tensor(out=dst, in0=ps[ob][:], in1=rec, op=ALU.mult), r=['ps%d' % ob, rkey], w=[dkey])
                after(qb, dst, dkey)

        prev = None
        for st_ in steps:
            cur = stage_a(*st_)
            if prev is not None:
                stage_b(prev)
            prev = cur
        stage_b(prev)

    def phase_even(l):
        i = l // 2
        win = w_in_even[i]
        lam_init = 0.8 - 0.6 * math.exp(-0.3 * l)
        AR.reset()
        hn = AR.bf(16 * T)
        wstate['bufs'] = [AR.bf(16 * 512) for _ in range(2)]
        phase_norm(l, hn)
        base0 = AR.off
        e1, Gc, eb, beta, lnb, EG, BEG, tmp, nGc = [AR.f32(T, parts=8) for _ in range(9)]
        cm = AR.f32(T, parts=8)
        sm8 = AR.f32(64, parts=8)
        wab = AR.bf(16 * 16)
        wabv = wab.rearrange("p (k n) -> p k n", k=16, n=16)
        P.dma('pool', lambda e: e.dma_start(out=wabv, in_=win[:, 7168:7184].rearrange("(k p) n -> p k n", p=128)),
              w=['wab'], stream='wf')
        P.dma('sp', lambda e: e.dma_start(out=cm, in_=c_cmask), w=['cm'], stream='cm')
        P.act(lambda e: e.activation(out=sm8[:, 0:1], in_=sm[0:8, SM_ALOG + i:SM_ALOG + i + 1], func=AF.Exp), r=['sm'], w=['nA'])
        P.dve(lambda e: e.tensor_scalar(out=sm8[:, 0:1], in0=sm8[:, 0:1], scalar1=-1.0, scalar2=None, op0=ALU.mult), r=['nA'], w=['nA'])

        def ev_a(tb, bank):
            P.act(lambda e: e.activation(out=e1[:, tb * 512:(tb + 1) * 512], in_=ps[bank][0:8, :], func=AF.Exp,
                                         bias=sm[0:8, SM_DTB + i:SM_DTB + i + 1], scale=1.0), r=['ps%d' % bank, 'sm'], w=['e1'])
        proj_fm(hn, wabv, 'wab', 0, 8, ev_a)

        def ev_b(tb, bank):
            P.act(lambda e: e.activation(out=eb[:, tb * 512:(tb + 1) * 512], in_=ps[bank][0:8, :], func=AF.Exp, scale=-1.0),
                  r=['ps%d' % bank], w=['eb'])
        proj_fm(hn, wabv, 'wab', 8, 8, ev_b)
        P.act(lambda e: e.activation(out=e1, in_=e1, func=AF.Ln, bias=1.0, scale=1.0), r=['e1'], w=['e1'])
        P.dve(lambda e: e.tensor_scalar(out=e1, in0=e1, scalar1=sm8[:, 0:1], scalar2=None, op0=ALU.mult), r=['e1', 'nA'], w=['e1'])
        P.dve(lambda e: e.tensor_tensor_scan(out=Gc, data0=cm, data1=e1, initial=0.0, op0=ALU.mult, op1=ALU.add),
              r=['cm', 'e1'], w=['Gc'])
        P.dve(lambda e: e.tensor_scalar(out=eb, in0=eb, scalar1=1.0, scalar2=None, op0=ALU.add), r=['eb'], w=['eb'])
        P.dve(lambda e: e.reciprocal(out=beta, in_=eb), r=['eb'], w=['beta'])
        P.act(lambda e: e.activation(out=lnb, in_=eb, func=AF.Ln), r=['eb'], w=['lnb'])
        P.dve(lambda e: e.tensor_tensor(out=lnb, in0=Gc, in1=lnb, op=ALU.subtract), r=['Gc', 'lnb'], w=['lnb'])
        P.act(lambda e: e.activation(out=EG, in_=Gc, func=AF.Exp), r=['Gc'], w=['EG'])
        P.dve(lambda e: e.tensor_tensor(out=BEG, in0=beta, in1=EG, op=ALU.mult), r=['beta', 'EG'], w=['BEG'])
        Gc3 = Gc.rearrange("p (n c) -> p n c", n=32, c=64)
        P.dve(lambda e: e.tensor_copy(out=tmp.rearrange("p (n c) -> p n c", n=32, c=64), in_=Gc3[:, :, 63:64].to_broadcast([8, 32, 64])),
              r=['Gc'], w=['tmp'])
        P.dve(lambda e: e.tensor_tensor(out=tmp, in0=tmp, in1=Gc, op=ALU.subtract), r=['tmp', 'Gc'], w=['tmp'])
        P.act(lambda e: e.activation(out=tmp, in_=tmp, func=AF.Exp), r=['tmp'], w=['tmp'])
        P.act(lambda e: e.activation(out=sm8[:, 8:40], in_=Gc3[:, :, 63], func=AF.Exp), r=['Gc'], w=['egl'])
        P.dve(lambda e: e.tensor_scalar(out=nGc, in0=Gc, scalar1=-1.0, scalar2=None, op0=ALU.mult), r=['Gc'], w=['nGc'])
        for q, (tl, k_) in enumerate(((beta, 'beta'), (EG, 'EG'), (BEG, 'BEG'), (tmp, 'tmp'), (Gc, 'Gc'), (nGc, 'nGc'), (lnb, 'lnb'))):
            P.dma('sp', lambda e, q=q, tl=tl: e.dma_start(out=rows[q], in_=tl), r=[k_], w=['rows'], stream='rows%d' % q)
        P.dma('sp', lambda e: e.dma_start(out=eglr, in_=sm8[:, 8:40]), r=['egl'], w=['eglr'], stream='rows7')
        P.barrier()
        AR.reset(base0)
        QT = AR.bf(4 * T)
        KT = AR.bf(4 * T)
        V = AR.bf(16 * 512)
        pT = [AR.bf(512) for _ in range(3)]
        o12 = [AR.f32(T), AR.f32(T)]
        rec = [AR.f32(512) for _ in range(2)]
        sqd = AR.bf(T)
        rsd = AR.f32(512)
        ob = [AR.bf(512) for _ in range(2)]
        aL = AR.bf(T, parts=68)
        aR0 = AR.f32(T, parts=68)
        aR = [AR.bf(T, parts=68) for _ in range(2)]
        lamv = AR.f32(80, parts=1)
        nlam = AR.f32(2)
        gcol = AR.f32(1)
        for pb in (0, 64):
            P.dma('pool', lambda e, pb=pb: e.dma_start(out=aL[pb:pb + 4, :], in_=c_alibi[:, 0:T]), w=['aL'], stream='aL%d' % pb)
            P.dma('sp', lambda e, pb=pb: e.dma_start(out=aR0[pb:pb + 4, :], in_=c_alibi[:, T:2 * T]), w=['aR0'], stream='aR0%d' % pb)
        lo = SM_LAM + i * 256
        for j in range(2):
            P.dve(lambda e, j=j: e.tensor_tensor(out=lamv[:, 8:72], in0=sm[0:1, lo + 2 * j * 64:lo + 2 * j * 64 + 64],
                                                 in1=sm[0:1, lo + (2 * j + 1) * 64:lo + (2 * j + 1) * 64 + 64], op=ALU.mult), r=['sm', 'lamj'], w=['lamt'])
            P.dve(lambda e, j=j: e.reduce_sum(out=lamv[:, j:j + 1], in_=lamv[:, 8:72], axis=mybir.AxisListType.X),
                  r=['lamt'], w=['lamj'])
        P.act(lambda e: e.activation(out=lamv[:, 0:2], in_=lamv[:, 0:2], func=AF.Exp), r=['lamj'], w=['lamj'])
        P.dve(lambda e: e.tensor_tensor(out=lamv[:, 2:3], in0=lamv[:, 1:2], in1=lamv[:, 0:1], op=ALU.subtract), r=['lamj'], w=['lamn'])
        P.dve(lambda e: e.tensor_scalar(out=lamv[:, 2:3], in0=lamv[:, 2:3], scalar1=-lam_init, scalar2=None, op0=ALU.add), r=['lamn'], w=['lamn'])
        P.dve(lambda e: e.tensor_copy(out=lamv[:, 3:4], in_=lamv[:, 2:3]), r=['lamn'], w=['lamn'])
        P.pe(lambda e: e.matmul(ps[0][:, 0:2], lhsT=ones32[0:1, :], rhs=lamv[0:1, 2:4], start=True, stop=True), r=['lamn', 'ones32'], w=['ps0'])
        P.dve(lambda e: e.tensor_copy(out=nlam, in_=ps[0][:, 0:2]), r=['ps0'], w=['nlam'])
        P.dve(lambda e: e.tensor_scalar(out=gcol, in0=sm[:, SM_DAN + i:SM_DAN + i + 1], scalar1=1.0 - lam_init, scalar2=None, op0=ALU.mult),
              r=['sm'], w=['gcol'])
        for g in range(2):
            wq, kq = wtile(win[:, g * 512:(g + 1) * 512])
            for hh in range(4):
                def ev_q(tb, bank, hh=hh):
                    P.act(lambda e: e.activation(out=QT[:, hh * T + tb * 512: hh * T + (tb + 1) * 512], in_=ps[bank][:],
                                                 func=AF.Copy, scale=0.125), r=['ps%d' % bank], w=['QT'])
                proj_fm(hn, wq, kq, hh * 128, 128, ev_q)
            wk, kk_ = wtile(win[:, 1024 + g * 512:1024 + (g + 1) * 512])
            for hh in range(4):
                def ev_k(tb, bank, hh=hh):
                    P.dve(lambda e: e.tensor_copy(out=KT[:, hh * T + tb * 512: hh * T + (tb + 1) * 512], in_=ps[bank][:]),
                          r=['ps%d' % bank], w=['KT'])
                proj_fm(hn, wk, kk_, hh * 128, 128, ev_k)
            wv, kv = wtile(win[:, 2048 + g * 512:2048 + (g + 1) * 512])
            for tt in range(16):
                bank = pj['n'] % 4 + 4
                pj['n'] += 1
                for kc in range(16):
                    P.pe(lambda e, bank=bank, kc=kc, tt=tt, wv=wv: e.matmul(ps[bank][:], lhsT=hn[:, kc * T + tt * 128: kc * T + (tt + 1) * 128],
                                                                           rhs=wv[:, kc, :], start=(kc == 0), stop=(kc == 15)),
                         r=[kv, 'hn'], w=['ps%d' % bank])
                if tt % 2 == 0:
                    P.act(lambda e, bank=bank, tt=tt: e.activation(out=V[:, tt * 512:(tt + 1) * 512], in_=ps[bank][:], func=AF.Copy),
                          r=['ps%d' % bank], w=['V'])
                else:
                    P.dve(lambda e, bank=bank, tt=tt: e.tensor_copy(out=V[:, tt * 512:(tt + 1) * 512], in_=ps[bank][:]),
                          r=['ps%d' % bank], w=['V'])
            for hh in range(4):
                h = g * 4 + hh
                b = h % 2
                slope = 2.0 ** (-(h + 1))
                for pb in (0, 64):
                    P.dve(lambda e, b=b, slope=slope, pb=pb: e.tensor_scalar(out=aR[b][pb:pb + 4, :], in0=aR0[pb:pb + 4, :], scalar1=slope,
                                                                             scalar2=None, op0=ALU.mult), r=['aR0'], w=['aR%d' % b])
                for m in range(2):
                    def osbf(qb, m=m):
                        return o12[m][:, qb * 512:(qb + 1) * 512], 'o%d_%d' % (m, qb), rec[qb % 2], 'rec%d' % (qb % 2)
                    attention_qb(QT[64 * m:64 * m + 64, hh * T:(hh + 1) * T], KT[64 * m:64 * m + 64, hh * T:(hh + 1) * T],
                                 lambda kt, hh=hh: V[:, kt * 512 + hh * 128: kt * 512 + (hh + 1) * 128],
                                 aL[64 * m:64 * m + 4, :], aR[b][64 * m:64 * m + 4, :], pT, osbf, ['QT', 'KT', 'V', 'aL', 'aR%d' % b],
                                 lambda qb, dst, dkey: None)
                okeys = ['o%d_%d' % (m, qb) for m in range(2) for qb in range(4)]
                P.dve(lambda e: e.scalar_tensor_tensor(out=o12[0], in0=o12[1], scalar=nlam[:, 0:1], in1=o12[0], op0=ALU.mult, op1=ALU.add),
                      r=okeys + ['nlam'], w=okeys)
                P.act(lambda e: e.activation(out=sqd, in_=o12[0], func=AF.Square), r=okeys, w=['sqd'])
                for tb in range(4):
                    bank = 4 + tb
                    P.pe(lambda e, bank=bank, tb=tb: e.matmul(ps[bank][:], lhsT=onesb[:], rhs=sqd[:, tb * 512:(tb + 1) * 512], start=True, stop=True),
                         r=['sqd', 'onesb'], w=['ps%d' % bank])
                    P.act(lambda e, bank=bank: e.activation(out=rsd, in_=ps[bank][:], func=AF.Sqrt, bias=1e-6, scale=1.0 / 128),
                          r=['ps%d' % bank], w=['rsd'])
                    P.dve(lambda e: e.reciprocal(out=rsd, in_=rsd), r=['rsd'], w=['rsd'])
                    ob_ = tb % 2
                    P.dve(lambda e, tb=tb, ob_=ob_: e.scalar_tensor_tensor(out=ob[ob_], in0=o12[0][:, tb * 512:(tb + 1) * 512], scalar=gcol[:, 0:1],
                                                                          in1=rsd, op0=ALU.mult, op1=ALU.mult),
                          r=okeys + ['rsd', 'gcol'], w=['ob%d' % ob_])
                    P.dma('sp', lambda e, tb=tb, ob_=ob_, h=h: e.dma_start(out=oscr[h, :, tb * 512:(tb + 1) * 512], in_=ob[ob_]),
                          r=['ob%d' % ob_], w=['oscr'], stream='ob%d' % ob_)
        P.barrier()
        AR.reset(base0)
        stg = [AR.f32(512) for _ in range(3)]
        sn = 0
        for j in range(8):
            wg, kg = wtile(win[:, 3072 + j * 512:3072 + (j + 1) * 512])
            for c4 in range(4):
                c = j * 4 + c4

                def ev_g(tb, bank, c=c):
                    nonlocal sn
                    k3 = sn % 3
                    sn += 1
                    if sn % 2 == 0:
                        P.act(lambda e: e.activation(out=stg[k3], in_=ps[bank][:], func=AF.Copy), r=['ps%d' % bank], w=['stg%d' % k3])
                    else:
                        P.dve(lambda e: e.tensor_copy(out=stg[k3], in_=ps[bank][:]), r=['ps%d' % bank], w=['stg%d' % k3])
                    P.dma('sp', lambda e: e.dma_start(out=pscr[c, :, tb * 512:(tb + 1) * 512], in_=stg[k3]),
                          r=['stg%d' % k3], w=['pscr'], stream='stg%d' % k3)
                proj_fm(hn, wg, kg, c4 * 128, 128, ev_g)
        P.barrier()
        phase_gdn(l)
        phase_outproj(w_out_even[i])

    def phase_gdn(l):
        i = l // 2
        AR.reset()
        tri = AR.f32(1536, parts=64)
        eye = AR.f32(512, parts=64)
        tL = AR.f32(T, parts=66)
        tR = AR.f32(T, parts=66)
        S = [AR.f32(128) for _ in range(2)]
        vn = [AR.f32(128, parts=64) for _ in range(2)]
        eglh = AR.f32(32)
        rsd4 = [AR.f32(512) for _ in range(4)]
        sqd = AR.bf(T)
        ob = [AR.bf(512) for _ in range(2)]
        qT, kT, qdT = AR.f32(T), AR.f32(T), AR.f32(T)
        vb, kbg, kd = [AR.f32(32 * 128, parts=64) for _ in range(3)]
        intraT = AR.f32(T, parts=64)
        nwT = AR.f32(T)
        Xbase = AR.off
        P.dma('sp', lambda e: e.dma_start(out=tri, in_=c_tri), w=['tri'], stream='tri')
        P.dve(lambda e: e.tensor_tensor(out=eye, in0=tri[:, 0:512], in1=tri[:, 512:1024], op=ALU.subtract), r=['tri'], w=['eye'])
        P.dve(lambda e: e.memset(tL, 1.0), w=['tL'])
        P.dve(lambda e: e.memset(tR, 1.0), w=['tR'])
        for h in range(8):
            AR.reset(Xbase)
            xp = [AR.f32(T + 3) for _ in range(2)]
            B = [AR.f32(T) for _ in range(4)]
            vT, kbgT, kdT = AR.f32(T), AR.f32(T), AR.f32(T)
            for q in range(4):
                P.dma('sp', lambda e, q=q, h=h: e.dma_start(out=B[q], in_=rows[q, h, :].partition_broadcast(128)), w=['B%d' % q], stream='B%d' % q)
            P.dma('sp', lambda e, h=h: e.dma_start(out=eglh, in_=eglr[h, :].partition_broadcast(128)), w=['eglh'], stream='eglh')
            for (tt_, prow, q) in ((tL, 0, 5), (tL, 32, 5), (tL, 64, 6), (tR, 1, 4), (tR, 33, 6), (tR, 65, 5)):
                nm = 'tL' if tt_ is tL else 'tR'
                P.dma('sp', lambda e, tt_=tt_, prow=prow, q=q, h=h: e.dma_start(out=tt_[prow:prow + 1, :], in_=rows[q, h:h + 1, :]),
                      w=[nm], stream='%s%d' % (nm, prow))
            for which, c, dst, dk in ((0, h, qT, 'qT'), (1, 8 + h, kT, 'kT'), (2, 16 + h, vT, 'vT')):
                b = which % 2
                P.dve(lambda e, b=b: e.memset(xp[b][:, 0:3], 0.0), w=['xp%d' % b])
                P.dma('sp', lambda e, b=b, c=c: e.dma_start(out=xp[b][:, 3:3 + T], in_=pscr[c]), w=['xp%d' % b], stream='xp%d' % b)
                wc = SM_CONV + (i * 24 + c) * 4
                P.dve(lambda e, b=b, dst=dst, wc=wc: e.tensor_scalar(out=dst, in0=xp[b][:, 0:T], scalar1=sm[:, wc:wc + 1], scalar2=None, op0=ALU.mult),
                      r=['xp%d' % b, 'sm'], w=[dk])
                for j in range(1, 4):
                    P.dve(lambda e, b=b, dst=dst, wc=wc, j=j: e.scalar_tensor_tensor(out=dst, in0=xp[b][:, j:j + T], scalar=sm[:, wc + j:wc + j + 1],
                                                                                    in1=dst, op0=ALU.mult, op1=ALU.add),
                          r=['xp%d' % b, 'sm', dk], w=[dk])
                P.act(lambda e, dst=dst: e.activation(out=dst, in_=dst, func=AF.Silu), r=[dk], w=[dk])
                if which < 2:
                    P.act(lambda e, dst=dst: e.activation(out=sqd, in_=dst, func=AF.Square), r=[dk], w=['sqd'])
                    qs = (128 ** -0.5) if which == 0 else 1.0
                    for tb in range(4):
                        bank = 4 + tb
                        P.pe(lambda e, bank=bank, tb=tb: e.matmul(ps[bank][:], lhsT=onesb[:], rhs=sqd[:, tb * 512:(tb + 1) * 512], start=True, stop=True),
                             r=['sqd', 'onesb'], w=['ps%d' % bank])
                        P.act(lambda e, bank=bank, tb=tb: e.activation(out=rsd4[tb], in_=ps[bank][:], func=AF.Sqrt, bias=1e-6, scale=1.0),
                              r=['ps%d' % bank], w=['rsd%d' % tb])
                        P.dve(lambda e, tb=tb: e.reciprocal(out=rsd4[tb], in_=rsd4[tb]), r=['rsd%d' % tb], w=['rsd%d' % tb])
                        P.dve(lambda e, dst=dst, tb=tb, qs=qs: e.scalar_tensor_tensor(out=dst[:, tb * 512:(tb + 1) * 512], in0=dst[:, tb * 512:(tb + 1) * 512],
                                                                                      scalar=qs, in1=rsd4[tb], op0=ALU.mult, op1=ALU.mult),
                              r=[dk, 'rsd%d' % tb], w=[dk])
            P.dve(lambda e: e.tensor_tensor(out=qdT, in0=qT, in1=B[1], op=ALU.mult), r=['qT', 'B1'], w=['qdT'])
            P.dve(lambda e: e.tensor_tensor(out=vT, in0=vT, in1=B[0], op=ALU.mult), r=['vT', 'B0'], w=['vT'])
            P.dve(lambda e: e.tensor_tensor(out=kbgT, in0=kT, in1=B[2], op=ALU.mult), r=['kT', 'B2'], w=['kbgT'])
            P.dve(lambda e: e.tensor_tensor(out=kdT, in0=kT, in1=B[3], op=ALU.mult), r=['kT', 'B3'], w=['kdT'])
            tn = 0
            for X_, Xk, Y_, Yk in ((vT, 'vT', vb, 'vb'), (kbgT, 'kbgT', kbg, 'kbg'), (kdT, 'kdT', kd, 'kd')):
                for n4 in range(8):
                    bank = tn % 8
                    tn += 1
                    for j in range(4):
                        n = n4 * 4 + j
                        P.pe(lambda e, bank=bank, j=j, n=n, X_=X_: e.transpose(out=ps[bank][0:64, j * 128:(j + 1) * 128],
                                                                               in_=X_[:, n * 64:(n + 1) * 64], identity=ident[:]),
                             r=[Xk, 'ident'], w=['ps%d' % bank])
                    if tn % 2 == 0:
                        P.act(lambda e, bank=bank, n4=n4, Y_=Y_: e.activation(out=Y_[:, n4 * 512:(n4 + 1) * 512], in_=ps[bank][0:64, :], func=AF.Copy),
                              r=['ps%d' % bank], w=[Yk])
                    else:
                        P.dve(lambda e, bank=bank, n4=n4, Y_=Y_: e.tensor_copy(out=Y_[:, n4 * 512:(n4 + 1) * 512], in_=ps[bank][0:64, :]),
                              r=['ps%d' % bank], w=[Yk])
            P.barrier()
            AR.reset(Xbase)
            DT, DTb, DTbT = [AR.f32(T, parts=64) for _ in range(3)]
            Pm = [AR.f32(T, parts=64) for _ in range(2)]
            PTm = [AR.f32(T, parts=64) for _ in range(2)]
            Rm = [AR.f32(T, parts=64) for _ in range(2)]
            bn = 0
            for vi, (pb, dst, dk) in enumerate(((0, DT, 'DT'), (32, DTb, 'DTb'), (64, DTbT, 'DTbT'))):
                for n8 in range(4):
                    bank = bn % 8
                    bn += 1
                    for j in range(8):
                        n = n8 * 8 + j
                        P.pe(lambda e, bank=bank, j=j, n=n, pb=pb: e.matmul(ps[bank][0:64, j * 64:(j + 1) * 64], lhsT=tL[pb:pb + 2, n * 64:(n + 1) * 64],
                                                                            rhs=tR[pb:pb + 2, n * 64:(n + 1) * 64], start=True, stop=True),
                             r=['tL', 'tR'], w=['ps%d' % bank])
                    blk = slice(n8 * 512, (n8 + 1) * 512)
                    P.dve(lambda e, bank=bank, dst=dst, blk=blk: e.tensor_scalar(out=dst[:, blk], in0=ps[bank][0:64, :], scalar1=0.0, scalar2=None, op0=ALU.min),
                          r=['ps%d' % bank], w=[dk])
                    P.act(lambda e, dst=dst, blk=blk: e.activation(out=dst[:, blk], in_=dst[:, blk], func=AF.Exp), r=[dk], w=[dk])
                    P.dve(lambda e, dst=dst, blk=blk, vi=vi: e.tensor_tensor(out=dst[:, blk], in0=dst[:, blk], in1=tri[:, vi * 512:(vi + 1) * 512], op=ALU.mult),
                          r=[dk, 'tri'], w=[dk])
            for n8 in range(4):
                blk = slice(n8 * 512, (n8 + 1) * 512)
                bank = bn % 8
                bn += 1
                for j in range(8):
                    n = n8 * 8 + j
                    P.pe(lambda e, bank=bank, j=j, n=n: e.matmul(ps[bank][0:64, j * 64:(j + 1) * 64], lhsT=kT[:, n * 64:(n + 1) * 64],
                                                                 rhs=kT[:, n * 64:(n + 1) * 64], start=True, stop=True), r=['kT'], w=['ps%d' % bank])
                P.dve(lambda e, bank=bank, blk=blk: e.scalar_tensor_tensor(out=Pm[0][:, blk], in0=ps[bank][0:64, :], scalar=-1.0, in1=DTb[:, blk],
                                                                           op0=ALU.mult, op1=ALU.mult), r=['ps%d' % bank, 'DTb'], w=['P0'])
                P.dve(lambda e, bank=bank, blk=blk: e.scalar_tensor_tensor(out=PTm[0][:, blk], in0=ps[bank][0:64, :], scalar=-1.0, in1=DTbT[:, blk],
                                                                           op0=ALU.mult, op1=ALU.mult), r=['ps%d' % bank, 'DTbT'], w=['PT0'])
                bank2 = bn % 8
                bn += 1
                for j in range(8):
                    n = n8 * 8 + j
                    P.pe(lambda e, bank2=bank2, j=j, n=n: e.matmul(ps[bank2][0:64, j * 64:(j + 1) * 64], lhsT=kT[:, n * 64:(n + 1) * 64],
                                                                   rhs=qT[:, n * 64:(n + 1) * 64], start=True, stop=True), r=['kT', 'qT'], w=['ps%d' % bank2])
                P.dve(lambda e, bank2=bank2, blk=blk: e.tensor_tensor(out=intraT[:, blk], in0=ps[bank2][0:64, :], in1=DT[:, blk], op=ALU.mult),
                      r=['ps%d' % bank2, 'DT'], w=['intraT'])
                P.dve(lambda e, blk=blk: e.tensor_tensor(out=Rm[0][:, blk], in0=Pm[0][:, blk], in1=eye, op=ALU.add), r=['P0', 'eye'], w=['R0'])
            cur = 0
            for m in range(1, 6):
                nxt = 1 - cur
                for n8 in range(4):
                    blk = slice(n8 * 512, (n8 + 1) * 512)
                    if m < 5:
                        bA = bn % 8
                        bn += 1
                        for j in range(8):
                            n = n8 * 8 + j
                            P.pe(lambda e, bA=bA, j=j, n=n, cur=cur: e.matmul(ps[bA][0:64, j * 64:(j + 1) * 64], lhsT=PTm[cur][:, n * 64:(n + 1) * 64],
                                                                              rhs=Pm[cur][:, n * 64:(n + 1) * 64], start=True, stop=True),
                                 r=['P%d' % cur, 'PT%d' % cur], w=['ps%d' % bA])
                        P.act(lambda e, bA=bA, blk=blk, nxt=nxt: e.activation(out=Pm[nxt][:, blk], in_=ps[bA][0:64, :], func=AF.Copy),
                              r=['ps%d' % bA], w=['P%d_%d' % (nxt, n8)])
                    bB = bn % 8
                    bn += 1
                    for j in range(8):
                        n = n8 * 8 + j
                        P.pe(lambda e, bB=bB, j=j, n=n, cur=cur: e.matmul(ps[bB][0:64, j * 64:(j + 1) * 64], lhsT=Pm[cur][:, n * 64:(n + 1) * 64],
                                                                          rhs=PTm[cur][:, n * 64:(n + 1) * 64], start=True, stop=True),
                             r=['P%d' % cur, 'PT%d' % cur], w=['ps%d' % bB])
                    P.dve(lambda e, bB=bB, blk=blk, nxt=nxt: e.tensor_copy(out=PTm[nxt][:, blk], in_=ps[bB][0:64, :]),
                          r=['ps%d' % bB], w=['PT%d_%d' % (nxt, n8)])
                for n8 in range(4):
                    blk = slice(n8 * 512, (n8 + 1) * 512)
                    bC = bn % 8
                    bn += 1
                    for j in range(8):
                        n = n8 * 8 + j
                        P.pe(lambda e, bC=bC, j=j, n=n, cur=cur, nxt=nxt: e.matmul(ps[bC][0:64, j * 64:(j + 1) * 64], lhsT=PTm[nxt][:, n * 64:(n + 1) * 64],
                                                                                   rhs=Rm[cur][:, n * 64:(n + 1) * 64], start=True, stop=True),
                             r=['PT%d_%d' % (nxt, n8), 'R%d' % cur], w=['ps%d' % bC])
                    P.dve(lambda e, bC=bC, blk=blk, cur=cur, nxt=nxt: e.tensor_tensor(out=Rm[nxt][:, blk], in0=ps[bC][0:64, :], in1=Rm[cur][:, blk], op=ALU.add),
                          r=['ps%d' % bC, 'R%d' % cur], w=['R%d' % nxt])
                P.dve(lambda e: e.memset(vn[0][:, 0:1], 0.0), r=['P%d_%d' % (nxt, n8) for n8 in range(4)] + ['PT%d_%d' % (nxt, n8) for n8 in range(4)],
                      w=['P%d' % nxt, 'PT%d' % nxt])
                cur = nxt
            Rf = Rm[cur]
            rk = 'R%d' % cur
            for n8 in range(4):
                blk = slice(n8 * 512, (n8 + 1) * 512)
                bank = bn % 8
                bn += 1
                for j in range(8):
                    n = n8 * 8 + j
                    P.pe(lambda e, bank=bank, j=j, n=n: e.matmul(ps[bank][:, j * 64:(j + 1) * 64], lhsT=kbg[:, n * 128:(n + 1) * 128],
                                                                 rhs=Rf[:, n * 64:(n + 1) * 64], start=True, stop=True), r=['kbg', rk], w=['ps%d' % bank])
                P.act(lambda e, bank=bank, blk=blk: e.activation(out=nwT[:, blk], in_=ps[bank][:], func=AF.Copy, scale=-1.0),
                      r=['ps%d' % bank], w=['nwT'])
            P.barrier()
            AR.reset(Xbase)
            oT = AR.f32(T)
            zs = AR.f32(T)
            P.dma('sp', lambda e, h=h: e.dma_start(out=zs, in_=pscr[24 + h]), w=['zs'], stream='zs')
            P.act(lambda e: e.activation(out=zs, in_=zs, func=AF.Silu), r=['zs'], w=['zs'])
            P.dve(lambda e: e.memset(S[0], 0.0), w=['S0'])
            for n in range(32):
                c_, x_ = n % 2, (n + 1) % 2
                bv, bs, bo = 4 + n % 2, 6 + n % 2, (n // 8) % 2
                j = n % 8
                P.pe(lambda e, bv=bv, n=n: e.matmul(ps[bv][0:64, 0:128], lhsT=Rf[:, n * 64:(n + 1) * 64], rhs=vb[:, n * 128:(n + 1) * 128],
                                                    start=True, stop=False), r=[rk, 'vb'], w=['ps%d' % bv])
                P.pe(lambda e, bv=bv, n=n, c_=c_: e.matmul(ps[bv][0:64, 0:128], lhsT=nwT[:, n * 64:(n + 1) * 64], rhs=S[c_],
                                                           start=False, stop=True), r=['nwT', 'S%d' % c_], w=['ps%d' % bv])
                P.act(lambda e, bv=bv, c_=c_: e.activation(out=vn[c_], in_=ps[bv][0:64, 0:128], func=AF.Copy), r=['ps%d' % bv], w=['vn%d' % c_])
                P.pe(lambda e, bo=bo, j=j, n=n, c_=c_: e.matmul(ps[bo][:, j * 64:(j + 1) * 64], lhsT=S[c_], rhs=qdT[:, n * 64:(n + 1) * 64],
                                                                start=True, stop=False), r=['S%d' % c_, 'qdT'], w=['ps%d' % bo])
                P.pe(lambda e, bs=bs, n=n, c_=c_: e.matmul(ps[bs][:, 0:128], lhsT=kd[:, n * 128:(n + 1) * 128], rhs=vn[c_], start=True, stop=True),
                     r=['kd', 'vn%d' % c_], w=['ps%d' % bs])
                P.dve(lambda e, bs=bs, n=n, c_=c_, x_=x_: e.scalar_tensor_tensor(out=S[x_], in0=S[c_], scalar=eglh[:, n:n + 1], in1=ps[bs][:, 0:128],
                                                                                 op0=ALU.mult, op1=ALU.add),
                      r=['S%d' % c_, 'eglh', 'ps%d' % bs], w=['S%d' % x_])
                P.pe(lambda e, bo=bo, j=j, n=n, c_=c_: e.matmul(ps[bo][:, j * 64:(j + 1) * 64], lhsT=vn[c_], rhs=intraT[:, n * 64:(n + 1) * 64],
                                                                start=False, stop=True), r=['vn%d' % c_, 'intraT'], w=['ps%d' % bo])
                if j == 7:
                    P.act(lambda e, bo=bo, n=n: e.activation(out=oT[:, (n - 7) * 64:(n + 1) * 64], in_=ps[bo][:], func=AF.Copy),
                          r=['ps%d' % bo], w=['oT'])
            P.act(lambda e: e.activation(out=sqd, in_=oT, func=AF.Square), r=['oT'], w=['sqd'])
            for tb in range(4):
                bank = 2 + tb % 2
                P.pe(lambda e, bank=bank, tb=tb: e.matmul(ps[bank][:], lhsT=onesb[:], rhs=sqd[:, tb * 512:(tb + 1) * 512], start=True, stop=True),
                     r=['sqd', 'onesb'], w=['ps%d' % bank])
                P.act(lambda e, bank=bank, tb=tb: e.activation(out=rsd4[tb], in_=ps[bank][:], func=AF.Sqrt, bias=1e-6, scale=1.0 / 128),
                      r=['ps%d' % bank], w=['rsd%d' % tb])
                P.dve(lambda e, tb=tb: e.reciprocal(out=rsd4[tb], in_=rsd4[tb]), r=['rsd%d' % tb], w=['rsd%d' % tb])
                blk = slice(tb * 512, (tb + 1) * 512)
                P.dve(lambda e, blk=blk, tb=tb: e.scalar_tensor_tensor(out=oT[:, blk], in0=oT[:, blk], scalar=sm[:, SM_GDN + i:SM_GDN + i + 1], in1=rsd4[tb],
                                                                       op0=ALU.mult, op1=ALU.mult), r=['oT', 'rsd%d' % tb, 'sm'], w=['oT'])
                ob_ = tb % 2
                P.dve(lambda e, blk=blk, ob_=ob_: e.tensor_tensor(out=ob[ob_], in0=oT[:, blk], in1=zs[:, blk], op=ALU.mult),
                      r=['oT', 'zs'], w=['gob%d' % ob_])
                P.dma('sp', lambda e, tb=tb, ob_=ob_, h=h: e.dma_start(out=oscr[8 + h, :, tb * 512:(tb + 1) * 512], in_=ob[ob_]),
                      r=['gob%d' % ob_], w=['oscr'], stream='gob%d' % ob_)
            P.barrier()

    def phase_final():
        AR.reset()
        st = [AR.f32(16 * 512) for _ in range(2)]
        sq = AR.bf(16 * 512)
        rs = AR.f32(512)
        yo = [AR.f32(D) for _ in range(2)]
        n = 0
        for tb in range(4):
            b = tb % 2
            P.dma('sp', lambda e, b=b, tb=tb: e.dma_start(
                out=st[b].rearrange("p (k t) -> p k t", k=16, t=512),
                in_=hT[:, :, tb * 512:(tb + 1) * 512].rearrange("k p t -> p k t")),
                w=['nst%d' % b], stream='nst%d' % b)
            P.act(lambda e, b=b: e.activation(out=sq, in_=st[b], func=AF.Square), r=['nst%d' % b], w=['nsq'])
            bank = tb % 2
            for kc in range(16):
                P.pe(lambda e, kc=kc, bank=bank: e.matmul(ps[bank][:], lhsT=onesb[:], rhs=sq[:, kc * 512:(kc + 1) * 512],
                                                           start=(kc == 0), stop=(kc == 15)),
                     r=['nsq', 'onesb'], w=['ps%d' % bank])
            P.act(lambda e, bank=bank: e.activation(out=rs, in_=ps[bank][:], func=AF.Sqrt, bias=1e-6, scale=1.0 / D),
                  r=['ps%d' % bank], w=['nrs'])
            P.dve(lambda e: e.reciprocal(out=rs, in_=rs), r=['nrs'], w=['nrs'])
            for kc in range(16):
                P.dve(lambda e, b=b, kc=kc: e.scalar_tensor_tensor(
                    out=st[b][:, kc * 512:(kc + 1) * 512], in0=st[b][:, kc * 512:(kc + 1) * 512],
                    scalar=sm[:, 8 * 16 + kc: 8 * 16 + kc + 1], in1=rs, op0=ALU.mult, op1=ALU.mult),
                    r=['nst%d' % b, 'nrs', 'sm'], w=['nst%d' % b])
            for t4 in range(4):
                yb = n % 2
                n += 1
                tt = tb * 4 + t4
                for q in range(4):
                    bank = 4 + (n * 4 + q) % 4
                    for j in range(4):
                        kc = q * 4 + j
                        P.pe(lambda e, b=b, kc=kc, bank=bank, j=j, t4=t4: e.transpose(
                            out=ps[bank][:, j * 128:(j + 1) * 128], in_=st[b][:, kc * 512 + t4 * 128: kc * 512 + (t4 + 1) * 128],
                            identity=ident[:]), r=['nst%d' % b, 'ident'], w=['ps%d' % bank])
                    if q % 2 == 0:
                        P.act(lambda e, yb=yb, q=q, bank=bank: e.activation(out=yo[yb][:, q * 512:(q + 1) * 512], in_=ps[bank][:], func=AF.Copy),
                              r=['ps%d' % bank], w=['yo%d_%d' % (yb, q)])
                    else:
                        P.dve(lambda e, yb=yb, q=q, bank=bank: e.tensor_copy(out=yo[yb][:, q * 512:(q + 1) * 512], in_=ps[bank][:]),
                              r=['ps%d' % bank], w=['yo%d_%d' % (yb, q)])
                P.dma('sp', lambda e, yb=yb, tt=tt: e.dma_start(out=y[tt * 128:(tt + 1) * 128, :], in_=yo[yb]),
                      r=['yo%d_%d' % (yb, q) for q in range(4)], w=['y'], stream='yo%d' % yb)

    phase_load_x()
    for kind, l in plan:
        if kind == 'fox':
            phase_fox(l)
        elif kind == 'even':
            phase_even(l)
        else:
            phase_mlp(l)
    phase_final()
    P.emit(final_wait_streams=['yo0', 'yo1'])
    return nc, P


SM_GAIN = 0
SM_CONV = 144
SM_DAN = SM_CONV + 192
SM_GDN = SM_DAN + 2
SM_ALOG = SM_GDN + 2
SM_DTB = SM_ALOG + 2
SM_FOXB = SM_DTB + 2
SM_LAM = SM_FOXB + 2
SM_COLS = SM_LAM + 512


def pack_small(inp):
    sm = np.zeros((128, SM_COLS), np.float32)
    gains = [inp['norm_mix'][l] for l in range(4)] + [inp['norm_mlp'][l] for l in range(4)] + [inp['norm_final']]
    for n, g in enumerate(gains):
        sm[:, SM_GAIN + n * 16: SM_GAIN + (n + 1) * 16] = np.asarray(g).reshape(16, 128).T
    cw = np.asarray(inp['conv_w'])
    for i in range(2):
        for c in range(24):
            for j in range(4):
                sm[:, SM_CONV + (i * 24 + c) * 4 + j] = cw[i, j, c * 128:(c + 1) * 128]
    for i in range(2):
        sm[:, SM_DAN + i] = inp['da_norm'][i]
        sm[:, SM_GDN + i] = inp['gdn_norm'][i]
        sm[0:8, SM_ALOG + i] = inp['gdn_a_log'][i]
        sm[0:8, SM_DTB + i] = inp['gdn_dt_bias'][i]
        sm[0:16, SM_FOXB + i] = inp['fox_b_f'][i]
        for j, nm in enumerate(['lam_q1', 'lam_k1', 'lam_q2', 'lam_k2']):
            sm[0, SM_LAM + (i * 4 + j) * 64: SM_LAM + (i * 4 + j + 1) * 64] = inp[nm][i]
    return sm


def make_consts():
    c = {}
    c['c_ident'] = np.eye(128, dtype=np.float32)
    s = np.arange(128)
    c['c_maskneg'] = np.where(s[:, None] > s[None, :], NEG, 0.0).astype(np.float32)
    t = np.arange(T)
    al = np.zeros((4, 2 * T), np.float32)
    al[0, :T] = (t // 128) * 128
    al[1, :T] = t % 128
    al[2, :T] = 1
    al[3, :T] = 1
    al[0, T:] = 1
    al[1, T:] = 1
    al[2, T:] = -((t // 128) * 128)
    al[3, T:] = -(t % 128)
    c['c_alibi'] = al
    j = np.arange(64)
    tri = np.zeros((64, 3, 8, 64), np.float32)
    tri[:, 0] = (j[None, :] >= j[:, None]).astype(np.float32)[:, None, :]
    tri[:, 1] = (j[None, :] > j[:, None]).astype(np.float32)[:, None, :]
    tri[:, 2] = (j[None, :] < j[:, None]).astype(np.float32)[:, None, :]
    c['c_tri'] = tri.reshape(64, 3 * 512)
    cm = np.ones((8, T), np.float32)
    cm[:, ::64] = 0
    c['c_cmask'] = cm
    return c


_CACHE = {}
REAL_CORES = [0, 1, 4, 5]


def kernel(**inputs):
    inp = {k: np.asarray(v) for k, v in inputs.items()}
    if 'nc' not in _CACHE:
        _CACHE['nc'] = build_program()[0]
    nc = _CACHE['nc']
    sm = pack_small(inp)
    consts = make_consts()
    shared = dict(w_in_even=inp['w_in_even'], w_out_even=inp['w_out_even'], w_in_odd=inp['w_in_odd'],
                  w_out_odd=inp['w_out_odd'], w_up=inp['w_up'], w_down=inp['w_down'], sm=sm, **consts)
    zeros = {k: np.zeros_like(v) for k, v in shared.items()}
    zeros['x'] = np.zeros_like(inp['x'][0])
    in_maps = []
    for c in range(8):
        if c in REAL_CORES:
            m = dict(shared)
            m['x'] = np.ascontiguousarray(inp['x'][REAL_CORES.index(c)])
        else:
            m = zeros
        in_maps.append(m)
    res = run_bass_kernel_spmd(nc, in_maps, core_ids=list(range(8)))
    out = np.stack([res.results[c]['y'] for c in REAL_CORES], axis=0)
    return out.astype(np.float32)
```
